# Optimizing a Trainium2 kernel written in Bass

```python
import jax, jax.numpy as jnp
from jax import lax
import numpy as np

D_MODEL = 2048
BATCH = 4
SEQ = 4096
DEPTH = 2

N_RET_HEADS = 8
RET_HEAD_DIM = 128
RET_WIDTH = N_RET_HEADS * RET_HEAD_DIM
N_MLA_HEADS = 8
MLA_NOPE_DIM = 128
MLA_ROPE_DIM = 64
MLA_QK_DIM = MLA_NOPE_DIM + MLA_ROPE_DIM
MLA_V_DIM = 128
MLA_Q_RANK = 512
MLA_KV_RANK = 512
MLA_WIDTH = N_MLA_HEADS * MLA_V_DIM
MIX_WIDTH_A = RET_WIDTH + MLA_WIDTH
SPLIT_A = (RET_WIDTH, 2 * RET_WIDTH, 3 * RET_WIDTH,
           3 * RET_WIDTH + MLA_Q_RANK,
           3 * RET_WIDTH + MLA_Q_RANK + MLA_KV_RANK,
           3 * RET_WIDTH + MLA_Q_RANK + MLA_KV_RANK + MLA_ROPE_DIM)
IN_WIDTH_A = SPLIT_A[-1] + MIX_WIDTH_A
CONV_WIDTH = D_MODEL
CONV_KERNEL = 31
IN_WIDTH_C = 3 * CONV_WIDTH

CHUNK = 128
Q_BLOCK = 128
ROPE_BASE = 10000.0
EPS = 1e-6
N_EVEN = (DEPTH + 1) // 2
N_ODD = DEPTH // 2

kernel_name = 'hybrid_retention_mla_conformer'


def rms_norm(x, g):
    xf = x.astype(jnp.float32)
    y = xf * lax.rsqrt(jnp.mean(xf * xf, axis=-1, keepdims=True) + EPS)
    return (y * g.astype(jnp.float32)).astype(x.dtype)


def layer_norm(x, g, b):
    xf = x.astype(jnp.float32)
    mu = jnp.mean(xf, axis=-1, keepdims=True)
    xc = xf - mu
    y = xc * lax.rsqrt(jnp.mean(xc * xc, axis=-1, keepdims=True) + EPS)
    return (y * g.astype(jnp.float32) + b.astype(jnp.float32)).astype(x.dtype)


def rope(x, pos):
    d = x.shape[-1]
    inv = ROPE_BASE ** (-jnp.arange(0, d, 2, dtype=jnp.float32) / d)
    ang = pos.astype(jnp.float32)[..., None] * inv
    cos = jnp.cos(ang)[:, :, None, :]
    sin = jnp.sin(ang)[:, :, None, :]
    xf = x.astype(jnp.float32)
    x1, x2 = xf[..., : d // 2], xf[..., d // 2:]
    out = jnp.concatenate([x1 * cos - x2 * sin, x2 * cos + x1 * sin], axis=-1)
    return out.astype(x.dtype)


def retention_chunkwise(q, k, v):
    f32 = jnp.float32
    B, S, H, dk = q.shape
    dv = v.shape[-1]
    nc = S // CHUNK
    log_g = jnp.log1p(-(2.0 ** (-5.0 - jnp.arange(H, dtype=f32))))
    qc = q.astype(f32).reshape(B, nc, CHUNK, H, dk)
    kc = k.astype(f32).reshape(B, nc, CHUNK, H, dk) * (dk ** -0.5)
    vc = v.astype(f32).reshape(B, nc, CHUNK, H, dv)
    idx = jnp.arange(CHUNK, dtype=f32)
    rel = idx[:, None] - idx[None, :]
    causal = rel >= 0
    dmask = jnp.where(causal[None], jnp.exp(log_g[:, None, None] * jnp.where(causal, rel, 0.0)[None]), 0.0)
    scores = jnp.einsum('bnihd,bnjhd->bnhij', qc, kc) * dmask
    inner = jnp.einsum('bnhij,bnjhe->bnihe', scores, vc)
    to_end = jnp.exp(log_g[None, :] * (CHUNK - 1.0 - idx)[:, None])
    kv = jnp.einsum('bnjhd,jh,bnjhe->bnhde', kc, to_end, vc)
    chunk_decay = jnp.exp(log_g * CHUNK)[None, :, None, None]

    def step(state, kv_n):
        return chunk_decay * state + kv_n, state

    _, prev = lax.scan(step, jnp.zeros((B, H, dk, dv), f32), jnp.moveaxis(kv, 1, 0))
    prev = jnp.moveaxis(prev, 0, 1)
    from_start = jnp.exp(log_g[None, :] * (idx + 1.0)[:, None])
    cross = jnp.einsum('bnihd,ih,bnhde->bnihe', qc, from_start, prev)
    return (inner + cross).reshape(B, S, H, dv)


def causal_attention_blocked(q, k, v):
    f32 = jnp.float32
    B, S, H, d = q.shape
    dv = v.shape[-1]
    nb = S // Q_BLOCK
    scale = d ** -0.5
    kf = k.astype(f32)
    vf = v.astype(f32)
    qb = jnp.moveaxis(q.astype(f32).reshape(B, nb, Q_BLOCK, H, d), 1, 0)
    key_pos = jnp.arange(S)
    neg = jnp.finfo(f32).min

    def one_block(args):
        q_blk, bi = args
        s = jnp.einsum('bqhd,bkhd->bhqk', q_blk, kf) * scale
        q_pos = bi * Q_BLOCK + jnp.arange(Q_BLOCK)
        s = jnp.where((key_pos[None, :] <= q_pos[:, None])[None, None], s, neg)
        p = jax.nn.softmax(s, axis=-1)
        return jnp.einsum('bhqk,bkhe->bqhe', p, vf)

    out = lax.map(one_block, (qb, jnp.arange(nb)))
    return jnp.moveaxis(out, 0, 1).reshape(B, S, H, dv)


def layer_retention_mla(x, pos, norm_g, w_in, q_a_norm_g, w_q_b, kv_a_norm_g, w_kv_b,
                        q_norm_g, k_norm_g, ret_norm_g, w_out):
    B, S, _ = x.shape
    h = rms_norm(x, norm_g)
    proj = h @ w_in
    rq, rk, rv, cq, ckv, krope, gate = jnp.split(proj, list(SPLIT_A), axis=-1)
    rq = rope(rq.reshape(B, S, N_RET_HEADS, RET_HEAD_DIM), pos)
    rk = rope(rk.reshape(B, S, N_RET_HEADS, RET_HEAD_DIM), pos)
    rv = rv.reshape(B, S, N_RET_HEADS, RET_HEAD_DIM)
    ret = retention_chunkwise(rq, rk, rv)
    ret = rms_norm(ret, ret_norm_g.reshape(N_RET_HEADS, RET_HEAD_DIM)).reshape(B, S, RET_WIDTH)
    q = (rms_norm(cq, q_a_norm_g) @ w_q_b).reshape(B, S, N_MLA_HEADS, MLA_QK_DIM)
    kv = (rms_norm(ckv, kv_a_norm_g) @ w_kv_b).reshape(B, S, N_MLA_HEADS, MLA_NOPE_DIM + MLA_V_DIM)
    k_nope, v = kv[..., :MLA_NOPE_DIM], kv[..., MLA_NOPE_DIM:]
    k_rope = jnp.broadcast_to(krope[:, :, None, :], (B, S, N_MLA_HEADS, MLA_ROPE_DIM))
    k = jnp.concatenate([k_nope, k_rope], axis=-1)
    q = rms_norm(q, q_norm_g)
    k = rms_norm(k, k_norm_g)
    q = jnp.concatenate([q[..., :MLA_NOPE_DIM], rope(q[..., MLA_NOPE_DIM:], pos)], axis=-1)
    k = jnp.concatenate([k[..., :MLA_NOPE_DIM], rope(k[..., MLA_NOPE_DIM:], pos)], axis=-1)
    att = causal_attention_blocked(q, k, v).reshape(B, S, MLA_WIDTH)
    mix = jnp.concatenate([ret, att], axis=-1).astype(x.dtype) * jax.nn.silu(gate)
    return x + mix @ w_out


def layer_conformer_conv(x, norm_g, w_in, conv_w, conv_b, ln_g, ln_b, w_out):
    h = rms_norm(x, norm_g)
    a, b, gate = jnp.split(h @ w_in, 3, axis=-1)
    u = a * jax.nn.sigmoid(b)
    u = lax.conv_general_dilated(
        u, conv_w[:, None, :].astype(u.dtype), window_strides=(1,),
        padding=[(CONV_KERNEL - 1, 0)], dimension_numbers=('NWC', 'WIO', 'NWC'),
        feature_group_count=CONV_WIDTH) + conv_b
    u = jax.nn.silu(layer_norm(u, ln_g, ln_b))
    return x + (u * jax.nn.silu(gate)) @ w_out


def setup_inputs(seed: int = 0) -> dict:
    key = jax.random.key(seed)
    ks = jax.random.split(key, 24)
    f32 = jnp.float32

    def nrm(k, shape, scale):
        return jax.random.normal(k, shape, f32) * scale

    def gain(k, shape):
        return 1.0 + 0.02 * jax.random.normal(k, shape, f32)

    x = jax.random.normal(ks[0], (BATCH, SEQ, D_MODEL), f32)
    offs = jax.random.randint(ks[1], (BATCH, 1), 0, 1024, dtype=jnp.int32)
    positions = offs + jnp.arange(SEQ, dtype=jnp.int32)[None, :]
    E, O = N_EVEN, N_ODD
    return {
        'x': x,
        'positions': positions,
        'a_norm_g': gain(ks[2], (E, D_MODEL)),
        'a_w_in': nrm(ks[3], (E, D_MODEL, IN_WIDTH_A), D_MODEL ** -0.5),
        'a_q_a_norm_g': gain(ks[4], (E, MLA_Q_RANK)),
        'a_w_q_b': nrm(ks[5], (E, MLA_Q_RANK, N_MLA_HEADS * MLA_QK_DIM), MLA_Q_RANK ** -0.5),
        'a_kv_a_norm_g': gain(ks[6], (E, MLA_KV_RANK)),
        'a_w_kv_b': nrm(ks[7], (E, MLA_KV_RANK, N_MLA_HEADS * (MLA_NOPE_DIM + MLA_V_DIM)), MLA_KV_RANK ** -0.5),
        'a_q_norm_g': gain(ks[8], (E, MLA_QK_DIM)),
        'a_k_norm_g': gain(ks[9], (E, MLA_QK_DIM)),
        'a_ret_norm_g': gain(ks[10], (E, RET_WIDTH)),
        'a_w_out': nrm(ks[11], (E, MIX_WIDTH_A, D_MODEL), MIX_WIDTH_A ** -0.5),
        'c_norm_g': gain(ks[12], (O, D_MODEL)),
        'c_w_in': nrm(ks[13], (O, D_MODEL, IN_WIDTH_C), D_MODEL ** -0.5),
        'c_conv_w': nrm(ks[14], (O, CONV_KERNEL, CONV_WIDTH), CONV_KERNEL ** -0.5),
        'c_conv_b': 0.02 * jax.random.normal(ks[15], (O, CONV_WIDTH), f32),
        'c_ln_g': gain(ks[16], (O, CONV_WIDTH)),
        'c_ln_b': 0.02 * jax.random.normal(ks[17], (O, CONV_WIDTH), f32),
        'c_w_out': nrm(ks[18], (O, CONV_WIDTH, D_MODEL), CONV_WIDTH ** -0.5),
    }


def reference(x, positions, a_norm_g, a_w_in, a_q_a_norm_g, a_w_q_b, a_kv_a_norm_g, a_w_kv_b,
              a_q_norm_g, a_k_norm_g, a_ret_norm_g, a_w_out,
              c_norm_g, c_w_in, c_conv_w, c_conv_b, c_ln_g, c_ln_b, c_w_out):
    for layer in range(DEPTH):
        i = layer // 2
        if layer % 2 == 0:
            x = layer_retention_mla(x, positions, a_norm_g[i], a_w_in[i], a_q_a_norm_g[i], a_w_q_b[i],
                                    a_kv_a_norm_g[i], a_w_kv_b[i], a_q_norm_g[i], a_k_norm_g[i],
                                    a_ret_norm_g[i], a_w_out[i])
        else:
            x = layer_conformer_conv(x, c_norm_g[i], c_w_in[i], c_conv_w[i], c_conv_b[i],
                                     c_ln_g[i], c_ln_b[i], c_w_out[i])
    return x
```

```python
import math
from contextlib import ExitStack
import numpy as np
import ml_dtypes
import concourse.bass as bass
import concourse.mybir as mybir
from concourse.bass_utils import run_bass_kernel_spmd

F32 = mybir.dt.float32
BF16 = mybir.dt.bfloat16
I32 = mybir.dt.int32
ALU = mybir.AluOpType
AF = mybir.ActivationFunctionType
AX = mybir.AxisListType

D = 2048
T = 4096
TP = 1920
TO = 2176
NB = 32
EPS = 1e-6
NDMASEM = 8


class Tok:
    __slots__ = ("name", "w", "r", "x")

    def __init__(self, name, x=False):
        self.name = name
        self.w = None
        self.r = []
        self.x = x or name.startswith("ps") or name.startswith("pT") or name.startswith("pS")


class Op:
    __slots__ = ("eng", "fn", "deps", "signal", "semval", "dma", "dmaidx", "idx", "kind")


class KB:
    ENGS = ("pe", "act", "dve", "pool", "sp")

    def __init__(self, nc):
        self.nc = nc
        self.ops = []
        self.ndma = {e: 0 for e in self.ENGS}
        self.dma_hist = {e: [] for e in self.ENGS}

    def op(self, eng, fn, reads=(), writes=(), dma=False, kind=None):
        o = Op()
        o.eng, o.fn, o.dma, o.kind = eng, fn, dma, kind
        o.signal = False
        o.semval = None
        o.idx = len(self.ops)
        deps = []
        for t in reads:
            if t.w is not None:
                deps.append(t.w)
            if t.x and t.r and t.r[-1].eng != eng:
                deps.append(t.r[-1])
        for t in writes:
            if t.w is not None:
                deps.append(t.w)
            deps.extend(t.r)
        if dma:
            o.dmaidx = self.ndma[eng]
            self.ndma[eng] += 1
            h = self.dma_hist[eng]
            if len(h) >= NDMASEM:
                deps.append(h[len(h) - NDMASEM])
            h.append(o)
        dd = []
        seen = set()
        for d in deps:
            if d.idx in seen:
                continue
            seen.add(d.idx)
            if d.eng == "pe" and eng == "pe" and not d.dma and not dma:
                continue
            dd.append(d)
            d.signal = True
        o.deps = dd
        for t in reads:
            t.r.append(o)
        for t in writes:
            t.w = o
            t.r = []
        self.ops.append(o)
        return o

    def barrier(self):
        lasts = []
        for e in self.ENGS:
            comp = [o for o in self.ops if o.eng == e and not o.dma]
            if comp:
                lasts.append(comp[-1])
            lasts.extend(self.dma_hist[e][-NDMASEM:])
        bt = Tok("barrier")
        for e in ("pe", "act", "dve", "pool", "sp"):
            o = self.op(e, lambda eng: eng.nop(), kind="nop")
            for d in lasts:
                if d.idx not in [x.idx for x in o.deps] and not (d.eng == e and not d.dma and e == "pe"):
                    o.deps.append(d)
                    d.signal = True

    def emit(self, sems, dsems):
        self._bsem = sems["sp"]
        nc = self.nc
        cnt = {e: 0 for e in self.ENGS}
        for o in self.ops:
            if o.dma:
                o.semval = (dsems[o.eng][o.dmaidx % NDMASEM], 16 * (o.dmaidx // NDMASEM + 1))
            elif o.signal:
                cnt[o.eng] += 1
                o.semval = (sems[o.eng], cnt[o.eng])
        per = {e: [o for o in self.ops if o.eng == e] for e in self.ENGS}
        final = []
        for e in self.ENGS:
            pass

        def run(engobj, lst, ename):
            known = {}
            for o in lst:
                for d in o.deps:
                    s, v = d.semval
                    k = id(s)
                    if known.get(k, 0) >= v:
                        continue
                    known[k] = v
                    engobj.wait_ge(s, v)
                ins = o.fn(engobj)
                if o.dma:
                    ins.then_inc(o.semval[0], 16)
                elif o.signal:
                    ins.then_inc(o.semval[0], 1)
            if ename == "sp":
                for e2 in self.ENGS:
                    if cnt[e2] > 0:
                        engobj.wait_ge(sems[e2], cnt[e2])
                    n = self.ndma[e2]
                    for i in range(min(n, NDMASEM)):
                        last = ((n - 1 - i) // NDMASEM) * NDMASEM + i
                        engobj.wait_ge(dsems[e2][i], 16 * (last // NDMASEM + 1))

        with nc.Block() as block:
            @block.tensor
            def _(e):
                run(e, per["pe"], "pe")

            @block.scalar
            def _(e):
                run(e, per["act"], "act")

            @block.vector
            def _(e):
                run(e, per["dve"], "dve")

            @block.gpsimd
            def _(e):
                run(e, per["pool"], "pool")

            @block.sync
            def _(e):
                run(e, per["sp"], "sp")


class Ring:
    def __init__(self, items):
        self.items = items
        self.i = 0

    def next(self):
        it = self.items[self.i % len(self.items)]
        self.i += 1
        return it


class _Stop(Exception):
    pass


def build(debug=None):
    debug = debug or {}
    stop_at = debug.get("stop", 10 ** 9)
    open_stacks = []

    def ckpt(n):
        if n >= stop_at:
            raise _Stop()
    nc = bass.Bass("TRN2", target_bir_lowering=False)
    es = ExitStack()
    kb = KB(nc)

    def din(name, shape, dt=F32):
        return nc.dram_tensor(name, list(shape), dt, kind="ExternalInput").ap()

    def dscr(name, shape, dt):
        kind = "ExternalOutput" if name in debug else "Internal"
        return nc.dram_tensor(name, list(shape), dt, kind=kind).ap()

    xT = din("xT", [D, T])
    pos_tm = din("pos_tm", [128, NB], I32)
    kbias_d = din("kbias", [128, NB])
    a_norm_g = din("a_norm_g", [D])
    a_w_in = din("a_w_in", [D, 6208])
    a_q_a_norm_g = din("a_q_a_norm_g", [512])
    a_w_q_b = din("a_w_q_b", [512, 1536])
    a_kv_a_norm_g = din("a_kv_a_norm_g", [512])
    a_w_kv_b = din("a_w_kv_b", [512, 2048])
    a_q_norm_g = din("a_q_norm_g", [192])
    a_k_norm_g = din("a_k_norm_g", [192])
    a_ret_norm_g = din("a_ret_norm_g", [1024])
    a_w_out = din("a_w_out", [D, D])
    c_norm_g = din("c_norm_g", [D])
    c_w_in = din("c_w_in", [D, 6144])
    c_conv_wT = din("c_conv_wT", [D, 31])
    c_conv_b = din("c_conv_b", [D])
    c_ln_g = din("c_ln_g", [D])
    c_ln_b = din("c_ln_b", [D])
    c_w_out = din("c_w_out", [D, D])
    c_ident = din("c_ident", [128, 128], BF16)
    c_tri = din("c_tri", [128, 128], BF16)
    c_dmaskT = din("c_dmaskT", [128, 8, 128])
    c_toend = din("c_toend", [128, 8])
    c_fscol = din("c_fscol", [128, 8])
    c_invr = din("c_invr", [128, 64])
    c_invm = din("c_invm", [128, 32])
    outT = nc.dram_tensor("outT", [D, 2048], F32, kind="ExternalOutput").ap()

    gate_s = dscr("gate_s", [D, TO], BF16)
    mix_s = dscr("mix_s", [D, TO], BF16)
    x1_s = dscr("x1_s", [D, TO], F32)
    v_s = dscr("v_s", [D, 2048], F32)
    gate1_s = dscr("gate1_s", [D, 2048], BF16)
    dbg_s = {k: dscr(k, v[0], v[1]) for k, v in debug.items() if k.startswith("dbg_")}

    def sb(name, shape, dt):
        return es.enter_context(nc.sbuf_tensor(name, list(shape), dt))

    def pst(name, shape, dt=F32):
        return es.enter_context(nc.psum_tensor(name, list(shape), dt))

    psum = [pst(f"ps{i}", [128, 512]) for i in range(8)]
    ptok = [Tok(f"ps{i}") for i in range(8)]
    PS = Ring(list(zip(psum, ptok)))

    def bufring(name, n, shape, dt):
        return Ring([(sb(f"{name}{i}", shape, dt), Tok(f"{name}{i}")) for i in range(n)])

    hT_tok = [Tok(f"hT{g}") for g in range(8)]
    latT_tok = [Tok(f"lat{g}") for g in range(8)]
    kropeT_tok = [Tok(f"krT{b}") for b in range(NB)]
    ssrope = sb("ssrope", [128, NB], F32)
    ssrope_tok = [Tok(f"ssr{b}") for b in range(NB)]
    cqn_tok = [Tok(f"cqn{g}") for g in range(5)]

    ident = sb("ident", [128, 128], BF16)
    tri = sb("tri", [128, 128], BF16)
    ones_bf = sb("ones_bf", [128, 128], BF16)
    dmaskT = sb("dmaskT", [128, 8, 128], F32)
    toend = sb("toend", [128, 8], F32)
    fscol = sb("fscol", [128, 8], F32)
    invr = sb("invr", [128, 64], F32)
    invm = sb("invm", [128, 32], F32)
    posi = sb("posi", [128, NB], I32)
    posf = sb("posf", [128, NB], F32)
    kbias = sb("kbias_sb", [128, NB], F32)
    CONST = Tok("const")

    def dma(eng, out, in_, reads=(), writes=(), **kw):
        return kb.op(eng, lambda e: e.dma_start(out=out, in_=in_, **kw), reads, writes, dma=True)

    for (dst, src) in ((ident, c_ident), (tri, c_tri), (dmaskT, c_dmaskT), (toend, c_toend),
                       (fscol, c_fscol), (invr, c_invr), (invm, c_invm), (posi, pos_tm),
                       (kbias, kbias_d)):
        dma("sp", dst[:], src, writes=[CONST])
    kb.op("dve", lambda e: e.memset(ones_bf[:], 1.0), writes=[CONST])
    kb.op("dve", lambda e: e.tensor_copy(out=posf[:], in_=posi[:]), reads=[CONST], writes=[CONST])

    def gvec(name, src, n):
        t = sb(name, [128, n], F32)
        dma("sp", t[:], src.rearrange("(c p) -> p c", p=128), writes=[CONST],
            allow_slow_non_contiguous=True)
        return t

    g_a = gvec("g_a", a_norm_g, 16)
    g_c = gvec("g_c", c_norm_g, 16)
    g_qa = gvec("g_qa", a_q_a_norm_g, 4)
    g_kva = gvec("g_kva", a_kv_a_norm_g, 4)
    g_ret = gvec("g_ret", a_ret_norm_g, 8)
    cb = gvec("cb", c_conv_b, 16)
    lng = gvec("lng", c_ln_g, 16)
    lnb = gvec("lnb", c_ln_b, 16)

    xs_ring = bufring("xs", 3, [128, 512], F32)
    XS_DEFAULT = [xs_ring]
    sq_ring = bufring("sq", 3, [128, 512], BF16)
    rstd_ring = bufring("rstd", 2, [128, 512], F32)

    eps_t = sb("eps_t", [128, 1], F32)
    kb.op("dve", lambda e: e.memset(eps_t[:], EPS), writes=[CONST])

    def rsqrt_ms(out_ap, in_ap, n, in_tok, out_tok):
        kb.op("act", lambda e: e.activation(out=out_ap, in_=in_ap, func=AF.Ln, bias=eps_t[:out_ap.shape[0], :], scale=1.0 / n),
              reads=[in_tok, CONST], writes=[out_tok])
        kb.op("act", lambda e: e.activation(out=out_ap, in_=out_ap, func=AF.Exp, scale=-0.5), reads=[out_tok], writes=[out_tok])

    def norm_pass(src, src_col0, groups, gvec_t, out_t, out_toks, ncheck=D, xs_ring=None):
        xs_ring = xs_ring or XS_DEFAULT[0]
        for gi, (c0, n) in enumerate(groups):
            pss, pst_ = PS.next()
            xts = []
            for c in range(16):
                xt, xtok = xs_ring.next() if False else (None, None)
            for c in range(16):
                xt, xtok = xs_ring.next()
                dma("sp", xt[:, :n], src[c * 128:(c + 1) * 128, src_col0 + c0: src_col0 + c0 + n],
                    writes=[xtok])
                sq, sqtok = sq_ring.next()
                kb.op("act", lambda e, sq=sq, xt=xt, n=n: e.activation(out=sq[:, :n], in_=xt[:, :n], func=AF.Square),
                      reads=[xtok], writes=[sqtok])
                kb.op("pe", lambda e, pss=pss, sq=sq, n=n, c=c: e.matmul(pss[:, :n], lhsT=ones_bf[:], rhs=sq[:, :n],
                                                                      start=(c == 0), stop=(c == 15)),
                      reads=[sqtok, CONST], writes=[pst_])
            rs, rstok = rstd_ring.next()
            rsqrt_ms(rs[:, :n], pss[:, :n], ncheck, pst_, rstok)
            for c in range(16):
                xt, xtok = xs_ring.next()
                dma("sp", xt[:, :n], src[c * 128:(c + 1) * 128, src_col0 + c0: src_col0 + c0 + n],
                    writes=[xtok])
                eng = "dve"
                kb.op(eng, lambda e, xt=xt, rs=rs, c=c, c0=c0, n=n: e.scalar_tensor_tensor(
                    out=out_t[:, c, c0:c0 + n], in0=xt[:, :n], scalar=gvec_t[:, c:c + 1], in1=rs[:, :n],
                    op0=ALU.mult, op1=ALU.mult),
                    reads=[xtok, rstok, CONST], writes=[out_toks[gi]])


    def load_w(w_dram, col0, ncols, kch=16):
        wt, wtok = w_ring.next()
        dma("pool", wt[:, :kch, :ncols],
            w_dram[:, col0:col0 + ncols].rearrange("(k p) n -> p k n", p=128),
            writes=[wtok])
        return wt, wtok


    def own_groups():
        return [(0, 128), (128, 512), (640, 512), (1152, 512), (1664, 512)]

    def pre_groups():
        return [(0, 512), (512, 512), (1024, 512), (1536, 384)]

    try:
        def mm(ps_ap, lhsT, rhs, start, stop, reads, writes):
            return kb.op("pe", lambda e: e.matmul(ps_ap, lhsT=lhsT, rhs=rhs, start=start, stop=stop), reads, writes)

        def tr(ps_ap, in_ap, reads, writes):
            k = in_ap.shape[0]
            return kb.op("pe", lambda e: e.transpose(out=ps_ap, in_=in_ap, identity=ident[:k, :k]), list(reads) + [CONST], writes)

        def act(out, in_, func, reads, writes, **kw):
            return kb.op("act", lambda e: e.activation(out=out, in_=in_, func=func, **kw), reads, writes)

        def tt(eng, out, in0, in1, op, reads, writes):
            return kb.op(eng, lambda e: e.tensor_tensor(out=out, in0=in0, in1=in1, op=op), reads, writes)

        def ts(eng, out, in0, s1, s2, op0, op1, reads, writes):
            if s2 is None:
                return kb.op(eng, lambda e: e.tensor_scalar(out=out, in0=in0, scalar1=s1, scalar2=None, op0=op0), reads, writes)
            return kb.op(eng, lambda e: e.tensor_scalar(out=out, in0=in0, scalar1=s1, scalar2=s2, op0=op0, op1=op1), reads, writes)

        def stt(out, in0, scalar, in1, op0, op1, reads, writes):
            return kb.op("dve", lambda e: e.scalar_tensor_tensor(out=out, in0=in0, scalar=scalar, in1=in1, op0=op0, op1=op1),
                         reads, writes)

        def cp(eng, out, in_, reads, writes):
            if eng == "act":
                return act(out, in_, AF.Copy, reads, writes)
            return kb.op(eng, lambda e: e.tensor_copy(out=out, in_=in_), reads, writes)

        def rsqrt_ms_dve(out_ap, in_ap, n, in_tok, out_tok):
            ts("dve", out_ap, in_ap, 1.0 / n, EPS, ALU.mult, ALU.add, [in_tok], [out_tok])
            act(out_ap, out_ap, AF.Ln, [out_tok], [out_tok])
            act(out_ap, out_ap, AF.Exp, [out_tok], [out_tok], scale=-0.5)

        PS.items = PS.items[:4]
        pT4 = psum[4][:].bitcast(BF16)[:, 0:128]
        pT6 = psum[6][:].bitcast(BF16)[:, 0:128]
        PT = Ring([(pT4, ptok[4]), (pT6, ptok[6])])
        PSS = Ring([(psum[5][:, 0:128], ptok[5]), (psum[7][:, 0:128], ptok[7])])
        ACC = [(psum[6], ptok[6]), (psum[7], ptok[7])]

        krope_s = dscr("krope_s", [64, T], BF16)
        lat_s = dscr("lat_s", [512, T], BF16)
        cqn_s = dscr("cqn_s", [512, TO], BF16)

        def bc_row(name, src_ap, n):
            t = sb(name, [128, n], F32)
            dma("sp", t[:], src_ap.partition_broadcast(128), writes=[CONST])
            return t

        gk_rope_bc = bc_row("gk_rope_bc", a_k_norm_g[128:192], 64)
        gq_bc = bc_row("gq_bc", a_q_norm_g, 192)
        gk_nope = sb("gk_nope", [128, 1], F32)
        dma("sp", gk_nope[:], a_k_norm_g[0:128].rearrange("(p o) -> p o", o=1), writes=[CONST])
        kb.op("dve", lambda e: e.memset(ssrope[:], 0.0), [], ssrope_tok)
        decay = [float(x) for x in _consts()["_decay"]]

        st_ring = bufring("stg", 3, [128, 512], BF16)
        rp_ring = bufring("rp", 4, [128, 256], F32)
        craw_ring = bufring("craw", 4, [128, 512], F32)


        def rope_tm(out_bf, x_sb, nh, half, cos_ap, sin_ap, reads, wtok):
            x4 = x_sb.rearrange("p h (t d) -> p h t d", t=2)
            xsw = x4[:, :, ::-1, :] if False else None
            cb = cos_ap.unsqueeze(1).unsqueeze(1).to_broadcast([128, nh, 2, half])
            sbb = sin_ap.unsqueeze(1).to_broadcast([128, nh, half])
            (ta, tatok), (tb, tbtok) = rp_ring.next(), rp_ring.next()
            n2 = nh * 2 * half
            ta4 = ta[:, :n2].rearrange("p (h t d) -> p h t d", h=nh, t=2)
            tb4 = tb[:, :n2].rearrange("p (h t d) -> p h t d", h=nh, t=2)
            rd = list(reads) + [ROPE]
            tt("dve", ta4, x4, cb, ALU.mult, rd, [tatok])
            tt("dve", tb4[:, :, 0, :], x4[:, :, 1, :], sbb, ALU.mult, rd, [tbtok])
            tt("dve", tb4[:, :, 1, :], x4[:, :, 0, :], sbb, ALU.mult, rd + [tbtok], [tbtok])
            o4 = out_bf.rearrange("p h (t d) -> p h t d", t=2)
            tt("dve", o4[:, :, 0, :], ta4[:, :, 0, :], tb4[:, :, 0, :], ALU.subtract, [tatok, tbtok], [wtok])
            tt("dve", o4[:, :, 1, :], ta4[:, :, 1, :], tb4[:, :, 1, :], ALU.add, [tatok, tbtok, wtok], [wtok])

        def fm_sweep(w_dram, col0, ncols, groups, evac, act_t, act_toks, act_col0=0, kch=16):
            for t0 in range(0, ncols, 256):
                wt, wtok = load_w(w_dram, col0 + t0, 256, kch)
                for gi, (c0, n) in enumerate(groups):
                    for j in range(2):
                        pss, pst_ = PS.next()
                        for k in range(kch):
                            mm(pss[:, :n], wt[:, k, j * 128:(j + 1) * 128], act_t[:, k, act_col0 + c0:act_col0 + c0 + n],
                               k == 0, k == kch - 1, [wtok, act_toks[gi]], [pst_])
                        evac((t0 // 128) + j, gi, c0, n, pss, pst_)

        def lat_sweep(col0, gv, groups, toks, out_dram, out_col0):
            w0, w0tok = load_w(a_w_in, col0, 256)
            w1, w1tok = load_w(a_w_in, col0 + 256, 256)
            lm = debug.get("lat_mode", 0)
            for gi, (c0, n) in enumerate(groups):
                ssp, ssptok = ACC[gi % 2]
                raws = []
                for cc in range(4):
                    wt, wtok = (w0, w0tok) if cc < 2 else (w1, w1tok)
                    pss, pst_ = PS.next()
                    for k in range(16):
                        mm(pss[:, :n], wt[:, k, (cc % 2) * 128:(cc % 2 + 1) * 128], hT[:, k, c0:c0 + n], k == 0, k == 15,
                           [wtok, toks[gi]], [pst_])
                    raw, rawtok = craw_ring.next()
                    cp(debug.get("cpeng", "dve"), raw[:, :n], pss[:, :n], [pst_], [rawtok])
                    sq, sqtok = sq_ring.next()
                    if debug.get("sqsrc", "psum") == "psum":
                        act(sq[:, :n], pss[:, :n], AF.Square, [pst_], [sqtok])
                    else:
                        act(sq[:, :n], raw[:, :n], AF.Square, [rawtok], [sqtok])
                    mm(ssp[:, :n], ones_bf[:], sq[:, :n], cc == 0, cc == 3, [sqtok, CONST], [ssptok])
                    raws.append((raw, rawtok))
                rs, rstok = rstd_ring.next()
                rsqrt_ms(rs[:, :n], ssp[:, :n], 512, ssptok, rstok)
                for cc in range(4):
                    raw, rawtok = raws[cc]
                    st, sttok = st_ring.next()
                    stt(st[:, :n], raw[:, :n], gv[:, cc:cc + 1], rs[:, :n], ALU.mult, ALU.mult, [rawtok, rstok, CONST], [sttok])
                    dma("sp", out_dram[cc * 128:(cc + 1) * 128, out_col0 + c0:out_col0 + c0 + n], st[:, :n], reads=[sttok])
                yield

        junk = sb("junk", [128, 192], F32)
        JUNK = Tok("junk")

        def krope_sweep(blocks, tok_of_block, list_b0):
            wt, wtok = krw, krwtok
            dma("pool", wt[:, :, :64], a_w_in[:, 4096:4160].rearrange("(k p) n -> p k n", p=128), writes=[wtok])
            for lb in blocks:
                gb = list_b0 + lb
                pss, pst_ = PSS.next()
                for k in range(16):
                    mm(pss[:, :64], hT[:, k, lb * 128:(lb + 1) * 128], wt[:, k, :64], k == 0, k == 15,
                       [wtok, tok_of_block(lb)], [pst_])
                kraw, krawtok = kr_ring.next()
                cp("dve", kraw[:], pss[:, :64], [pst_], [krawtok])
                yield
                act(junk[:, :64], kraw[:], AF.Square, [krawtok, JUNK, ssrope_tok[gb]], [JUNK, ssrope_tok[gb]], accum_out=ssrope[:, gb:gb + 1])
                kr, krtok = kr_ring.next()
                tt("pool", kr[:], kraw[:], gk_rope_bc[:], ALU.mult, [krawtok, CONST], [krtok])
                krb, krbtok = krb_ring.next()
                rope_tm(krb[:].rearrange("p (h d) -> p h d", h=1), kr[:].rearrange("p (h d) -> p h d", h=1), 1, 32,
                        cosm[:, gb, :], sinm[:, gb, :], [krtok], krbtok)
                pt, pttok = PT.next()
                tr(pt[:64, :], krb[:], [krbtok], [pttok])
                krt, krttok = krt_ring.next()
                cp("act", krt[:], pt[:64, :], [pttok], [krttok])
                dma("act", krope_s[:, gb * 128:(gb + 1) * 128], krt[:], reads=[krttok])
                yield

        state = sb("state", [128, 8, 128], F32)
        state_bf = sb("state_bf", [128, 8, 128], BF16)
        ST_tok = [Tok(f"state{h}") for h in range(8)]
        STB_tok = [Tok(f"stateb{h}") for h in range(8)]
        kb.op("dve", lambda e: e.memset(state[:], 0.0), writes=ST_tok)
        kb.op("pool", lambda e: e.memset(state_bf[:], 0.0), writes=STB_tok)
        PTW = Ring([(psum[4][:].bitcast(BF16), ptok[4]), (psum[6][:].bitcast(BF16), ptok[6])])
        PSW = Ring([(psum[5], ptok[5]), (psum[7], ptok[7])])

        def v3(ap):
            return ap.rearrange("p (h d) -> p h d", h=2)

        def run_threads(gens):
            gens = list(gens)
            while gens:
                for g_ in list(gens):
                    try:
                        next(g_)
                    except StopIteration:
                        gens.remove(g_)

        def ret_thread(hp, RS, blocks, tok_of_block, list_b0, own):
            h0 = hp * 2
            wk, wktok = load_w(a_w_in, 1024 + hp * 256, 256)
            wv, wvtok = load_w(a_w_in, 2048 + hp * 256, 256)
            if own:
                wq, wqtok = load_w(a_w_in, hp * 256, 256)

            def stage_a(lb):
                gb = list_b0 + lb
                ht = tok_of_block(lb)
                cols = slice(lb * 128, (lb + 1) * 128)
                A = {"cols": cols}

                def proj(w, wtok_):
                    pss, pst_ = PS.next()
                    for k in range(16):
                        mm(pss[:, :256], hT[:, k, cols], w[:, k, :], k == 0, k == 15, [wtok_, ht], [pst_])
                    return pss, pst_
                if own:
                    pss, pst_ = proj(wq, wqtok)
                    A["qtm"], A["qtmtok"] = RS["qtm"].next()
                    rope_tm(v3(A["qtm"][:]), v3(pss[:, :256]), 2, 64, cosr[:, gb, :], sinr[:, gb, :], [pst_], A["qtmtok"])
                    A["gt"], A["gttok"] = RS["gt"].next()
                    dma("sp", A["gt"][:], gate_s[h0 * 128:(h0 + 2) * 128, cols].rearrange("(j p) c -> p j c", p=128),
                        writes=[A["gttok"]])
                    yield
                pss, pst_ = proj(wk, wktok)
                A["ktm"], A["ktmtok"] = RS["ktm"].next()
                rope_tm(v3(A["ktm"][:]), v3(pss[:, :256]), 2, 64, cosr[:, gb, :], sinr[:, gb, :], [pst_], A["ktmtok"])
                yield
                pss, pst_ = proj(wv, wvtok)
                A["vtm"], A["vtmtok"] = RS["vtm"].next()
                cp("act", A["vtm"][:], pss[:, :256], [pst_], [A["vtmtok"]])
                yield
                return A

            def stage_b(A):
                cols = A["cols"]
                ktm, ktmtok, vtm, vtmtok = A["ktm"], A["ktmtok"], A["vtm"], A["vtmtok"]
                sts = [ST_tok[h0], ST_tok[h0 + 1]]
                stbs = [STB_tok[h0], STB_tok[h0 + 1]]
                ks, kstok = RS["b256"].next()
                tt("dve", v3(ks[:]), v3(ktm[:]), toend[:, h0:h0 + 2].unsqueeze(2).to_broadcast([128, 2, 128]), ALU.mult,
                   [ktmtok, CONST], [kstok])
                if own:
                    qtm, qtmtok = A["qtm"], A["qtmtok"]
                    qs, qstok = RS["b256"].next()
                    tt("dve", v3(qs[:]), v3(qtm[:]), fscol[:, h0:h0 + 2].unsqueeze(2).to_broadcast([128, 2, 128]), ALU.mult,
                       [qtmtok, CONST], [qstok])
                    pT, pTtok = PTW.next()
                    for j in range(2):
                        tr(pT[:, j * 128:(j + 1) * 128], qtm[:, j * 128:(j + 1) * 128], [qtmtok], [pTtok])
                        tr(pT[:, 256 + j * 128:256 + (j + 1) * 128], qs[:, j * 128:(j + 1) * 128], [qstok], [pTtok])
                        tr(pT[:, 512 + j * 128:512 + (j + 1) * 128], ktm[:, j * 128:(j + 1) * 128], [ktmtok], [pTtok])
                    tT, tTtok = RS["tT"].next()
                    cp("act", tT[:], pT[:, :768], [pTtok], [tTtok])
                    yield
                    sc, sctok = PSW.next()
                    for j in range(2):
                        mm(sc[:, j * 128:(j + 1) * 128], tT[:, 512 + j * 128:512 + (j + 1) * 128], tT[:, j * 128:(j + 1) * 128],
                           True, True, [tTtok], [sctok])
                    AT, ATtok = RS["b256"].next()
                    tt("dve", v3(AT[:]), v3(sc[:, :256]), dmaskT[:, h0:h0 + 2, :], ALU.mult, [sctok, CONST], [ATtok])
                    yield
                    rp, rptok = PSW.next()
                    for j in range(2):
                        js = slice(j * 128, (j + 1) * 128)
                        mm(rp[:, js], vtm[:, js], AT[:, js], True, False, [vtmtok, ATtok], [rptok])
                        mm(rp[:, js], state_bf[:, h0 + j, :], tT[:, 256 + j * 128:256 + (j + 1) * 128], False, True,
                           [stbs[j], tTtok], [rptok])
                    oraw, orawtok = RS["f256"].next()
                    cp("dve", oraw[:], rp[:, :256], [rptok], [orawtok])
                    sq, sqtok = RS["b256"].next()
                    act(sq[:], oraw[:], AF.Square, [orawtok], [sqtok])
                    yield
                kv, kvtok = PSW.next()
                for j in range(2):
                    js = slice(j * 128, (j + 1) * 128)
                    mm(kv[:, js], ks[:, js], vtm[:, js], True, True, [kstok, vtmtok], [kvtok])
                for j in range(2):
                    js = slice(j * 128, (j + 1) * 128)
                    stt(state[:, h0 + j, :], state[:, h0 + j, :], decay[h0 + j], kv[:, js], ALU.mult, ALU.add,
                        [sts[j], kvtok], [sts[j]])
                cp("pool", state_bf[:, h0:h0 + 2, :], state[:, h0:h0 + 2, :], sts, stbs)
                yield
                if own:
                    ssp, ssptok = PSW.next()
                    for j in range(2):
                        js = slice(j * 128, (j + 1) * 128)
                        mm(ssp[:, js], ones_bf[:], sq[:, js], True, True, [sqtok, CONST], [ssptok])
                    rs, rstok = RS["f256"].next()
                    rsqrt_ms_dve(rs[:], ssp[:, :256], 128, ssptok, rstok)
                    tt("dve", rs[:], oraw[:], rs[:], ALU.mult, [orawtok, rstok], [rstok])
                    tt("pool", v3(rs[:]), v3(rs[:]), g_ret[:, h0:h0 + 2].unsqueeze(2).to_broadcast([128, 2, 128]), ALU.mult,
                       [rstok, CONST], [rstok])
                    st, sttok = st_ring.next()
                    tt("pool", st[:, :256], rs[:], A["gt"][:].rearrange("p j c -> p (j c)"), ALU.mult, [rstok, A["gttok"]], [sttok])
                    dma("pool", mix_s[h0 * 128:(h0 + 2) * 128, cols].rearrange("(j p) c -> p j c", p=128), v3(st[:, :256]),
                        reads=[sttok])
                    yield

            pendA = None
            for lb in blocks:
                ga = stage_a(lb)
                gb_ = stage_b(pendA) if pendA is not None else iter(())
                A = None
                a_done = b_done = False
                while not (a_done and b_done):
                    if not a_done:
                        try:
                            next(ga)
                        except StopIteration as e_:
                            A = e_.value
                            a_done = True
                    if not b_done:
                        try:
                            next(gb_)
                        except StopIteration:
                            b_done = True
                    yield
                pendA = A
            for _ in stage_b(pendA):
                yield

        def ret_sweep(blocks, tok_of_block, list_b0, own, extra=()):
            for hp0 in (0, 2):
                run_threads([ret_thread(hp0 + i, RSETS[i], blocks, tok_of_block, list_b0, own) for i in range(2)]
                            + (list(extra) if hp0 == 0 else []))

        def gate_evac(dst, dst_col0):
            def f(cc, gi, c0, n, pss, pst_):
                st, sttok = st_ring.next()
                act(st[:, :n], pss[:, :n], AF.Silu, [pst_], [sttok])
                dma("act", dst[cc * 128:(cc + 1) * 128, dst_col0 + c0:dst_col0 + c0 + n], st[:, :n], reads=[sttok])
            return f

        es_rope = ExitStack()
        es_rope_r = ExitStack()
        open_stacks.extend([es_rope, es_rope_r])

        def sbx(stack, name, shape, dt):
            return stack.enter_context(nc.sbuf_tensor(name, list(shape), dt))

        cosm = sbx(es_rope, "cosm", [128, NB, 32], F32)
        sinm = sbx(es_rope, "sinm", [128, NB, 32], F32)
        w_ring = Ring([(sbx(es_rope_r, f"wt{i}", [128, 16, 256], BF16), Tok(f"wt{i}")) for i in range(6)])
        hT = sbx(es_rope_r, "hT", [128, 16, TO], BF16)
        cosr = sbx(es_rope_r, "cosr", [128, NB, 64], F32)
        sinr = sbx(es_rope_r, "sinr", [128, NB, 64], F32)

        def bufring_s(stack, name, n, shape, dt):
            return Ring([(sbx(stack, f"{name}{i}", shape, dt), Tok(f"{name}{i}")) for i in range(n)])
        RSETS = []
        for ti in range(2):
            RSETS.append(dict(
                ktm=bufring_s(es_rope_r, f"ktm{ti}_", 2, [128, 256], BF16),
                vtm=bufring_s(es_rope_r, f"vtm{ti}_", 2, [128, 256], BF16),
                qtm=bufring_s(es_rope_r, f"qtm{ti}_", 2, [128, 256], BF16),
                b256=bufring_s(es_rope_r, f"b256{ti}_", 4, [128, 256], BF16),
                f256=bufring_s(es_rope_r, f"f256{ti}_", 2, [128, 256], F32),
                tT=bufring_s(es_rope_r, f"tT{ti}_", 1, [128, 768], BF16),
                gt=bufring_s(es_rope_r, f"gt{ti}_", 2, [128, 2, 128], BF16)))
        kr_ring = bufring_s(es_rope_r, "kr", 4, [128, 64], F32)
        krb_ring = bufring_s(es_rope_r, "krb", 2, [128, 64], BF16)
        krt_ring = bufring_s(es_rope_r, "krt", 2, [64, 128], BF16)
        krw = sbx(es_rope_r, "krw", [128, 16, 64], BF16)
        krwtok = Tok("krw")
        ROPE = Tok("rope")
        with ExitStack() as tmp:
            r2 = xs_ring.items[0][0][:, 0:512].rearrange("p (b f) -> p b f", b=8)
            rf = xs_ring.items[1][0][:, 0:512].rearrange("p (b f) -> p b f", b=8)
            ri = xs_ring.items[2][0][:, 0:512].bitcast(I32).rearrange("p (b f) -> p b f", b=8)
            RTL = [it[1] for it in xs_ring.items]
        if True:
            def rope_tables():
              for (cs, sn, inv_t, half) in ((cosm, sinm, invm, 32), (cosr, sinr, invr, 64)):
                for q8 in range(4):
                    bs = slice(q8 * 8, q8 * 8 + 8)
                    for (dst, shift) in ((sn, 0.0), (cs, 0.25)):
                        tt("dve", r2[:, :, :half], posf[:, bs].unsqueeze(2).to_broadcast([128, 8, half]),
                           inv_t[:, :half].unsqueeze(1).to_broadcast([128, 8, half]), ALU.mult, [CONST] + RTL, RTL)
                        ts("dve", r2[:, :, :half], r2[:, :, :half], 1.0 / (2 * math.pi), shift, ALU.mult, ALU.add, RTL, RTL)
                        cp("dve", ri[:, :, :half], r2[:, :, :half], RTL, RTL)
                        cp("dve", rf[:, :, :half], ri[:, :, :half], RTL, RTL)
                        tt("dve", r2[:, :, :half], r2[:, :, :half], rf[:, :, :half], ALU.subtract, RTL, RTL)
                        ts("dve", rf[:, :, :half], r2[:, :, :half], 0.5, None, ALU.is_gt, None, RTL, RTL)
                        tt("dve", r2[:, :, :half], r2[:, :, :half], rf[:, :, :half], ALU.subtract, RTL, RTL)
                        ts("dve", rf[:, :, :half], r2[:, :, :half], -0.5, None, ALU.is_lt, None, RTL, RTL)
                        tt("dve", r2[:, :, :half], r2[:, :, :half], rf[:, :, :half], ALU.add, RTL, RTL)
                        act(dst[:, bs, :], r2[:, :, :half], AF.Sin, RTL, [ROPE] + RTL, scale=2 * math.pi)
                    yield

        ckpt(1)
        PG = pre_groups()
        OG = own_groups()

        def tokP(lb):
            return hT_tok[lb // 4]

        def tokO(lb):
            return hT_tok[0] if lb == 0 else hT_tok[1 + (lb - 1) // 4]

        xs_big = Ring(xs_ring.items + craw_ring.items)
        norm_pass(xT, 0, PG, g_a, hT, hT_tok[:4], xs_ring=xs_big)
        ckpt(2)
        run_threads([lat_sweep(3584, g_kva, PG, hT_tok[:4], lat_s, 0), rope_tables()])
        ckpt(3)
        ckpt(4)
        ret_sweep(range(15), tokP, 0, False, extra=[krope_sweep(range(15), tokP, 0)])
        ckpt(5)
        norm_pass(xT, TP, OG, g_a, hT, hT_tok[:5], xs_ring=xs_big)
        ckpt(6)
        fm_sweep(a_w_in, 4160, 2048, OG, gate_evac(gate_s, 0), hT, hT_tok[:5])
        kb.barrier()
        for _ in lat_sweep(3584, g_kva, OG, hT_tok[:5], lat_s, TP):
            pass
        ckpt(8)
        for _ in lat_sweep(3072, g_qa, OG, hT_tok[:5], cqn_s, 0):
            pass
        ckpt(9)
        ckpt(10)
        ret_sweep(range(17), tokO, 15, True, extra=[krope_sweep(range(17), tokO, 15)])
        ckpt(11)
        kb.barrier()
        es_rope_r.close()

        with ExitStack() as ph3:
            latT = sbx(ph3, "latT", [128, 4, T], BF16)
            kropeT = sbx(ph3, "kropeT", [128, T], BF16)
            wkvb_ring = Ring([(sbx(ph3, f"wkvb{i}", [128, 4, 256], BF16), Tok(f"wkvb{i}")) for i in range(2)])
            wqb_ring = Ring([(sbx(ph3, f"wqb{i}", [128, 4, 192], BF16), Tok(f"wqb{i}")) for i in range(2)])
            cqn_ring = Ring([(sbx(ph3, f"cqnb{i}", [128, 4, 128], BF16), Tok(f"cqnb{i}")) for i in range(3)])
            BS = []
            for i in range(2):
                BS.append(dict(KT=sbx(ph3, f"KT{i}", [128, T], BF16), Vh=sbx(ph3, f"Vh{i}", [128, NB, 128], BF16),
                               qT=sbx(ph3, f"qT{i}", [128, TO], BF16), qrT=sbx(ph3, f"qrT{i}", [128, TO], BF16),
                               kscale=sbx(ph3, f"kscale{i}", [128, NB], F32),
                               KTt=Tok(f"KT{i}"), VHt=Tok(f"Vh{i}"), QTt=Tok(f"qT{i}"), KSC=Tok(f"ksc{i}")))
            qn = sbx(ph3, "qn", [128, 192], F32)
            qbf = sbx(ph3, "qbf", [128, 192], BF16)
            qss = sbx(ph3, "qss", [128, 1], F32)
            rden_ring = Ring([(sbx(ph3, f"rden{i}", [128, 512], F32), Tok(f"rden{i}")) for i in range(2)])
            attf_ring = Ring([(sbx(ph3, f"attf{i}", [128, 512], F32), Tok(f"attf{i}")) for i in range(2)])
            gta_ring = Ring([(sbx(ph3, f"gta{i}", [128, 512], BF16), Tok(f"gta{i}")) for i in range(2)])
            pt_ring = Ring([(sbx(ph3, f"ptile{i}", [128, 512], BF16), Tok(f"ptile{i}")) for i in range(5)])
            psum_acc = [(sbx(ph3, f"psm{i}", [128, 512], F32), Tok(f"psm{i}")) for i in range(2)]
            ones_f32 = sbx(ph3, "ones_f32", [128, 128], F32)
            kb.op("pool", lambda e: e.memset(ones_f32[:], 1.0), writes=[CONST])
            LAT, KRT, CQN, WKV, WQ = Tok("lat"), Tok("krt"), Tok("cqn"), Tok("wkv"), Tok("wq")
            QN, QBF, QSS = Tok("qn"), Tok("qbf"), Tok("qss")
            for c in range(4):
                dma("sp", latT[:, c, :], lat_s[c * 128:(c + 1) * 128, :], writes=[LAT])
            kb.op("pool", lambda e: e.memset(kropeT[64:128, :], 0.0), writes=[KRT])
            dma("sp", kropeT[0:64, :], krope_s, writes=[KRT])
            for i_ in range(2):
                kb.op("pool", lambda e, i_=i_: e.memset(BS[i_]["qrT"][64:128, :], 0.0), writes=[BS[i_]["QTt"]])
            PT = Ring([(pT4, ptok[4])])
            PS_save = PS.items
            PS.items = PS_save[:3]
            PSS = PS
            ACCP = [((psum[3], ptok[3]), (psum[6], ptok[6])), ((psum[5], ptok[5]), (psum[7], ptok[7]))]
            def prep(h, B):
                KT, Vh, qT, qrT, kscale = B["KT"], B["Vh"], B["qT"], B["qrT"], B["kscale"]
                KTt, VHt, QTt, KSC = B["KTt"], B["VHt"], B["QTt"], B["KSC"]
                wkvb, WKV = wkvb_ring.next()
                dma("pool", wkvb[:], a_w_kv_b[:, h * 256:(h + 1) * 256].rearrange("(k p) n -> p k n", p=128), writes=[WKV])
                wqb, WQ = wqb_ring.next()
                dma("pool", wqb[:], a_w_q_b[:, h * 192:(h + 1) * 192].rearrange("(k p) n -> p k n", p=128), writes=[WQ])
                kss, ksstok = psum[4][:, 0:128], ptok[4]
                for g in range(8):
                    gs = slice(g * 512, (g + 1) * 512)
                    pss, pst_ = PS.next()
                    for k in range(4):
                        mm(pss[:], wkvb[:, k, 0:128], latT[:, k, gs], k == 0, k == 3, [WKV, LAT], [pst_])
                    sq, sqtok = sq_ring.next()
                    act(sq[:], pss[:], AF.Square, [pst_], [sqtok])
                    ts("dve", KT[:, gs], pss[:], gk_nope[:, 0:1], None, ALU.mult, None, [pst_, CONST], [KTt])
                    for b4 in range(4):
                        b = g * 4 + b4
                        mm(kss[:, b:b + 1], sq[:, b4 * 128:(b4 + 1) * 128], ones_bf[:, 0:1], True, True, [sqtok, CONST], [ksstok])
                    yield
                tt("dve", kscale[:], kss[:, :NB], ssrope[:], ALU.add, [ksstok] + ssrope_tok, [KSC])
                rsqrt_ms(kscale[:], kscale[:], 192, KSC, KSC)
                ts("dve", kscale[:], kscale[:], 192.0 ** -0.5, None, ALU.mult, None, [KSC], [KSC])
                for g in range(8):
                    pss, pst_ = PS.next()
                    for b4 in range(4):
                        b = g * 4 + b4
                        for k in range(4):
                            mm(pss[:, b4 * 128:(b4 + 1) * 128], latT[:, k, b * 128:(b + 1) * 128],
                               wkvb[:, k, 128:256], k == 0, k == 3, [WKV, LAT], [pst_])
                    cp("act", Vh[:, g * 4:(g + 1) * 4, :].rearrange("p b d -> p (b d)"), pss[:], [pst_], [VHt])
                    yield
                for lb in range(17):
                    gb = 15 + lb
                    cols = slice(lb * 128, (lb + 1) * 128)
                    pss, pst_ = PS.next()
                    cqb, CQN = cqn_ring.next()
                    dma("sp", cqb[:], cqn_s[:, cols].rearrange("(k p) n -> p k n", p=128), writes=[CQN])
                    for k in range(4):
                        mm(pss[:, :192], cqb[:, k, :], wqb[:, k, :], k == 0, k == 3, [CQN, WQ], [pst_])
                    kb.op("dve", lambda e: e.memset(qss[:], 0.0), [QSS], [QSS])
                    act(junk[:, :192], pss[:, :192], AF.Square, [pst_, JUNK, QSS], [JUNK, QSS], accum_out=qss[:, 0:1])
                    rsqrt_ms(qss[:], qss[:], 192, QSS, QSS)
                    stt(qn[:], pss[:, :192], qss[:, 0:1], gq_bc[:], ALU.mult, ALU.mult, [pst_, QSS, CONST], [QN])
                    cp("pool", qbf[:, 0:128], qn[:, 0:128], [QN], [QBF])
                    rope_tm(qbf[:, 128:192].rearrange("p (h d) -> p h d", h=1), qn[:, 128:192].rearrange("p (h d) -> p h d", h=1),
                            1, 32, cosm[:, gb, :], sinm[:, gb, :], [QN], QBF)
                    yield
                    p1, p1tok = PT.next()
                    tr(p1, qbf[:, 0:128], [QBF], [p1tok])
                    cp("act", qT[:, cols], p1, [p1tok], [QTt])
                    p2, p2tok = PT.next()
                    tr(p2[:64, :], qbf[:, 128:192], [QBF], [p2tok])
                    cp("act", qrT[0:64, cols], p2[:64, :], [p2tok], [QTt])
                    yield

            def attn(h, B):
                KT, Vh, qT, qrT, kscale = B["KT"], B["Vh"], B["qT"], B["qrT"], B["kscale"]
                KTt, VHt, QTt, KSC = B["KTt"], B["VHt"], B["QTt"], B["KSC"]
                work = []
                for gi, (c0, n) in enumerate(OG):
                    qb0 = (TP + c0) // 128
                    nk = qb0 + n // 128
                    for j in range(nk):
                        work.append((gi, c0, n, qb0, nk, j))

                def s_stage(w):
                    gi, c0, n, qb0, nk, j = w
                    lo = max(0, j - qb0) * 128
                    pss, pst_ = PS.next()
                    mm(pss[:, lo:n], KT[:, j * 128:(j + 1) * 128], qT[:, c0 + lo:c0 + n], True, False, [KTt, QTt], [pst_])
                    mm(pss[:, lo:n], kropeT[:, j * 128:(j + 1) * 128], qrT[:, c0 + lo:c0 + n], False, True, [KRT, QTt], [pst_])
                    pt, pttok = pt_ring.next()
                    act(pt[:, lo:n], pss[:, lo:n], AF.Exp, [pst_, KSC, CONST], [pttok], scale=kscale[:, j:j + 1],
                        bias=kbias[:, j:j + 1])
                    if j >= qb0:
                        tt("pool", pt[:, lo:lo + 128], pt[:, lo:lo + 128], tri[:], ALU.mult, [pttok, CONST], [pttok])
                    return pt, pttok, lo

                def pv_stage(w, sres):
                    gi, c0, n, qb0, nk, j = w
                    pt, pttok, lo = sres
                    (ao, aotok), (ad, adtok) = ACCP[(h * 5 + gi) % 2]
                    mm(ao[:, lo:n], Vh[:, j, :], pt[:, lo:n], j == 0, j == nk - 1, [VHt, pttok], [aotok])
                    psm, psmtok = psum_acc[(h * 5 + gi) % 2]
                    if j == 0:
                        cp("dve", psm[:, :n], pt[:, :n], [pttok, psmtok], [psmtok])
                    else:
                        tt("dve", psm[:, lo:n], psm[:, lo:n], pt[:, lo:n], ALU.add, [pttok, psmtok], [psmtok])
                    if j == nk - 1:
                        deferred.append([2, lambda: epilogue(gi, c0, n, ao, aotok, ad, adtok, psm, psmtok)])

                def epilogue(gi, c0, n, ao, aotok, ad, adtok, psm, psmtok):
                    if True:
                        mm(ad[:, :n], ones_f32[:], psm[:, :n], True, True, [CONST, psmtok], [adtok])
                        rden, RDEN = rden_ring.next()
                        attf, ATTF = attf_ring.next()
                        gta, GTA = gta_ring.next()
                        ts("dve", rden[:, :n], ad[:, :n], 1e-30, None, ALU.add, None, [adtok], [RDEN])
                        act(rden[:, :n], rden[:, :n], AF.Ln, [RDEN], [RDEN])
                        act(rden[:, :n], rden[:, :n], AF.Exp, [RDEN], [RDEN], scale=-1.0)
                        tt("dve", attf[:, :n], ao[:, :n], rden[:, :n], ALU.mult, [aotok, RDEN], [ATTF])
                        dma("sp", gta[:, :n], gate_s[(8 + h) * 128:(9 + h) * 128, c0:c0 + n], writes=[GTA])
                        st, sttok = st_ring.next()
                        tt("pool", st[:, :n], attf[:, :n], gta[:, :n], ALU.mult, [ATTF, GTA], [sttok])
                        dma("pool", mix_s[(8 + h) * 128:(9 + h) * 128, c0:c0 + n], st[:, :n], reads=[sttok])

                deferred = []

                def tick():
                    for d_ in list(deferred):
                        d_[0] -= 1
                        if d_[0] <= 0:
                            deferred.remove(d_)
                            d_[1]()

                pend = []
                for w in work:
                    pend.append((w, s_stage(w)))
                    if len(pend) > 2:
                        pv_stage(*pend.pop(0))
                    tick()
                    yield
                while pend:
                    pv_stage(*pend.pop(0))
                    tick()
                    yield
                while deferred:
                    tick()
                    yield

            for _ in prep(0, BS[0]):
                pass
            for h in range(8):
                ga = attn(h, BS[h % 2])
                gp = prep(h + 1, BS[(h + 1) % 2]) if h < 7 else iter(())
                a_alive = p_alive = True
                while a_alive:
                    for _ in range(3):
                        try:
                            next(ga)
                        except StopIteration:
                            a_alive = False
                            break
                    if p_alive:
                        try:
                            next(gp)
                        except StopIteration:
                            p_alive = False
                for _ in gp:
                    pass
            kb.barrier()
            PS.items = PS_save
        es_rope.close()

        ckpt(12)
        def load_act(src, ncols_total, groups, src_col0=0):
            lo = min(c0 for c0, n in groups)
            hi = max(c0 + n for c0, n in groups)
            for c in range(16):
                dma("sp" if c % 2 == 0 else "act", hT[:, c, lo:hi], src[c * 128:(c + 1) * 128, src_col0 + lo:src_col0 + hi],
                    writes=[hT_tok[gi] for gi in range(len(groups))])

        def resid_evac(res_src, res_col0, dst, dst_col0):
            pend = []

            def flush():
                while pend:
                    pend.pop(0)()

            def f(cc, gi, c0, n, pss, pst_):
                xt, xtok = xs_ring.next()
                dma("sp", xt[:, :n], res_src[cc * 128:(cc + 1) * 128, res_col0 + c0:res_col0 + c0 + n], writes=[xtok])
                ev, evtok = ev_ring.next()
                tt("dve", ev[:, :n], pss[:, :n], xt[:, :n], ALU.add, [pst_, xtok], [evtok])
                flush()
                pend.append(lambda: dma("sp", dst[cc * 128:(cc + 1) * 128, dst_col0 + c0:dst_col0 + c0 + n], ev[:, :n],
                                        reads=[evtok]))
            f.flush = flush
            return f

        es_C = ExitStack()
        open_stacks.append(es_C)
        w_ring = Ring([(sbx(es_C, f"wtc{i}", [128, 16, 256], BF16), Tok(f"wtc{i}")) for i in range(4)])
        hT = sbx(es_C, "hTc", [128, 16, TO], BF16)
        xs_ring = Ring([(sbx(es_C, f"xsc{i}", [128, 512], F32), Tok(f"xsc{i}")) for i in range(8)])
        ev_ring = Ring([(sbx(es_C, f"evc{i}", [128, 512], F32), Tok(f"evc{i}")) for i in range(3)])
        load_act(mix_s, TO, OG)
        _rev = resid_evac(xT, TP, x1_s, 0)
        fm_sweep(a_w_out, 0, 2048, OG, _rev, hT, hT_tok[:5])
        _rev.flush()
        kb.barrier()

        ckpt(13)
        norm_pass(x1_s, 0, OG, g_c, hT, hT_tok[:5], xs_ring=xs_ring)
        OG4 = [(128 + i * 512, 512) for i in range(4)]
        with ExitStack() as ph5:
            cw = sbx(ph5, "cw", [128, 16, 31], F32)
            dma("sp", cw[:], c_conv_wT.rearrange("(c p) k -> p c k", p=128), writes=[CONST])
            u_ring = Ring([(sbx(ph5, f"u{i}", [128, TO], BF16), Tok(f"u{i}")) for i in range(3)])
            dg_ring = Ring([(sbx(ph5, f"dg{i}", [128, 31, 128], BF16), Tok(f"dg{i}")) for i in range(2)])
            sg_ring = Ring([(sbx(ph5, f"sg{i}", [128, 512], F32), Tok(f"sg{i}")) for i in range(2)])

            def conv_pe(c, u, utok):
                dg, dgtok = dg_ring.next()
                tt("dve", dg[:], ident[:].unsqueeze(1).to_broadcast([128, 31, 128]),
                   cw[:, c, :].unsqueeze(2).to_broadcast([128, 31, 128]), ALU.mult, [CONST], [dgtok])
                for g in range(4):
                    pc, pctok = PS.next()
                    for kk in range(31):
                        o = 98 + kk + g * 512
                        mm(pc[:], dg[:, kk, :], u[:, o:o + 512], kk == 0, kk == 30, [dgtok, utok], [pctok])
                    ev, evtok = ev_ring.next()
                    act(ev[:], pc[:], AF.Identity, [pctok, CONST], [evtok], bias=cb[:, c:c + 1])
                    dma("act", v_s[c * 128:(c + 1) * 128, g * 512:(g + 1) * 512], ev[:], reads=[evtok])

            prev = None
            for t0 in range(0, 2048, 256):
                wa, watok = load_w(c_w_in, t0, 256)
                wb, wbtok = load_w(c_w_in, 2048 + t0, 256)
                for j in range(2):
                    c = t0 // 128 + j
                    u, utok = u_ring.next()
                    for gi, (c0, n) in enumerate(OG):
                        pa, patok = PS.next()
                        for k in range(16):
                            mm(pa[:, :n], wa[:, k, j * 128:(j + 1) * 128], hT[:, k, c0:c0 + n], k == 0, k == 15, [watok, hT_tok[gi]], [patok])
                        pb, pbtok = PS.next()
                        for k in range(16):
                            mm(pb[:, :n], wb[:, k, j * 128:(j + 1) * 128], hT[:, k, c0:c0 + n], k == 0, k == 15, [wbtok, hT_tok[gi]], [pbtok])
                        sg, sgtok = sg_ring.next()
                        act(sg[:, :n], pb[:, :n], AF.Sigmoid, [pbtok], [sgtok])
                        tt("dve", u[:, c0:c0 + n], pa[:, :n], sg[:, :n], ALU.mult, [patok, sgtok, utok], [utok])
                    if prev is not None:
                        conv_pe(*prev)
                    prev = (c, u, utok)
            conv_pe(*prev)
            fm_sweep(c_w_in, 4096, 2048, OG4, gate_evac(gate1_s, -128), hT, hT_tok[1:5])
            kb.barrier()

        es_C.close()
        with ExitStack() as ph6:
            wout = sbx(ph6, "wout", [128, 16, 2048], BF16)
            ev_ring = Ring([(sbx(ph6, f"evd{i}", [128, 512], F32), Tok(f"evd{i}")) for i in range(4)])
            xs_ring = Ring([(sbx(ph6, f"xsd{i}", [128, 512], F32), Tok(f"xsd{i}")) for i in range(4)])
            WOUT = [Tok(f"wout{i}") for i in range(8)]
            for i in range(8):
                dma("pool", wout[:, :, i * 256:(i + 1) * 256], c_w_out[:, i * 256:(i + 1) * 256].rearrange("(k p) n -> p k n", p=128),
                    writes=[WOUT[i]])
            vg = sbx(ph6, "vg", [128, 16, 512], F32)
            mg_ring = Ring([(sbx(ph6, f"mg{i}", [128, 16, 512], BF16), [Tok(f"mg{i}_{c}") for c in range(16)]) for i in range(2)])
            mean = sbx(ph6, "mean", [128, 512], F32)
            m2 = sbx(ph6, "m2", [128, 512], F32)
            lrs = sbx(ph6, "lrs", [128, 512], F32)
            sl_ring = Ring([(sbx(ph6, f"sl{i}", [128, 512], F32), Tok(f"sl{i}")) for i in range(3)])
            g1_ring = Ring([(sbx(ph6, f"g1{i}", [128, 512], BF16), Tok(f"g1{i}")) for i in range(3)])
            VG = [Tok(f"vg{c}") for c in range(16)]
            MEAN, M2, LRS = Tok("mean"), Tok("m2"), Tok("lrs")

            def ln_group(g):
                gs = slice(g * 512, (g + 1) * 512)
                mg, mgtoks = mg_ring.next()
                pm, pmtok = psum[6], ptok[6]
                pq, pqtok = psum[7], ptok[7]
                for c in range(16):
                    dma("sp", vg[:, c, :], v_s[c * 128:(c + 1) * 128, gs], writes=[VG[c]])
                    vb, vbtok = st_ring.next()
                    cp("dve", vb[:], vg[:, c, :], [VG[c]], [vbtok])
                    sq, sqtok = sq_ring.next()
                    act(sq[:], vg[:, c, :], AF.Square, [VG[c]], [sqtok])
                    mm(pm[:], ones_bf[:], vb[:], c == 0, c == 15, [vbtok, CONST], [pmtok])
                    mm(pq[:], ones_bf[:], sq[:], c == 0, c == 15, [sqtok, CONST], [pqtok])
                    yield
                act(mean[:], pm[:], AF.Copy, [pmtok], [MEAN], scale=1.0 / 2048)
                tt("dve", m2[:], mean[:], mean[:], ALU.mult, [MEAN], [M2])
                stt(lrs[:], pq[:], 1.0 / 2048, m2[:], ALU.mult, ALU.subtract, [pqtok, M2], [LRS])
                rsqrt_ms(lrs[:], lrs[:], 1, LRS, LRS)
                def st1(c):
                    tt("dve", vg[:, c, :], vg[:, c, :], mean[:], ALU.subtract, [VG[c], MEAN], [VG[c]])
                    tt("pool", vg[:, c, :], vg[:, c, :], lrs[:], ALU.mult, [VG[c], LRS], [VG[c]])

                def st2(c):
                    sl, SL = sl_ring.next()
                    act(sl[:], vg[:, c, :], AF.Silu, [VG[c], CONST], [SL], scale=lng[:, c:c + 1], bias=lnb[:, c:c + 1])
                    g1, G1 = g1_ring.next()
                    dma("sp", g1[:], gate1_s[c * 128:(c + 1) * 128, gs], writes=[G1])
                    return sl, SL, g1, G1

                def st3(c, sl, SL, g1, G1):
                    tt("dve", mg[:, c, :], sl[:], g1[:], ALU.mult, [SL, G1], [mgtoks[c]])

                r2s = {}
                for c in range(16 + 2):
                    if c < 16:
                        st1(c)
                    if 1 <= c <= 16:
                        r2s[c - 1] = st2(c - 1)
                    if c >= 2:
                        st3(c - 2, *r2s.pop(c - 2))
                    yield
                LNRES[g] = (g, mg, mgtoks)

            def out_group(g, mg, mgtoks):
                ev_f = resid_evac(x1_s, 128, outT, 0)
                for m in range(16):
                    pss, pst_ = PS.next()
                    for k in range(16):
                        mm(pss[:], wout[:, k, m * 128:(m + 1) * 128], mg[:, k, :], k == 0, k == 15, [WOUT[m // 2], mgtoks[k]], [pst_])
                    ev_f(m, g, g * 512, 512, pss, pst_)
                    yield
                ev_f.flush()

            LNRES = {}
            for _ in ln_group(0):
                pass
            for g in range(4):
                go = out_group(*LNRES[g])
                gl = ln_group(g + 1) if g < 3 else iter(())
                o_alive = l_alive = True
                while o_alive or l_alive:
                    if o_alive:
                        try:
                            next(go)
                        except StopIteration:
                            o_alive = False
                    for _ in range(2):
                        if l_alive:
                            try:
                                next(gl)
                            except StopIteration:
                                l_alive = False
    except _Stop:
        for st_ in reversed(open_stacks):
            st_.close()
    with ExitStack() as ses:
        sems = {e: ses.enter_context(nc.semaphore(f"s_{e}")) for e in KB.ENGS}
        dsems = {e: [ses.enter_context(nc.semaphore(f"d_{e}{i}")) for i in range(NDMASEM)] for e in KB.ENGS}
        with nc.allow_low_precision("bf16 matmul operands, fp32 accumulation"):
            kb.emit(sems, dsems)
    es.close()
    return nc


def _consts():
    bf = ml_dtypes.bfloat16
    c = {}
    c["c_ident"] = np.eye(128, dtype=np.float32).astype(bf)
    k = np.arange(128)
    c["c_tri"] = (k[:, None] <= k[None, :]).astype(np.float32).astype(bf)
    log_g = np.log1p(-(2.0 ** (-5.0 - np.arange(8, dtype=np.float64))))
    rel = (k[None, :] - k[:, None]).astype(np.float64)
    dm = np.where(rel[:, None, :] >= 0, np.exp(log_g[None, :, None] * np.maximum(rel, 0)[:, None, :]), 0.0)
    c["c_dmaskT"] = (dm * (128 ** -0.5)).astype(np.float32)
    c["c_toend"] = (np.exp(log_g[None, :] * (127.0 - k)[:, None]) * (128 ** -0.5)).astype(np.float32)
    fs = np.exp(log_g[:, None] * (k + 1.0)[None, :])
    c["c_fscol"] = np.ascontiguousarray(fs.T).astype(np.float32)
    invr = (10000.0 ** (-np.arange(0, 128, 2, dtype=np.float32) / 128)).astype(np.float32)
    invm = (10000.0 ** (-np.arange(0, 64, 2, dtype=np.float32) / 64)).astype(np.float32)
    c["c_invr"] = np.broadcast_to(invr[None], (128, 64)).copy()
    c["c_invm"] = np.broadcast_to(invm[None], (128, 32)).copy()
    c["_decay"] = np.exp(log_g * 128.0)
    return c


def make_in_maps(inputs):
    x = np.asarray(inputs["x"], dtype=np.float32)
    positions = np.asarray(inputs["positions"], dtype=np.int32)
    consts = _consts()
    shared = {}
    for k in ("a_norm_g", "a_w_in", "a_q_a_norm_g", "a_w_q_b", "a_kv_a_norm_g", "a_w_kv_b", "a_q_norm_g",
              "a_k_norm_g", "a_ret_norm_g", "a_w_out", "c_norm_g", "c_w_in", "c_conv_b", "c_ln_g", "c_ln_b",
              "c_w_out"):
        shared[k] = np.ascontiguousarray(np.asarray(inputs[k], dtype=np.float32)[0])
    shared["c_conv_wT"] = np.ascontiguousarray(np.asarray(inputs["c_conv_w"], dtype=np.float32)[0].T)
    for k, v in consts.items():
        if not k.startswith("_"):
            shared[k] = v
    in_maps = []
    for core in range(8):
        b, h = core // 2, core % 2
        xl = np.zeros((T, D), np.float32)
        pl = np.zeros((T,), np.int32)
        kbv = np.zeros((T,), np.float32)
        if h == 0:
            xl[2048:] = x[b, :2048]
            pl[2048:] = positions[b, :2048]
            kbv[:2048] = -30000.0
        else:
            xl[:] = x[b]
            pl[:] = positions[b]
        m = dict(shared)
        m["xT"] = np.ascontiguousarray(xl.T)
        m["pos_tm"] = np.ascontiguousarray(pl.reshape(NB, 128).T)
        m["kbias"] = np.ascontiguousarray(kbv.reshape(NB, 128).T)
        in_maps.append(m)
    return in_maps


def kernel(**inputs):
    nc = build()
    in_maps = make_in_maps(inputs)
    res = run_bass_kernel_spmd(nc, in_maps, core_ids=list(range(8)))
    out = np.zeros((4, 4096, D), np.float32)
    for core in range(8):
        b, h = core // 2, core % 2
        out[b, h * 2048:(h + 1) * 2048] = res.results[core]["outT"].T
    return out
```

```python
import math
from contextlib import ExitStack
import numpy as np
import ml_dtypes
import concourse.bass as bass
import concourse.mybir as mybir
from concourse.bass_utils import run_bass_kernel_spmd

F32 = mybir.dt.float32
BF16 = mybir.dt.bfloat16
I32 = mybir.dt.int32
ALU = mybir.AluOpType
AF = mybir.ActivationFunctionType
AX = mybir.AxisListType

D = 2048
T = 4096
TP = 1920
TO = 2176
NB = 32
EPS = 1e-6
NDMASEM = 8


class Tok:
    __slots__ = ("name", "w", "r", "x")

    def __init__(self, name, x=False):
        self.name = name
        self.w = None
        self.r = []
        self.x = x or name.startswith("ps") or name.startswith("pT") or name.startswith("pS")


class Op:
    __slots__ = ("eng", "fn", "deps", "signal", "semval", "dma", "dmaidx", "idx", "kind")


class KB:
    ENGS = ("pe", "act", "dve", "pool", "sp")

    def __init__(self, nc):
        self.nc = nc
        self.ops = []
        self.ndma = {e: 0 for e in self.ENGS}
        self.dma_hist = {e: [] for e in self.ENGS}

    def op(self, eng, fn, reads=(), writes=(), dma=False, kind=None):
        o = Op()
        o.eng, o.fn, o.dma, o.kind = eng, fn, dma, kind
        o.signal = False
        o.semval = None
        o.idx = len(self.ops)
        deps = []
        for t in reads:
            if t.w is not None:
                deps.append(t.w)
            if t.x and t.r and t.r[-1].eng != eng:
                deps.append(t.r[-1])
        for t in writes:
            if t.w is not None:
                deps.append(t.w)
            deps.extend(t.r)
        if dma:
            o.dmaidx = self.ndma[eng]
            self.ndma[eng] += 1
            h = self.dma_hist[eng]
            if len(h) >= NDMASEM:
                deps.append(h[len(h) - NDMASEM])
            h.append(o)
        dd = []
        seen = set()
        for d in deps:
            if d.idx in seen:
                continue
            seen.add(d.idx)
            if d.eng == "pe" and eng == "pe" and not d.dma and not dma:
                continue
            dd.append(d)
            d.signal = True
        o.deps = dd
        for t in reads:
            t.r.append(o)
        for t in writes:
            t.w = o
            t.r = []
        self.ops.append(o)
        return o

    def barrier(self):
        lasts = []
        for e in self.ENGS:
            comp = [o for o in self.ops if o.eng == e and not o.dma]
            if comp:
                lasts.append(comp[-1])
            lasts.extend(self.dma_hist[e][-NDMASEM:])
        bt = Tok("barrier")
        for e in ("pe", "act", "dve", "pool", "sp"):
            o = self.op(e, lambda eng: eng.nop(), kind="nop")
            for d in lasts:
                if d.idx not in [x.idx for x in o.deps] and not (d.eng == e and not d.dma and e == "pe"):
                    o.deps.append(d)
                    d.signal = True

    def emit(self, sems, dsems):
        self._bsem = sems["sp"]
        nc = self.nc
        cnt = {e: 0 for e in self.ENGS}
        for o in self.ops:
            if o.dma:
                o.semval = (dsems[o.eng][o.dmaidx % NDMASEM], 16 * (o.dmaidx // NDMASEM + 1))
            elif o.signal:
                cnt[o.eng] += 1
                o.semval = (sems[o.eng], cnt[o.eng])
        per = {e: [o for o in self.ops if o.eng == e] for e in self.ENGS}
        final = []
        for e in self.ENGS:
            pass

        def run(engobj, lst, ename):
            known = {}
            for o in lst:
                for d in o.deps:
                    s, v = d.semval
                    k = id(s)
                    if known.get(k, 0) >= v:
                        continue
                    known[k] = v
                    engobj.wait_ge(s, v)
                ins = o.fn(engobj)
                if o.dma:
                    ins.then_inc(o.semval[0], 16)
                elif o.signal:
                    ins.then_inc(o.semval[0], 1)
            if ename == "sp":
                for e2 in self.ENGS:
                    if cnt[e2] > 0:
                        engobj.wait_ge(sems[e2], cnt[e2])
                    n = self.ndma[e2]
                    for i in range(min(n, NDMASEM)):
                        last = ((n - 1 - i) // NDMASEM) * NDMASEM + i
                        engobj.wait_ge(dsems[e2][i], 16 * (last // NDMASEM + 1))

        with nc.Block() as block:
            @block.tensor
            def _(e):
                run(e, per["pe"], "pe")

            @block.scalar
            def _(e):
                run(e, per["act"], "act")

            @block.vector
            def _(e):
                run(e, per["dve"], "dve")

            @block.gpsimd
            def _(e):
                run(e, per["pool"], "pool")

            @block.sync
            def _(e):
                run(e, per["sp"], "sp")


class Ring:
    def __init__(self, items):
        self.items = items
        self.i = 0

    def next(self):
        it = self.items[self.i % len(self.items)]
        self.i += 1
        return it


class _Stop(Exception):
    pass


def build(debug=None):
    debug = debug or {}
    stop_at = debug.get("stop", 10 ** 9)
    open_stacks = []

    def ckpt(n):
        if n >= stop_at:
            raise _Stop()
    nc = bass.Bass("TRN2", target_bir_lowering=False)
    es = ExitStack()
    kb = KB(nc)

    def din(name, shape, dt=F32):
        return nc.dram_tensor(name, list(shape), dt, kind="ExternalInput").ap()

    def dscr(name, shape, dt):
        kind = "ExternalOutput" if name in debug else "Internal"
        return nc.dram_tensor(name, list(shape), dt, kind=kind).ap()

    xT = din("xT", [D, T])
    pos_tm = din("pos_tm", [128, NB], I32)
    kbias_d = din("kbias", [128, NB])
    a_norm_g = din("a_norm_g", [D])
    a_w_in = din("a_w_in", [D, 6208])
    a_q_a_norm_g = din("a_q_a_norm_g", [512])
    a_w_q_b = din("a_w_q_b", [512, 1536])
    a_kv_a_norm_g = din("a_kv_a_norm_g", [512])
    a_w_kv_b = din("a_w_kv_b", [512, 2048])
    a_q_norm_g = din("a_q_norm_g", [192])
    a_k_norm_g = din("a_k_norm_g", [192])
    a_ret_norm_g = din("a_ret_norm_g", [1024])
    a_w_out = din("a_w_out", [D, D])
    c_norm_g = din("c_norm_g", [D])
    c_w_in = din("c_w_in", [D, 6144])
    c_conv_wT = din("c_conv_wT", [D, 31])
    c_conv_b = din("c_conv_b", [D])
    c_ln_g = din("c_ln_g", [D])
    c_ln_b = din("c_ln_b", [D])
    c_w_out = din("c_w_out", [D, D])
    c_ident = din("c_ident", [128, 128], BF16)
    c_tri = din("c_tri", [128, 128], BF16)
    c_dmaskT = din("c_dmaskT", [128, 8, 128])
    c_toend = din("c_toend", [128, 8])
    c_fscol = din("c_fscol", [128, 8])
    c_invr = din("c_invr", [128, 64])
    c_invm = din("c_invm", [128, 32])
    outT = nc.dram_tensor("outT", [D, 2048], F32, kind="ExternalOutput").ap()

    gate_s = dscr("gate_s", [D, TO], BF16)
    mix_s = dscr("mix_s", [D, TO], BF16)
    x1_s = dscr("x1_s", [D, TO], F32)
    v_s = dscr("v_s", [D, 2048], F32)
    gate1_s = dscr("gate1_s", [D, 2048], BF16)
    dbg_s = {k: dscr(k, v[0], v[1]) for k, v in debug.items() if k.startswith("dbg_")}

    def sb(name, shape, dt):
        return es.enter_context(nc.sbuf_tensor(name, list(shape), dt))

    def pst(name, shape, dt=F32):
        return es.enter_context(nc.psum_tensor(name, list(shape), dt))

    psum = [pst(f"ps{i}", [128, 512]) for i in range(8)]
    ptok = [Tok(f"ps{i}") for i in range(8)]
    PS = Ring(list(zip(psum, ptok)))

    def bufring(name, n, shape, dt):
        return Ring([(sb(f"{name}{i}", shape, dt), Tok(f"{name}{i}")) for i in range(n)])

    hT_tok = [Tok(f"hT{g}") for g in range(8)]
    latT_tok = [Tok(f"lat{g}") for g in range(8)]
    kropeT_tok = [Tok(f"krT{b}") for b in range(NB)]
    ssrope = sb("ssrope", [128, NB], F32)
    ssrope_tok = [Tok(f"ssr{b}") for b in range(NB)]
    cqn_tok = [Tok(f"cqn{g}") for g in range(5)]

    ident = sb("ident", [128, 128], BF16)
    tri = sb("tri", [128, 128], BF16)
    ones_bf = sb("ones_bf", [128, 128], BF16)
    dmaskT = sb("dmaskT", [128, 8, 128], F32)
    toend = sb("toend", [128, 8], F32)
    fscol = sb("fscol", [128, 8], F32)
    invr = sb("invr", [128, 64], F32)
    invm = sb("invm", [128, 32], F32)
    posi = sb("posi", [128, NB], I32)
    posf = sb("posf", [128, NB], F32)
    kbias = sb("kbias_sb", [128, NB], F32)
    CONST = Tok("const")

    def dma(eng, out, in_, reads=(), writes=(), **kw):
        return kb.op(eng, lambda e: e.dma_start(out=out, in_=in_, **kw), reads, writes, dma=True)

    for (dst, src) in ((ident, c_ident), (tri, c_tri), (dmaskT, c_dmaskT), (toend, c_toend),
                       (fscol, c_fscol), (invr, c_invr), (invm, c_invm), (posi, pos_tm),
                       (kbias, kbias_d)):
        dma("sp", dst[:], src, writes=[CONST])
    kb.op("dve", lambda e: e.memset(ones_bf[:], 1.0), writes=[CONST])
    kb.op("dve", lambda e: e.tensor_copy(out=posf[:], in_=posi[:]), reads=[CONST], writes=[CONST])

    def gvec(name, src, n):
        t = sb(name, [128, n], F32)
        dma("sp", t[:], src.rearrange("(c p) -> p c", p=128), writes=[CONST],
            allow_slow_non_contiguous=True)
        return t

    g_a = gvec("g_a", a_norm_g, 16)
    g_c = gvec("g_c", c_norm_g, 16)
    g_qa = gvec("g_qa", a_q_a_norm_g, 4)
    g_kva = gvec("g_kva", a_kv_a_norm_g, 4)
    g_ret = gvec("g_ret", a_ret_norm_g, 8)
    cb = gvec("cb", c_conv_b, 16)
    lng = gvec("lng", c_ln_g, 16)
    lnb = gvec("lnb", c_ln_b, 16)

    xs_ring = bufring("xs", 3, [128, 512], F32)
    XS_DEFAULT = [xs_ring]
    sq_ring = bufring("sq", 3, [128, 512], BF16)
    rstd_ring = bufring("rstd", 2, [128, 512], F32)

    eps_t = sb("eps_t", [128, 1], F32)
    kb.op("dve", lambda e: e.memset(eps_t[:], EPS), writes=[CONST])

    def rsqrt_ms(out_ap, in_ap, n, in_tok, out_tok):
        kb.op("act", lambda e: e.activation(out=out_ap, in_=in_ap, func=AF.Ln, bias=eps_t[:out_ap.shape[0], :], scale=1.0 / n),
              reads=[in_tok, CONST], writes=[out_tok])
        kb.op("act", lambda e: e.activation(out=out_ap, in_=out_ap, func=AF.Exp, scale=-0.5), reads=[out_tok], writes=[out_tok])

    def norm_pass(src, src_col0, groups, gvec_t, out_t, out_toks, ncheck=D, xs_ring=None):
        xs_ring = xs_ring or XS_DEFAULT[0]
        for gi, (c0, n) in enumerate(groups):
            pss, pst_ = PS.next()
            xts = []
            for c in range(16):
                xt, xtok = xs_ring.next() if False else (None, None)
            for c in range(16):
                xt, xtok = xs_ring.next()
                dma("sp", xt[:, :n], src[c * 128:(c + 1) * 128, src_col0 + c0: src_col0 + c0 + n],
                    writes=[xtok])
                sq, sqtok = sq_ring.next()
                kb.op("act", lambda e, sq=sq, xt=xt, n=n: e.activation(out=sq[:, :n], in_=xt[:, :n], func=AF.Square),
                      reads=[xtok], writes=[sqtok])
                kb.op("pe", lambda e, pss=pss, sq=sq, n=n, c=c: e.matmul(pss[:, :n], lhsT=ones_bf[:], rhs=sq[:, :n],
                                                                      start=(c == 0), stop=(c == 15)),
                      reads=[sqtok, CONST], writes=[pst_])
            rs, rstok = rstd_ring.next()
            rsqrt_ms(rs[:, :n], pss[:, :n], ncheck, pst_, rstok)
            for c in range(16):
                xt, xtok = xs_ring.next()
                dma("sp", xt[:, :n], src[c * 128:(c + 1) * 128, src_col0 + c0: src_col0 + c0 + n],
                    writes=[xtok])
                eng = "dve"
                kb.op(eng, lambda e, xt=xt, rs=rs, c=c, c0=c0, n=n: e.scalar_tensor_tensor(
                    out=out_t[:, c, c0:c0 + n], in0=xt[:, :n], scalar=gvec_t[:, c:c + 1], in1=rs[:, :n],
                    op0=ALU.mult, op1=ALU.mult),
                    reads=[xtok, rstok, CONST], writes=[out_toks[gi]])


    def load_w(w_dram, col0, ncols, kch=16):
        wt, wtok = w_ring.next()
        dma("pool", wt[:, :kch, :ncols],
            w_dram[:, col0:col0 + ncols].rearrange("(k p) n -> p k n", p=128),
            writes=[wtok])
        return wt, wtok


    def own_groups():
        return [(0, 128), (128, 512), (640, 512), (1152, 512), (1664, 512)]

    def pre_groups():
        return [(0, 512), (512, 512), (1024, 512), (1536, 384)]

    try:
        def mm(ps_ap, lhsT, rhs, start, stop, reads, writes):
            return kb.op("pe", lambda e: e.matmul(ps_ap, lhsT=lhsT, rhs=rhs, start=start, stop=stop), reads, writes)

        def tr(ps_ap, in_ap, reads, writes):
            k = in_ap.shape[0]
            return kb.op("pe", lambda e: e.transpose(out=ps_ap, in_=in_ap, identity=ident[:k, :k]), list(reads) + [CONST], writes)

        def act(out, in_, func, reads, writes, **kw):
            return kb.op("act", lambda e: e.activation(out=out, in_=in_, func=func, **kw), reads, writes)

        def tt(eng, out, in0, in1, op, reads, writes):
            return kb.op(eng, lambda e: e.tensor_tensor(out=out, in0=in0, in1=in1, op=op), reads, writes)

        def ts(eng, out, in0, s1, s2, op0, op1, reads, writes):
            if s2 is None:
                return kb.op(eng, lambda e: e.tensor_scalar(out=out, in0=in0, scalar1=s1, scalar2=None, op0=op0), reads, writes)
            return kb.op(eng, lambda e: e.tensor_scalar(out=out, in0=in0, scalar1=s1, scalar2=s2, op0=op0, op1=op1), reads, writes)

        def stt(out, in0, scalar, in1, op0, op1, reads, writes):
            return kb.op("dve", lambda e: e.scalar_tensor_tensor(out=out, in0=in0, scalar=scalar, in1=in1, op0=op0, op1=op1),
                         reads, writes)

        def cp(eng, out, in_, reads, writes):
            if eng == "act":
                return act(out, in_, AF.Copy, reads, writes)
            return kb.op(eng, lambda e: e.tensor_copy(out=out, in_=in_), reads, writes)

        def rsqrt_ms_dve(out_ap, in_ap, n, in_tok, out_tok):
            ts("dve", out_ap, in_ap, 1.0 / n, EPS, ALU.mult, ALU.add, [in_tok], [out_tok])
            act(out_ap, out_ap, AF.Ln, [out_tok], [out_tok])
            act(out_ap, out_ap, AF.Exp, [out_tok], [out_tok], scale=-0.5)

        PS.items = PS.items[:4]
        pT4 = psum[4][:].bitcast(BF16)[:, 0:128]
        pT6 = psum[6][:].bitcast(BF16)[:, 0:128]
        PT = Ring([(pT4, ptok[4]), (pT6, ptok[6])])
        PSS = Ring([(psum[5][:, 0:128], ptok[5]), (psum[7][:, 0:128], ptok[7])])
        ACC = [(psum[6], ptok[6]), (psum[7], ptok[7])]

        krope_s = dscr("krope_s", [64, T], BF16)
        lat_s = dscr("lat_s", [512, T], BF16)
        cqn_s = dscr("cqn_s", [512, TO], BF16)

        def bc_row(name, src_ap, n):
            t = sb(name, [128, n], F32)
            dma("sp", t[:], src_ap.partition_broadcast(128), writes=[CONST])
            return t

        gk_rope_bc = bc_row("gk_rope_bc", a_k_norm_g[128:192], 64)
        gq_bc = bc_row("gq_bc", a_q_norm_g, 192)
        gk_nope = sb("gk_nope", [128, 1], F32)
        dma("sp", gk_nope[:], a_k_norm_g[0:128].rearrange("(p o) -> p o", o=1), writes=[CONST])
        kb.op("dve", lambda e: e.memset(ssrope[:], 0.0), [], ssrope_tok)
        decay = [float(x) for x in _consts()["_decay"]]

        st_ring = bufring("stg", 3, [128, 512], BF16)
        rp_ring = bufring("rp", 4, [128, 256], F32)
        craw_ring = bufring("craw", 4, [128, 512], F32)


        def rope_tm(out_bf, x_sb, nh, half, cos_ap, sin_ap, reads, wtok):
            x4 = x_sb.rearrange("p h (t d) -> p h t d", t=2)
            xsw = x4[:, :, ::-1, :] if False else None
            cb = cos_ap.unsqueeze(1).unsqueeze(1).to_broadcast([128, nh, 2, half])
            sbb = sin_ap.unsqueeze(1).to_broadcast([128, nh, half])
            (ta, tatok), (tb, tbtok) = rp_ring.next(), rp_ring.next()
            n2 = nh * 2 * half
            ta4 = ta[:, :n2].rearrange("p (h t d) -> p h t d", h=nh, t=2)
            tb4 = tb[:, :n2].rearrange("p (h t d) -> p h t d", h=nh, t=2)
            rd = list(reads) + [ROPE]
            tt("dve", ta4, x4, cb, ALU.mult, rd, [tatok])
            tt("dve", tb4[:, :, 0, :], x4[:, :, 1, :], sbb, ALU.mult, rd, [tbtok])
            tt("dve", tb4[:, :, 1, :], x4[:, :, 0, :], sbb, ALU.mult, rd + [tbtok], [tbtok])
            o4 = out_bf.rearrange("p h (t d) -> p h t d", t=2)
            tt("dve", o4[:, :, 0, :], ta4[:, :, 0, :], tb4[:, :, 0, :], ALU.subtract, [tatok, tbtok], [wtok])
            tt("dve", o4[:, :, 1, :], ta4[:, :, 1, :], tb4[:, :, 1, :], ALU.add, [tatok, tbtok, wtok], [wtok])

        def fm_sweep(w_dram, col0, ncols, groups, evac, act_t, act_toks, act_col0=0, kch=16):
            for t0 in range(0, ncols, 256):
                wt, wtok = load_w(w_dram, col0 + t0, 256, kch)
                for gi, (c0, n) in enumerate(groups):
                    for j in range(2):
                        pss, pst_ = PS.next()
                        for k in range(kch):
                            mm(pss[:, :n], wt[:, k, j * 128:(j + 1) * 128], act_t[:, k, act_col0 + c0:act_col0 + c0 + n],
                               k == 0, k == kch - 1, [wtok, act_toks[gi]], [pst_])
                        evac((t0 // 128) + j, gi, c0, n, pss, pst_)

        def lat_sweep(col0, gv, groups, toks, out_dram, out_col0):
            w0, w0tok = load_w(a_w_in, col0, 256)
            w1, w1tok = load_w(a_w_in, col0 + 256, 256)
            lm = debug.get("lat_mode", 0)
            for gi, (c0, n) in enumerate(groups):
                ssp, ssptok = ACC[gi % 2]
                raws = []
                for cc in range(4):
                    wt, wtok = (w0, w0tok) if cc < 2 else (w1, w1tok)
                    pss, pst_ = PS.next()
                    for k in range(16):
                        mm(pss[:, :n], wt[:, k, (cc % 2) * 128:(cc % 2 + 1) * 128], hT[:, k, c0:c0 + n], k == 0, k == 15,
                           [wtok, toks[gi]], [pst_])
                    raw, rawtok = craw_ring.next()
                    cp(debug.get("cpeng", "dve"), raw[:, :n], pss[:, :n], [pst_], [rawtok])
                    sq, sqtok = sq_ring.next()
                    if debug.get("sqsrc", "psum") == "psum":
                        act(sq[:, :n], pss[:, :n], AF.Square, [pst_], [sqtok])
                    else:
                        act(sq[:, :n], raw[:, :n], AF.Square, [rawtok], [sqtok])
                    mm(ssp[:, :n], ones_bf[:], sq[:, :n], cc == 0, cc == 3, [sqtok, CONST], [ssptok])
                    raws.append((raw, rawtok))
                rs, rstok = rstd_ring.next()
                rsqrt_ms(rs[:, :n], ssp[:, :n], 512, ssptok, rstok)
                for cc in range(4):
                    raw, rawtok = raws[cc]
                    st, sttok = st_ring.next()
                    stt(st[:, :n], raw[:, :n], gv[:, cc:cc + 1], rs[:, :n], ALU.mult, ALU.mult, [rawtok, rstok, CONST], [sttok])
                    dma("sp", out_dram[cc * 128:(cc + 1) * 128, out_col0 + c0:out_col0 + c0 + n], st[:, :n], reads=[sttok])
                yield

        junk = sb("junk", [128, 192], F32)
        JUNK = Tok("junk")

        def krope_sweep(blocks, tok_of_block, list_b0):
            wt, wtok = krw, krwtok
            dma("pool", wt[:, :, :64], a_w_in[:, 4096:4160].rearrange("(k p) n -> p k n", p=128), writes=[wtok])
            for lb in blocks:
                gb = list_b0 + lb
                pss, pst_ = PSS.next()
                for k in range(16):
                    mm(pss[:, :64], hT[:, k, lb * 128:(lb + 1) * 128], wt[:, k, :64], k == 0, k == 15,
                       [wtok, tok_of_block(lb)], [pst_])
                kraw, krawtok = kr_ring.next()
                cp("dve", kraw[:], pss[:, :64], [pst_], [krawtok])
                yield
                act(junk[:, :64], kraw[:], AF.Square, [krawtok, JUNK, ssrope_tok[gb]], [JUNK, ssrope_tok[gb]], accum_out=ssrope[:, gb:gb + 1])
                kr, krtok = kr_ring.next()
                tt("pool", kr[:], kraw[:], gk_rope_bc[:], ALU.mult, [krawtok, CONST], [krtok])
                krb, krbtok = krb_ring.next()
                rope_tm(krb[:].rearrange("p (h d) -> p h d", h=1), kr[:].rearrange("p (h d) -> p h d", h=1), 1, 32,
                        cosm[:, gb, :], sinm[:, gb, :], [krtok], krbtok)
                pt, pttok = PT.next()
                tr(pt[:64, :], krb[:], [krbtok], [pttok])
                krt, krttok = krt_ring.next()
                cp("act", krt[:], pt[:64, :], [pttok], [krttok])
                dma("act", krope_s[:, gb * 128:(gb + 1) * 128], krt[:], reads=[krttok])
                yield

        state = sb("state", [128, 8, 128], F32)
        state_bf = sb("state_bf", [128, 8, 128], BF16)
        ST_tok = [Tok(f"state{h}") for h in range(8)]
        STB_tok = [Tok(f"stateb{h}") for h in range(8)]
        kb.op("dve", lambda e: e.memset(state[:], 0.0), writes=ST_tok)
        kb.op("pool", lambda e: e.memset(state_bf[:], 0.0), writes=STB_tok)
        PTW = Ring([(psum[4][:].bitcast(BF16), ptok[4]), (psum[6][:].bitcast(BF16), ptok[6])])
        PSW = Ring([(psum[5], ptok[5]), (psum[7], ptok[7])])

        def v3(ap):
            return ap.rearrange("p (h d) -> p h d", h=2)

        def run_threads(gens):
            gens = list(gens)
            while gens:
                for g_ in list(gens):
                    try:
                        next(g_)
                    except StopIteration:
                        gens.remove(g_)

        def ret_thread(hp, RS, blocks, tok_of_block, list_b0, own):
            h0 = hp * 2
            wk, wktok = load_w(a_w_in, 1024 + hp * 256, 256)
            wv, wvtok = load_w(a_w_in, 2048 + hp * 256, 256)
            if own:
                wq, wqtok = load_w(a_w_in, hp * 256, 256)

            def stage_a(lb):
                gb = list_b0 + lb
                ht = tok_of_block(lb)
                cols = slice(lb * 128, (lb + 1) * 128)
                A = {"cols": cols}

                def proj(w, wtok_):
                    pss, pst_ = PS.next()
                    for k in range(16):
                        mm(pss[:, :256], hT[:, k, cols], w[:, k, :], k == 0, k == 15, [wtok_, ht], [pst_])
                    return pss, pst_
                if own:
                    pss, pst_ = proj(wq, wqtok)
                    A["qtm"], A["qtmtok"] = RS["qtm"].next()
                    rope_tm(v3(A["qtm"][:]), v3(pss[:, :256]), 2, 64, cosr[:, gb, :], sinr[:, gb, :], [pst_], A["qtmtok"])
                    A["gt"], A["gttok"] = RS["gt"].next()
                    dma("sp", A["gt"][:], gate_s[h0 * 128:(h0 + 2) * 128, cols].rearrange("(j p) c -> p j c", p=128),
                        writes=[A["gttok"]])
                    yield
                pss, pst_ = proj(wk, wktok)
                A["ktm"], A["ktmtok"] = RS["ktm"].next()
                rope_tm(v3(A["ktm"][:]), v3(pss[:, :256]), 2, 64, cosr[:, gb, :], sinr[:, gb, :], [pst_], A["ktmtok"])
                yield
                pss, pst_ = proj(wv, wvtok)
                A["vtm"], A["vtmtok"] = RS["vtm"].next()
                cp("act", A["vtm"][:], pss[:, :256], [pst_], [A["vtmtok"]])
                yield
                return A

            def stage_b(A):
                cols = A["cols"]
                ktm, ktmtok, vtm, vtmtok = A["ktm"], A["ktmtok"], A["vtm"], A["vtmtok"]
                sts = [ST_tok[h0], ST_tok[h0 + 1]]
                stbs = [STB_tok[h0], STB_tok[h0 + 1]]
                ks, kstok = RS["b256"].next()
                tt("dve", v3(ks[:]), v3(ktm[:]), toend[:, h0:h0 + 2].unsqueeze(2).to_broadcast([128, 2, 128]), ALU.mult,
                   [ktmtok, CONST], [kstok])
                if own:
                    qtm, qtmtok = A["qtm"], A["qtmtok"]
                    qs, qstok = RS["b256"].next()
                    tt("dve", v3(qs[:]), v3(qtm[:]), fscol[:, h0:h0 + 2].unsqueeze(2).to_broadcast([128, 2, 128]), ALU.mult,
                       [qtmtok, CONST], [qstok])
                    pT, pTtok = PTW.next()
                    for j in range(2):
                        tr(pT[:, j * 128:(j + 1) * 128], qtm[:, j * 128:(j + 1) * 128], [qtmtok], [pTtok])
                        tr(pT[:, 256 + j * 128:256 + (j + 1) * 128], qs[:, j * 128:(j + 1) * 128], [qstok], [pTtok])
                        tr(pT[:, 512 + j * 128:512 + (j + 1) * 128], ktm[:, j * 128:(j + 1) * 128], [ktmtok], [pTtok])
                    tT, tTtok = RS["tT"].next()
                    cp("act", tT[:], pT[:, :768], [pTtok], [tTtok])
                    yield
                    sc, sctok = PSW.next()
                    for j in range(2):
                        mm(sc[:, j * 128:(j + 1) * 128], tT[:, 512 + j * 128:512 + (j + 1) * 128], tT[:, j * 128:(j + 1) * 128],
                           True, True, [tTtok], [sctok])
                    AT, ATtok = RS["b256"].next()
                    tt("dve", v3(AT[:]), v3(sc[:, :256]), dmaskT[:, h0:h0 + 2, :], ALU.mult, [sctok, CONST], [ATtok])
                    yield
                    rp, rptok = PSW.next()
                    for j in range(2):
                        js = slice(j * 128, (j + 1) * 128)
                        mm(rp[:, js], vtm[:, js], AT[:, js], True, False, [vtmtok, ATtok], [rptok])
                        mm(rp[:, js], state_bf[:, h0 + j, :], tT[:, 256 + j * 128:256 + (j + 1) * 128], False, True,
                           [stbs[j], tTtok], [rptok])
                    oraw, orawtok = RS["f256"].next()
                    cp("dve", oraw[:], rp[:, :256], [rptok], [orawtok])
                    sq, sqtok = RS["b256"].next()
                    act(sq[:], oraw[:], AF.Square, [orawtok], [sqtok])
                    yield
                kv, kvtok = PSW.next()
                for j in range(2):
                    js = slice(j * 128, (j + 1) * 128)
                    mm(kv[:, js], ks[:, js], vtm[:, js], True, True, [kstok, vtmtok], [kvtok])
                for j in range(2):
                    js = slice(j * 128, (j + 1) * 128)
                    stt(state[:, h0 + j, :], state[:, h0 + j, :], decay[h0 + j], kv[:, js], ALU.mult, ALU.add,
                        [sts[j], kvtok], [sts[j]])
                cp("pool", state_bf[:, h0:h0 + 2, :], state[:, h0:h0 + 2, :], sts, stbs)
                yield
                if own:
                    ssp, ssptok = PSW.next()
                    for j in range(2):
                        js = slice(j * 128, (j + 1) * 128)
                        mm(ssp[:, js], ones_bf[:], sq[:, js], True, True, [sqtok, CONST], [ssptok])
                    rs, rstok = RS["f256"].next()
                    rsqrt_ms_dve(rs[:], ssp[:, :256], 128, ssptok, rstok)
                    tt("dve", rs[:], oraw[:], rs[:], ALU.mult, [orawtok, rstok], [rstok])
                    tt("pool", v3(rs[:]), v3(rs[:]), g_ret[:, h0:h0 + 2].unsqueeze(2).to_broadcast([128, 2, 128]), ALU.mult,
                       [rstok, CONST], [rstok])
                    st, sttok = st_ring.next()
                    tt("pool", st[:, :256], rs[:], A["gt"][:].rearrange("p j c -> p (j c)"), ALU.mult, [rstok, A["gttok"]], [sttok])
                    dma("pool", mix_s[h0 * 128:(h0 + 2) * 128, cols].rearrange("(j p) c -> p j c", p=128), v3(st[:, :256]),
                        reads=[sttok])
                    yield

            pendA = None
            for lb in blocks:
                ga = stage_a(lb)
                gb_ = stage_b(pendA) if pendA is not None else iter(())
                A = None
                a_done = b_done = False
                while not (a_done and b_done):
                    if not a_done:
                        try:
                            next(ga)
                        except StopIteration as e_:
                            A = e_.value
                            a_done = True
                    if not b_done:
                        try:
                            next(gb_)
                        except StopIteration:
                            b_done = True
                    yield
                pendA = A
            for _ in stage_b(pendA):
                yield

        def ret_sweep(blocks, tok_of_block, list_b0, own, extra=()):
            for hp0 in (0, 2):
                run_threads([ret_thread(hp0 + i, RSETS[i], blocks, tok_of_block, list_b0, own) for i in range(2)]
                            + (list(extra) if hp0 == 0 else []))

        def gate_evac(dst, dst_col0):
            def f(cc, gi, c0, n, pss, pst_):
                st, sttok = st_ring.next()
                act(st[:, :n], pss[:, :n], AF.Silu, [pst_], [sttok])
                dma("act", dst[cc * 128:(cc + 1) * 128, dst_col0 + c0:dst_col0 + c0 + n], st[:, :n], reads=[sttok])
            return f

        es_rope = ExitStack()
        es_rope_r = ExitStack()
        open_stacks.extend([es_rope, es_rope_r])

        def sbx(stack, name, shape, dt):
            return stack.enter_context(nc.sbuf_tensor(name, list(shape), dt))

        cosm = sbx(es_rope, "cosm", [128, NB, 32], F32)
        sinm = sbx(es_rope, "sinm", [128, NB, 32], F32)
        w_ring = Ring([(sbx(es_rope_r, f"wt{i}", [128, 16, 256], BF16), Tok(f"wt{i}")) for i in range(6)])
        hT = sbx(es_rope_r, "hT", [128, 16, TO], BF16)
        cosr = sbx(es_rope_r, "cosr", [128, NB, 64], F32)
        sinr = sbx(es_rope_r, "sinr", [128, NB, 64], F32)

        def bufring_s(stack, name, n, shape, dt):
            return Ring([(sbx(stack, f"{name}{i}", shape, dt), Tok(f"{name}{i}")) for i in range(n)])
        RSETS = []
        for ti in range(2):
            RSETS.append(dict(
                ktm=bufring_s(es_rope_r, f"ktm{ti}_", 2, [128, 256], BF16),
                vtm=bufring_s(es_rope_r, f"vtm{ti}_", 2, [128, 256], BF16),
                qtm=bufring_s(es_rope_r, f"qtm{ti}_", 2, [128, 256], BF16),
                b256=bufring_s(es_rope_r, f"b256{ti}_", 4, [128, 256], BF16),
                f256=bufring_s(es_rope_r, f"f256{ti}_", 2, [128, 256], F32),
                tT=bufring_s(es_rope_r, f"tT{ti}_", 1, [128, 768], BF16),
                gt=bufring_s(es_rope_r, f"gt{ti}_", 2, [128, 2, 128], BF16)))
        kr_ring = bufring_s(es_rope_r, "kr", 4, [128, 64], F32)
        krb_ring = bufring_s(es_rope_r, "krb", 2, [128, 64], BF16)
        krt_ring = bufring_s(es_rope_r, "krt", 2, [64, 128], BF16)
        krw = sbx(es_rope_r, "krw", [128, 16, 64], BF16)
        krwtok = Tok("krw")
        ROPE = Tok("rope")
        with ExitStack() as tmp:
            r2 = xs_ring.items[0][0][:, 0:512].rearrange("p (b f) -> p b f", b=8)
            rf = xs_ring.items[1][0][:, 0:512].rearrange("p (b f) -> p b f", b=8)
            ri = xs_ring.items[2][0][:, 0:512].bitcast(I32).rearrange("p (b f) -> p b f", b=8)
            RTL = [it[1] for it in xs_ring.items]
        if True:
            def rope_tables():
              for (cs, sn, inv_t, half) in ((cosm, sinm, invm, 32), (cosr, sinr, invr, 64)):
                for q8 in range(4):
                    bs = slice(q8 * 8, q8 * 8 + 8)
                    for (dst, shift) in ((sn, 0.0), (cs, 0.25)):
                        tt("dve", r2[:, :, :half], posf[:, bs].unsqueeze(2).to_broadcast([128, 8, half]),
                           inv_t[:, :half].unsqueeze(1).to_broadcast([128, 8, half]), ALU.mult, [CONST] + RTL, RTL)
                        ts("dve", r2[:, :, :half], r2[:, :, :half], 1.0 / (2 * math.pi), shift, ALU.mult, ALU.add, RTL, RTL)
                        cp("dve", ri[:, :, :half], r2[:, :, :half], RTL, RTL)
                        cp("dve", rf[:, :, :half], ri[:, :, :half], RTL, RTL)
                        tt("dve", r2[:, :, :half], r2[:, :, :half], rf[:, :, :half], ALU.subtract, RTL, RTL)
                        ts("dve", rf[:, :, :half], r2[:, :, :half], 0.5, None, ALU.is_gt, None, RTL, RTL)
                        tt("dve", r2[:, :, :half], r2[:, :, :half], rf[:, :, :half], ALU.subtract, RTL, RTL)
                        ts("dve", rf[:, :, :half], r2[:, :, :half], -0.5, None, ALU.is_lt, None, RTL, RTL)
                        tt("dve", r2[:, :, :half], r2[:, :, :half], rf[:, :, :half], ALU.add, RTL, RTL)
                        act(dst[:, bs, :], r2[:, :, :half], AF.Sin, RTL, [ROPE] + RTL, scale=2 * math.pi)
                    yield

        ckpt(1)
        PG = pre_groups()
        OG = own_groups()

        def tokP(lb):
            return hT_tok[lb // 4]

        def tokO(lb):
            return hT_tok[0] if lb == 0 else hT_tok[1 + (lb - 1) // 4]

        xs_big = Ring(xs_ring.items + craw_ring.items)
        norm_pass(xT, 0, PG, g_a, hT, hT_tok[:4], xs_ring=xs_big)
        ckpt(2)
        run_threads([lat_sweep(3584, g_kva, PG, hT_tok[:4], lat_s, 0), rope_tables()])
        ckpt(3)
        ckpt(4)
        ret_sweep(range(15), tokP, 0, False, extra=[krope_sweep(range(15), tokP, 0)])
        ckpt(5)
        norm_pass(xT, TP, OG, g_a, hT, hT_tok[:5], xs_ring=xs_big)
        ckpt(6)
        fm_sweep(a_w_in, 4160, 2048, OG, gate_evac(gate_s, 0), hT, hT_tok[:5])
        kb.barrier()
        for _ in lat_sweep(3584, g_kva, OG, hT_tok[:5], lat_s, TP):
            pass
        ckpt(8)
        for _ in lat_sweep(3072, g_qa, OG, hT_tok[:5], cqn_s, 0):
            pass
        ckpt(9)
        ckpt(10)
        ret_sweep(range(17), tokO, 15, True, extra=[krope_sweep(range(17), tokO, 15)])
        ckpt(11)
        kb.barrier()
        es_rope_r.close()

        with ExitStack() as ph3:
            latT = sbx(ph3, "latT", [128, 4, T], BF16)
            kropeT = sbx(ph3, "kropeT", [128, T], BF16)
            wkvb_ring = Ring([(sbx(ph3, f"wkvb{i}", [128, 4, 256], BF16), Tok(f"wkvb{i}")) for i in range(2)])
            wqb_ring = Ring([(sbx(ph3, f"wqb{i}", [128, 4, 192], BF16), Tok(f"wqb{i}")) for i in range(2)])
            cqn_ring = Ring([(sbx(ph3, f"cqnb{i}", [128, 4, 128], BF16), Tok(f"cqnb{i}")) for i in range(3)])
            BS = []
            for i in range(2):
                BS.append(dict(KT=sbx(ph3, f"KT{i}", [128, T], BF16), Vh=sbx(ph3, f"Vh{i}", [128, NB, 128], BF16),
                               qT=sbx(ph3, f"qT{i}", [128, TO], BF16), qrT=sbx(ph3, f"qrT{i}", [128, TO], BF16),
                               kscale=sbx(ph3, f"kscale{i}", [128, NB], F32),
                               KTt=Tok(f"KT{i}"), VHt=Tok(f"Vh{i}"), QTt=Tok(f"qT{i}"), KSC=Tok(f"ksc{i}")))
            qn = sbx(ph3, "qn", [128, 192], F32)
            qbf_ring = Ring([(sbx(ph3, f"qbf{i}", [128, 192], BF16), Tok(f"qbf{i}")) for i in range(2)])
            qss = sbx(ph3, "qss", [128, 1], F32)
            rden_ring = Ring([(sbx(ph3, f"rden{i}", [128, 512], F32), Tok(f"rden{i}")) for i in range(2)])
            attf_ring = Ring([(sbx(ph3, f"attf{i}", [128, 512], F32), Tok(f"attf{i}")) for i in range(2)])
            gta_ring = Ring([(sbx(ph3, f"gta{i}", [128, 512], BF16), Tok(f"gta{i}")) for i in range(2)])
            pt_ring = Ring([(sbx(ph3, f"ptile{i}", [128, 512], BF16), Tok(f"ptile{i}")) for i in range(5)])
            psum_acc = [(sbx(ph3, f"psm{i}", [128, 512], F32), Tok(f"psm{i}")) for i in range(2)]
            ones_f32 = sbx(ph3, "ones_f32", [128, 128], F32)
            kb.op("pool", lambda e: e.memset(ones_f32[:], 1.0), writes=[CONST])
            LAT, KRT, CQN, WKV, WQ = Tok("lat"), Tok("krt"), Tok("cqn"), Tok("wkv"), Tok("wq")
            QN, QSS = Tok("qn"), Tok("qss")
            for c in range(4):
                dma("sp", latT[:, c, :], lat_s[c * 128:(c + 1) * 128, :], writes=[LAT])
            kb.op("pool", lambda e: e.memset(kropeT[64:128, :], 0.0), writes=[KRT])
            dma("sp", kropeT[0:64, :], krope_s, writes=[KRT])
            for i_ in range(2):
                kb.op("pool", lambda e, i_=i_: e.memset(BS[i_]["qrT"][64:128, :], 0.0), writes=[BS[i_]["QTt"]])
            PT = Ring([(pT4, ptok[4])])
            PS_save = PS.items
            PS.items = PS_save[:3]
            PSS = PS
            ACCP = [((psum[3], ptok[3]), (psum[6], ptok[6])), ((psum[5], ptok[5]), (psum[7], ptok[7]))]
            def prep(h, B):
                KT, Vh, qT, qrT, kscale = B["KT"], B["Vh"], B["qT"], B["qrT"], B["kscale"]
                KTt, VHt, QTt, KSC = B["KTt"], B["VHt"], B["QTt"], B["KSC"]
                wkvb, WKV = wkvb_ring.next()
                dma("pool", wkvb[:], a_w_kv_b[:, h * 256:(h + 1) * 256].rearrange("(k p) n -> p k n", p=128), writes=[WKV])
                wqb, WQ = wqb_ring.next()
                dma("pool", wqb[:], a_w_q_b[:, h * 192:(h + 1) * 192].rearrange("(k p) n -> p k n", p=128), writes=[WQ])
                kss, ksstok = psum[4][:, 0:128], ptok[4]
                def kss_mm(g, sq, sqtok):
                    for b4 in range(4):
                        b = g * 4 + b4
                        mm(kss[:, b:b + 1], sq[:, b4 * 128:(b4 + 1) * 128], ones_bf[:, 0:1], True, True, [sqtok, CONST], [ksstok])

                prevk = None
                for g in range(8):
                    gs = slice(g * 512, (g + 1) * 512)
                    pss, pst_ = PS.next()
                    for k in range(4):
                        mm(pss[:], wkvb[:, k, 0:128], latT[:, k, gs], k == 0, k == 3, [WKV, LAT], [pst_])
                    sq, sqtok = sq_ring.next()
                    act(sq[:], pss[:], AF.Square, [pst_], [sqtok])
                    ts("dve", KT[:, gs], pss[:], gk_nope[:, 0:1], None, ALU.mult, None, [pst_, CONST], [KTt])
                    if prevk is not None:
                        kss_mm(*prevk)
                    prevk = (g, sq, sqtok)
                    yield
                kss_mm(*prevk)
                tt("dve", kscale[:], kss[:, :NB], ssrope[:], ALU.add, [ksstok] + ssrope_tok, [KSC])
                rsqrt_ms(kscale[:], kscale[:], 192, KSC, KSC)
                ts("dve", kscale[:], kscale[:], 192.0 ** -0.5, None, ALU.mult, None, [KSC], [KSC])
                for g in range(8):
                    pss, pst_ = PS.next()
                    for b4 in range(4):
                        b = g * 4 + b4
                        for k in range(4):
                            mm(pss[:, b4 * 128:(b4 + 1) * 128], latT[:, k, b * 128:(b + 1) * 128],
                               wkvb[:, k, 128:256], k == 0, k == 3, [WKV, LAT], [pst_])
                    cp("act", Vh[:, g * 4:(g + 1) * 4, :].rearrange("p b d -> p (b d)"), pss[:], [pst_], [VHt])
                    yield
                def q_tr(qbf, QBF, cols):
                    p1, p1tok = PT.next()
                    tr(p1, qbf[:, 0:128], [QBF], [p1tok])
                    cp("act", qT[:, cols], p1, [p1tok], [QTt])
                    p2, p2tok = PT.next()
                    tr(p2[:64, :], qbf[:, 128:192], [QBF], [p2tok])
                    cp("act", qrT[0:64, cols], p2[:64, :], [p2tok], [QTt])

                prevq = None
                for lb in range(17):
                    gb = 15 + lb
                    cols = slice(lb * 128, (lb + 1) * 128)
                    pss, pst_ = PS.next()
                    cqb, CQN = cqn_ring.next()
                    dma("sp", cqb[:], cqn_s[:, cols].rearrange("(k p) n -> p k n", p=128), writes=[CQN])
                    for k in range(4):
                        mm(pss[:, :192], cqb[:, k, :], wqb[:, k, :], k == 0, k == 3, [CQN, WQ], [pst_])
                    kb.op("dve", lambda e: e.memset(qss[:], 0.0), [QSS], [QSS])
                    act(junk[:, :192], pss[:, :192], AF.Square, [pst_, JUNK, QSS], [JUNK, QSS], accum_out=qss[:, 0:1])
                    rsqrt_ms(qss[:], qss[:], 192, QSS, QSS)
                    stt(qn[:], pss[:, :192], qss[:, 0:1], gq_bc[:], ALU.mult, ALU.mult, [pst_, QSS, CONST], [QN])
                    qbf, QBF = qbf_ring.next()
                    cp("pool", qbf[:, 0:128], qn[:, 0:128], [QN], [QBF])
                    rope_tm(qbf[:, 128:192].rearrange("p (h d) -> p h d", h=1), qn[:, 128:192].rearrange("p (h d) -> p h d", h=1),
                            1, 32, cosm[:, gb, :], sinm[:, gb, :], [QN], QBF)
                    yield
                    if prevq is not None:
                        q_tr(*prevq)
                        yield
                    prevq = (qbf, QBF, cols)
                q_tr(*prevq)
                yield

            def attn(h, B):
                KT, Vh, qT, qrT, kscale = B["KT"], B["Vh"], B["qT"], B["qrT"], B["kscale"]
                KTt, VHt, QTt, KSC = B["KTt"], B["VHt"], B["QTt"], B["KSC"]
                work = []
                for gi, (c0, n) in enumerate(OG):
                    qb0 = (TP + c0) // 128
                    nk = qb0 + n // 128
                    for j in range(nk):
                        work.append((gi, c0, n, qb0, nk, j))

                def s_stage(w):
                    gi, c0, n, qb0, nk, j = w
                    lo = max(0, j - qb0) * 128
                    pss, pst_ = PS.next()
                    mm(pss[:, lo:n], KT[:, j * 128:(j + 1) * 128], qT[:, c0 + lo:c0 + n], True, False, [KTt, QTt], [pst_])
                    mm(pss[:, lo:n], kropeT[:, j * 128:(j + 1) * 128], qrT[:, c0 + lo:c0 + n], False, True, [KRT, QTt], [pst_])
                    pt, pttok = pt_ring.next()
                    act(pt[:, lo:n], pss[:, lo:n], AF.Exp, [pst_, KSC, CONST], [pttok], scale=kscale[:, j:j + 1],
                        bias=kbias[:, j:j + 1])
                    if j >= qb0:
                        tt("pool", pt[:, lo:lo + 128], pt[:, lo:lo + 128], tri[:], ALU.mult, [pttok, CONST], [pttok])
                    return pt, pttok, lo

                def pv_stage(w, sres):
                    gi, c0, n, qb0, nk, j = w
                    pt, pttok, lo = sres
                    (ao, aotok), (ad, adtok) = ACCP[(h * 5 + gi) % 2]
                    mm(ao[:, lo:n], Vh[:, j, :], pt[:, lo:n], j == 0, j == nk - 1, [VHt, pttok], [aotok])
                    psm, psmtok = psum_acc[(h * 5 + gi) % 2]
                    if j == 0:
                        cp("dve", psm[:, :n], pt[:, :n], [pttok, psmtok], [psmtok])
                    else:
                        tt("dve", psm[:, lo:n], psm[:, lo:n], pt[:, lo:n], ALU.add, [pttok, psmtok], [psmtok])
                    if j == nk - 1:
                        deferred.append([2, lambda: epilogue(gi, c0, n, ao, aotok, ad, adtok, psm, psmtok)])

                def epilogue(gi, c0, n, ao, aotok, ad, adtok, psm, psmtok):
                    if True:
                        mm(ad[:, :n], ones_f32[:], psm[:, :n], True, True, [CONST, psmtok], [adtok])
                        rden, RDEN = rden_ring.next()
                        attf, ATTF = attf_ring.next()
                        gta, GTA = gta_ring.next()
                        ts("dve", rden[:, :n], ad[:, :n], 1e-30, None, ALU.add, None, [adtok], [RDEN])
                        act(rden[:, :n], rden[:, :n], AF.Ln, [RDEN], [RDEN])
                        act(rden[:, :n], rden[:, :n], AF.Exp, [RDEN], [RDEN], scale=-1.0)
                        tt("dve", attf[:, :n], ao[:, :n], rden[:, :n], ALU.mult, [aotok, RDEN], [ATTF])
                        dma("sp", gta[:, :n], gate_s[(8 + h) * 128:(9 + h) * 128, c0:c0 + n], writes=[GTA])
                        st, sttok = st_ring.next()
                        tt("pool", st[:, :n], attf[:, :n], gta[:, :n], ALU.mult, [ATTF, GTA], [sttok])
                        dma("pool", mix_s[(8 + h) * 128:(9 + h) * 128, c0:c0 + n], st[:, :n], reads=[sttok])

                deferred = []

                def tick():
                    for d_ in list(deferred):
                        d_[0] -= 1
                        if d_[0] <= 0:
                            deferred.remove(d_)
                            d_[1]()

                pend = []
                for w in work:
                    pend.append((w, s_stage(w)))
                    if len(pend) > 2:
                        pv_stage(*pend.pop(0))
                    tick()
                    yield
                while pend:
                    pv_stage(*pend.pop(0))
                    tick()
                    yield
                while deferred:
                    tick()
                    yield

            for _ in prep(0, BS[0]):
                pass
            for h in range(8):
                ga = attn(h, BS[h % 2])
                gp = prep(h + 1, BS[(h + 1) % 2]) if h < 7 else iter(())
                a_alive = p_alive = True
                while a_alive:
                    for _ in range(3):
                        try:
                            next(ga)
                        except StopIteration:
                            a_alive = False
                            break
                    if p_alive:
                        try:
                            next(gp)
                        except StopIteration:
                            p_alive = False
                for _ in gp:
                    pass
            kb.barrier()
            PS.items = PS_save
        es_rope.close()

        ckpt(12)
        def load_act(src, ncols_total, groups, src_col0=0):
            lo = min(c0 for c0, n in groups)
            hi = max(c0 + n for c0, n in groups)
            for c in range(16):
                dma("sp" if c % 2 == 0 else "act", hT[:, c, lo:hi], src[c * 128:(c + 1) * 128, src_col0 + lo:src_col0 + hi],
                    writes=[hT_tok[gi] for gi in range(len(groups))])

        def resid_evac(res_src, res_col0, dst, dst_col0):
            pend = []

            def flush():
                while pend:
                    pend.pop(0)()

            def f(cc, gi, c0, n, pss, pst_):
                xt, xtok = xs_ring.next()
                dma("sp", xt[:, :n], res_src[cc * 128:(cc + 1) * 128, res_col0 + c0:res_col0 + c0 + n], writes=[xtok])
                ev, evtok = ev_ring.next()
                tt("dve", ev[:, :n], pss[:, :n], xt[:, :n], ALU.add, [pst_, xtok], [evtok])
                flush()
                pend.append(lambda: dma("sp", dst[cc * 128:(cc + 1) * 128, dst_col0 + c0:dst_col0 + c0 + n], ev[:, :n],
                                        reads=[evtok]))
            f.flush = flush
            return f

        es_C = ExitStack()
        open_stacks.append(es_C)
        w_ring = Ring([(sbx(es_C, f"wtc{i}", [128, 16, 256], BF16), Tok(f"wtc{i}")) for i in range(4)])
        hT = sbx(es_C, "hTc", [128, 16, TO], BF16)
        xs_ring = Ring([(sbx(es_C, f"xsc{i}", [128, 512], F32), Tok(f"xsc{i}")) for i in range(8)])
        ev_ring = Ring([(sbx(es_C, f"evc{i}", [128, 512], F32), Tok(f"evc{i}")) for i in range(3)])
        load_act(mix_s, TO, OG)
        _rev = resid_evac(xT, TP, x1_s, 0)
        fm_sweep(a_w_out, 0, 2048, OG, _rev, hT, hT_tok[:5])
        _rev.flush()
        kb.barrier()

        ckpt(13)
        norm_pass(x1_s, 0, OG, g_c, hT, hT_tok[:5], xs_ring=xs_ring)
        OG4 = [(128 + i * 512, 512) for i in range(4)]
        with ExitStack() as ph5:
            cw = sbx(ph5, "cw", [128, 16, 31], F32)
            dma("sp", cw[:], c_conv_wT.rearrange("(c p) k -> p c k", p=128), writes=[CONST])
            u_ring = Ring([(sbx(ph5, f"u{i}", [128, TO], BF16), Tok(f"u{i}")) for i in range(3)])
            dg_ring = Ring([(sbx(ph5, f"dg{i}", [128, 31, 128], BF16), Tok(f"dg{i}")) for i in range(2)])
            sg_ring = Ring([(sbx(ph5, f"sg{i}", [128, 512], F32), Tok(f"sg{i}")) for i in range(2)])

            def conv_pe(c, u, utok):
                dg, dgtok = dg_ring.next()
                tt("dve", dg[:], ident[:].unsqueeze(1).to_broadcast([128, 31, 128]),
                   cw[:, c, :].unsqueeze(2).to_broadcast([128, 31, 128]), ALU.mult, [CONST], [dgtok])
                for g in range(4):
                    pc, pctok = PS.next()
                    for kk in range(31):
                        o = 98 + kk + g * 512
                        mm(pc[:], dg[:, kk, :], u[:, o:o + 512], kk == 0, kk == 30, [dgtok, utok], [pctok])
                    ev, evtok = ev_ring.next()
                    act(ev[:], pc[:], AF.Identity, [pctok, CONST], [evtok], bias=cb[:, c:c + 1])
                    dma("act", v_s[c * 128:(c + 1) * 128, g * 512:(g + 1) * 512], ev[:], reads=[evtok])

            prev = None
            for t0 in range(0, 2048, 256):
                wa, watok = load_w(c_w_in, t0, 256)
                wb, wbtok = load_w(c_w_in, 2048 + t0, 256)
                for j in range(2):
                    c = t0 // 128 + j
                    u, utok = u_ring.next()
                    for gi, (c0, n) in enumerate(OG):
                        pa, patok = PS.next()
                        for k in range(16):
                            mm(pa[:, :n], wa[:, k, j * 128:(j + 1) * 128], hT[:, k, c0:c0 + n], k == 0, k == 15, [watok, hT_tok[gi]], [patok])
                        pb, pbtok = PS.next()
                        for k in range(16):
                            mm(pb[:, :n], wb[:, k, j * 128:(j + 1) * 128], hT[:, k, c0:c0 + n], k == 0, k == 15, [wbtok, hT_tok[gi]], [pbtok])
                        sg, sgtok = sg_ring.next()
                        act(sg[:, :n], pb[:, :n], AF.Sigmoid, [pbtok], [sgtok])
                        tt("dve", u[:, c0:c0 + n], pa[:, :n], sg[:, :n], ALU.mult, [patok, sgtok, utok], [utok])
                    if prev is not None:
                        conv_pe(*prev)
                    prev = (c, u, utok)
            conv_pe(*prev)
            fm_sweep(c_w_in, 4096, 2048, OG4, gate_evac(gate1_s, -128), hT, hT_tok[1:5])
            kb.barrier()

        es_C.close()
        with ExitStack() as ph6:
            wout = sbx(ph6, "wout", [128, 16, 2048], BF16)
            ev_ring = Ring([(sbx(ph6, f"evd{i}", [128, 512], F32), Tok(f"evd{i}")) for i in range(4)])
            xs_ring = Ring([(sbx(ph6, f"xsd{i}", [128, 512], F32), Tok(f"xsd{i}")) for i in range(4)])
            WOUT = [Tok(f"wout{i}") for i in range(8)]
            for i in range(8):
                dma("pool", wout[:, :, i * 256:(i + 1) * 256], c_w_out[:, i * 256:(i + 1) * 256].rearrange("(k p) n -> p k n", p=128),
                    writes=[WOUT[i]])
            vg = sbx(ph6, "vg", [128, 16, 512], F32)
            mg_ring = Ring([(sbx(ph6, f"mg{i}", [128, 16, 512], BF16), [Tok(f"mg{i}_{c}") for c in range(16)]) for i in range(2)])
            mean = sbx(ph6, "mean", [128, 512], F32)
            m2 = sbx(ph6, "m2", [128, 512], F32)
            lrs = sbx(ph6, "lrs", [128, 512], F32)
            sl_ring = Ring([(sbx(ph6, f"sl{i}", [128, 512], F32), Tok(f"sl{i}")) for i in range(3)])
            g1_ring = Ring([(sbx(ph6, f"g1{i}", [128, 512], BF16), Tok(f"g1{i}")) for i in range(3)])
            VG = [Tok(f"vg{c}") for c in range(16)]
            MEAN, M2, LRS = Tok("mean"), Tok("m2"), Tok("lrs")

            def ln_group(g):
                gs = slice(g * 512, (g + 1) * 512)
                mg, mgtoks = mg_ring.next()
                pm, pmtok = psum[6], ptok[6]
                pq, pqtok = psum[7], ptok[7]
                for c in range(16):
                    dma("sp", vg[:, c, :], v_s[c * 128:(c + 1) * 128, gs], writes=[VG[c]])
                    vb, vbtok = st_ring.next()
                    cp("dve", vb[:], vg[:, c, :], [VG[c]], [vbtok])
                    sq, sqtok = sq_ring.next()
                    act(sq[:], vg[:, c, :], AF.Square, [VG[c]], [sqtok])
                    mm(pm[:], ones_bf[:], vb[:], c == 0, c == 15, [vbtok, CONST], [pmtok])
                    mm(pq[:], ones_bf[:], sq[:], c == 0, c == 15, [sqtok, CONST], [pqtok])
                    yield
                act(mean[:], pm[:], AF.Copy, [pmtok], [MEAN], scale=1.0 / 2048)
                tt("dve", m2[:], mean[:], mean[:], ALU.mult, [MEAN], [M2])
                stt(lrs[:], pq[:], 1.0 / 2048, m2[:], ALU.mult, ALU.subtract, [pqtok, M2], [LRS])
                rsqrt_ms(lrs[:], lrs[:], 1, LRS, LRS)
                def st1(c):
                    tt("dve", vg[:, c, :], vg[:, c, :], mean[:], ALU.subtract, [VG[c], MEAN], [VG[c]])
                    tt("pool", vg[:, c, :], vg[:, c, :], lrs[:], ALU.mult, [VG[c], LRS], [VG[c]])

                def st2(c):
                    sl, SL = sl_ring.next()
                    act(sl[:], vg[:, c, :], AF.Silu, [VG[c], CONST], [SL], scale=lng[:, c:c + 1], bias=lnb[:, c:c + 1])
                    g1, G1 = g1_ring.next()
                    dma("sp", g1[:], gate1_s[c * 128:(c + 1) * 128, gs], writes=[G1])
                    return sl, SL, g1, G1

                def st3(c, sl, SL, g1, G1):
                    tt("dve", mg[:, c, :], sl[:], g1[:], ALU.mult, [SL, G1], [mgtoks[c]])

                r2s = {}
                for c in range(16 + 2):
                    if c < 16:
                        st1(c)
                    if 1 <= c <= 16:
                        r2s[c - 1] = st2(c - 1)
                    if c >= 2:
                        st3(c - 2, *r2s.pop(c - 2))
                    yield
                LNRES[g] = (g, mg, mgtoks)

            def out_group(g, mg, mgtoks):
                ev_f = resid_evac(x1_s, 128, outT, 0)
                for m in range(16):
                    pss, pst_ = PS.next()
                    for k in range(16):
                        mm(pss[:], wout[:, k, m * 128:(m + 1) * 128], mg[:, k, :], k == 0, k == 15, [WOUT[m // 2], mgtoks[k]], [pst_])
                    ev_f(m, g, g * 512, 512, pss, pst_)
                    yield
                ev_f.flush()

            LNRES = {}
            for _ in ln_group(0):
                pass
            for g in range(4):
                go = out_group(*LNRES[g])
                gl = ln_group(g + 1) if g < 3 else iter(())
                o_alive = l_alive = True
                while o_alive or l_alive:
                    if o_alive:
                        try:
                            next(go)
                        except StopIteration:
                            o_alive = False
                    for _ in range(2):
                        if l_alive:
                            try:
                                next(gl)
                            except StopIteration:
                                l_alive = False
    except _Stop:
        for st_ in reversed(open_stacks):
            st_.close()
    with ExitStack() as ses:
        sems = {e: ses.enter_context(nc.semaphore(f"s_{e}")) for e in KB.ENGS}
        dsems = {e: [ses.enter_context(nc.semaphore(f"d_{e}{i}")) for i in range(NDMASEM)] for e in KB.ENGS}
        with nc.allow_low_precision("bf16 matmul operands, fp32 accumulation"):
            kb.emit(sems, dsems)
    es.close()
    return nc


def _consts():
    bf = ml_dtypes.bfloat16
    c = {}
    c["c_ident"] = np.eye(128, dtype=np.float32).astype(bf)
    k = np.arange(128)
    c["c_tri"] = (k[:, None] <= k[None, :]).astype(np.float32).astype(bf)
    log_g = np.log1p(-(2.0 ** (-5.0 - np.arange(8, dtype=np.float64))))
    rel = (k[None, :] - k[:, None]).astype(np.float64)
    dm = np.where(rel[:, None, :] >= 0, np.exp(log_g[None, :, None] * np.maximum(rel, 0)[:, None, :]), 0.0)
    c["c_dmaskT"] = (dm * (128 ** -0.5)).astype(np.float32)
    c["c_toend"] = (np.exp(log_g[None, :] * (127.0 - k)[:, None]) * (128 ** -0.5)).astype(np.float32)
    fs = np.exp(log_g[:, None] * (k + 1.0)[None, :])
    c["c_fscol"] = np.ascontiguousarray(fs.T).astype(np.float32)
    invr = (10000.0 ** (-np.arange(0, 128, 2, dtype=np.float32) / 128)).astype(np.float32)
    invm = (10000.0 ** (-np.arange(0, 64, 2, dtype=np.float32) / 64)).astype(np.float32)
    c["c_invr"] = np.broadcast_to(invr[None], (128, 64)).copy()
    c["c_invm"] = np.broadcast_to(invm[None], (128, 32)).copy()
    c["_decay"] = np.exp(log_g * 128.0)
    return c


def make_in_maps(inputs):
    x = np.asarray(inputs["x"], dtype=np.float32)
    positions = np.asarray(inputs["positions"], dtype=np.int32)
    consts = _consts()
    shared = {}
    for k in ("a_norm_g", "a_w_in", "a_q_a_norm_g", "a_w_q_b", "a_kv_a_norm_g", "a_w_kv_b", "a_q_norm_g",
              "a_k_norm_g", "a_ret_norm_g", "a_w_out", "c_norm_g", "c_w_in", "c_conv_b", "c_ln_g", "c_ln_b",
              "c_w_out"):
        shared[k] = np.ascontiguousarray(np.asarray(inputs[k], dtype=np.float32)[0])
    shared["c_conv_wT"] = np.ascontiguousarray(np.asarray(inputs["c_conv_w"], dtype=np.float32)[0].T)
    for k, v in consts.items():
        if not k.startswith("_"):
            shared[k] = v
    in_maps = []
    for core in range(8):
        b, h = core // 2, core % 2
        xl = np.zeros((T, D), np.float32)
        pl = np.zeros((T,), np.int32)
        kbv = np.zeros((T,), np.float32)
        if h == 0:
            xl[2048:] = x[b, :2048]
            pl[2048:] = positions[b, :2048]
            kbv[:2048] = -30000.0
        else:
            xl[:] = x[b]
            pl[:] = positions[b]
        m = dict(shared)
        m["xT"] = np.ascontiguousarray(xl.T)
        m["pos_tm"] = np.ascontiguousarray(pl.reshape(NB, 128).T)
        m["kbias"] = np.ascontiguousarray(kbv.reshape(NB, 128).T)
        in_maps.append(m)
    return in_maps


def kernel(**inputs):
    nc = build()
    in_maps = make_in_maps(inputs)
    res = run_bass_kernel_spmd(nc, in_maps, core_ids=list(range(8)))
    out = np.zeros((4, 4096, D), np.float32)
    for core in range(8):
        b, h = core // 2, core % 2
        out[b, h * 2048:(h + 1) * 2048] = res.results[core]["outT"].T
    return out
```

```python
import math
from contextlib import ExitStack
import numpy as np
import ml_dtypes
import concourse.bass as bass
import concourse.mybir as mybir
from concourse.bass_utils import run_bass_kernel_spmd

F32 = mybir.dt.float32
BF16 = mybir.dt.bfloat16
I32 = mybir.dt.int32
ALU = mybir.AluOpType
AF = mybir.ActivationFunctionType
AX = mybir.AxisListType

D = 2048
T = 4096
TP = 1920
TO = 2176
NB = 32
EPS = 1e-6
NDMASEM = 8


class Tok:
    __slots__ = ("name", "w", "r", "x")

    def __init__(self, name, x=False):
        self.name = name
        self.w = None
        self.r = []
        self.x = x or name.startswith("ps") or name.startswith("pT") or name.startswith("pS")


class Op:
    __slots__ = ("eng", "fn", "deps", "signal", "semval", "dma", "dmaidx", "idx", "kind")


class KB:
    ENGS = ("pe", "act", "dve", "pool", "sp")

    def __init__(self, nc):
        self.nc = nc
        self.ops = []
        self.ndma = {e: 0 for e in self.ENGS}
        self.dma_hist = {e: [] for e in self.ENGS}

    def op(self, eng, fn, reads=(), writes=(), dma=False, kind=None):
        o = Op()
        o.eng, o.fn, o.dma, o.kind = eng, fn, dma, kind
        o.signal = False
        o.semval = None
        o.idx = len(self.ops)
        deps = []
        for t in reads:
            if t.w is not None:
                deps.append(t.w)
            if t.x and t.r and t.r[-1].eng != eng:
                deps.append(t.r[-1])
        for t in writes:
            if t.w is not None:
                deps.append(t.w)
            deps.extend(t.r)
        if dma:
            o.dmaidx = self.ndma[eng]
            self.ndma[eng] += 1
            h = self.dma_hist[eng]
            if len(h) >= NDMASEM:
                deps.append(h[len(h) - NDMASEM])
            h.append(o)
        dd = []
        seen = set()
        for d in deps:
            if d.idx in seen:
                continue
            seen.add(d.idx)
            if d.eng == "pe" and eng == "pe" and not d.dma and not dma:
                continue
            dd.append(d)
            d.signal = True
        o.deps = dd
        for t in reads:
            t.r.append(o)
        for t in writes:
            t.w = o
            t.r = []
        self.ops.append(o)
        return o

    def barrier(self):
        lasts = []
        for e in self.ENGS:
            comp = [o for o in self.ops if o.eng == e and not o.dma]
            if comp:
                lasts.append(comp[-1])
            lasts.extend(self.dma_hist[e][-NDMASEM:])
        bt = Tok("barrier")
        for e in ("pe", "act", "dve", "pool", "sp"):
            o = self.op(e, lambda eng: eng.nop(), kind="nop")
            for d in lasts:
                if d.idx not in [x.idx for x in o.deps] and not (d.eng == e and not d.dma and e == "pe"):
                    o.deps.append(d)
                    d.signal = True

    def emit(self, sems, dsems):
        self._bsem = sems["sp"]
        nc = self.nc
        cnt = {e: 0 for e in self.ENGS}
        for o in self.ops:
            if o.dma:
                o.semval = (dsems[o.eng][o.dmaidx % NDMASEM], 16 * (o.dmaidx // NDMASEM + 1))
            elif o.signal:
                cnt[o.eng] += 1
                o.semval = (sems[o.eng], cnt[o.eng])
        per = {e: [o for o in self.ops if o.eng == e] for e in self.ENGS}
        final = []
        for e in self.ENGS:
            pass

        def run(engobj, lst, ename):
            known = {}
            for o in lst:
                for d in o.deps:
                    s, v = d.semval
                    k = id(s)
                    if known.get(k, 0) >= v:
                        continue
                    known[k] = v
                    engobj.wait_ge(s, v)
                ins = o.fn(engobj)
                if o.dma:
                    ins.then_inc(o.semval[0], 16)
                elif o.signal:
                    ins.then_inc(o.semval[0], 1)
            if ename == "sp":
                for e2 in self.ENGS:
                    if cnt[e2] > 0:
                        engobj.wait_ge(sems[e2], cnt[e2])
                    n = self.ndma[e2]
                    for i in range(min(n, NDMASEM)):
                        last = ((n - 1 - i) // NDMASEM) * NDMASEM + i
                        engobj.wait_ge(dsems[e2][i], 16 * (last // NDMASEM + 1))

        with nc.Block() as block:
            @block.tensor
            def _(e):
                run(e, per["pe"], "pe")

            @block.scalar
            def _(e):
                run(e, per["act"], "act")

            @block.vector
            def _(e):
                run(e, per["dve"], "dve")

            @block.gpsimd
            def _(e):
                run(e, per["pool"], "pool")

            @block.sync
            def _(e):
                run(e, per["sp"], "sp")


class Ring:
    def __init__(self, items):
        self.items = items
        self.i = 0

    def next(self):
        it = self.items[self.i % len(self.items)]
        self.i += 1
        return it


class _Stop(Exception):
    pass


def build(debug=None):
    debug = debug or {}
    stop_at = debug.get("stop", 10 ** 9)
    open_stacks = []

    def ckpt(n):
        if n >= stop_at:
            raise _Stop()
    nc = bass.Bass("TRN2", target_bir_lowering=False)
    es = ExitStack()
    kb = KB(nc)

    def din(name, shape, dt=F32):
        return nc.dram_tensor(name, list(shape), dt, kind="ExternalInput").ap()

    def dscr(name, shape, dt):
        kind = "ExternalOutput" if name in debug else "Internal"
        return nc.dram_tensor(name, list(shape), dt, kind=kind).ap()

    xT = din("xT", [D, T])
    pos_tm = din("pos_tm", [128, NB], I32)
    kbias_d = din("kbias", [128, NB])
    a_norm_g = din("a_norm_g", [D])
    a_w_in = din("a_w_in", [D, 6208])
    a_q_a_norm_g = din("a_q_a_norm_g", [512])
    a_w_q_b = din("a_w_q_b", [512, 1536])
    a_kv_a_norm_g = din("a_kv_a_norm_g", [512])
    a_w_kv_b = din("a_w_kv_b", [512, 2048])
    a_q_norm_g = din("a_q_norm_g", [192])
    a_k_norm_g = din("a_k_norm_g", [192])
    a_ret_norm_g = din("a_ret_norm_g", [1024])
    a_w_out = din("a_w_out", [D, D])
    c_norm_g = din("c_norm_g", [D])
    c_w_in = din("c_w_in", [D, 6144])
    c_conv_wT = din("c_conv_wT", [D, 31])
    c_conv_b = din("c_conv_b", [D])
    c_ln_g = din("c_ln_g", [D])
    c_ln_b = din("c_ln_b", [D])
    c_w_out = din("c_w_out", [D, D])
    c_ident = din("c_ident", [128, 128], BF16)
    c_tri = din("c_tri", [128, 128], BF16)
    c_dmaskT = din("c_dmaskT", [128, 8, 128])
    c_toend = din("c_toend", [128, 8])
    c_fscol = din("c_fscol", [128, 8])
    c_invr = din("c_invr", [128, 64])
    c_invm = din("c_invm", [128, 32])
    outT = nc.dram_tensor("outT", [D, 2048], F32, kind="ExternalOutput").ap()

    gate_s = dscr("gate_s", [D, TO], BF16)
    mix_s = dscr("mix_s", [D, TO], BF16)
    x1_s = dscr("x1_s", [D, TO], F32)
    v_s = dscr("v_s", [D, 2048], F32)
    gate1_s = dscr("gate1_s", [D, 2048], BF16)
    dbg_s = {k: dscr(k, v[0], v[1]) for k, v in debug.items() if k.startswith("dbg_")}

    def sb(name, shape, dt):
        return es.enter_context(nc.sbuf_tensor(name, list(shape), dt))

    def pst(name, shape, dt=F32):
        return es.enter_context(nc.psum_tensor(name, list(shape), dt))

    psum = [pst(f"ps{i}", [128, 512]) for i in range(8)]
    ptok = [Tok(f"ps{i}") for i in range(8)]
    PS = Ring(list(zip(psum, ptok)))

    def bufring(name, n, shape, dt):
        return Ring([(sb(f"{name}{i}", shape, dt), Tok(f"{name}{i}")) for i in range(n)])

    hT_tok = [Tok(f"hT{g}") for g in range(8)]
    latT_tok = [Tok(f"lat{g}") for g in range(8)]
    kropeT_tok = [Tok(f"krT{b}") for b in range(NB)]
    ssrope = sb("ssrope", [128, NB], F32)
    ssrope_tok = [Tok(f"ssr{b}") for b in range(NB)]
    cqn_tok = [Tok(f"cqn{g}") for g in range(5)]

    ident = sb("ident", [128, 128], BF16)
    tri = sb("tri", [128, 128], BF16)
    ones_bf = sb("ones_bf", [128, 128], BF16)
    dmaskT = sb("dmaskT", [128, 8, 128], F32)
    toend = sb("toend", [128, 8], F32)
    fscol = sb("fscol", [128, 8], F32)
    invr = sb("invr", [128, 64], F32)
    invm = sb("invm", [128, 32], F32)
    posi = sb("posi", [128, NB], I32)
    posf = sb("posf", [128, NB], F32)
    kbias = sb("kbias_sb", [128, NB], F32)
    CONST = Tok("const")

    def dma(eng, out, in_, reads=(), writes=(), **kw):
        return kb.op(eng, lambda e: e.dma_start(out=out, in_=in_, **kw), reads, writes, dma=True)

    for (dst, src) in ((ident, c_ident), (tri, c_tri), (dmaskT, c_dmaskT), (toend, c_toend),
                       (fscol, c_fscol), (invr, c_invr), (invm, c_invm), (posi, pos_tm),
                       (kbias, kbias_d)):
        dma("sp", dst[:], src, writes=[CONST])
    kb.op("dve", lambda e: e.memset(ones_bf[:], 1.0), writes=[CONST])
    kb.op("dve", lambda e: e.tensor_copy(out=posf[:], in_=posi[:]), reads=[CONST], writes=[CONST])

    def gvec(name, src, n):
        t = sb(name, [128, n], F32)
        dma("sp", t[:], src.rearrange("(c p) -> p c", p=128), writes=[CONST],
            allow_slow_non_contiguous=True)
        return t

    g_a = gvec("g_a", a_norm_g, 16)
    g_c = gvec("g_c", c_norm_g, 16)
    g_qa = gvec("g_qa", a_q_a_norm_g, 4)
    g_kva = gvec("g_kva", a_kv_a_norm_g, 4)
    g_ret = gvec("g_ret", a_ret_norm_g, 8)
    cb = gvec("cb", c_conv_b, 16)
    lng = gvec("lng", c_ln_g, 16)
    lnb = gvec("lnb", c_ln_b, 16)

    xs_ring = bufring("xs", 3, [128, 512], F32)
    XS_DEFAULT = [xs_ring]
    sq_ring = bufring("sq", 3, [128, 512], BF16)
    rstd_ring = bufring("rstd", 2, [128, 512], F32)

    eps_t = sb("eps_t", [128, 1], F32)
    kb.op("dve", lambda e: e.memset(eps_t[:], EPS), writes=[CONST])

    def rsqrt_ms(out_ap, in_ap, n, in_tok, out_tok):
        kb.op("act", lambda e: e.activation(out=out_ap, in_=in_ap, func=AF.Ln, bias=eps_t[:out_ap.shape[0], :], scale=1.0 / n),
              reads=[in_tok, CONST], writes=[out_tok])
        kb.op("act", lambda e: e.activation(out=out_ap, in_=out_ap, func=AF.Exp, scale=-0.5), reads=[out_tok], writes=[out_tok])

    def norm_pass(src, src_col0, groups, gvec_t, out_t, out_toks, ncheck=D, xs_ring=None):
        xs_ring = xs_ring or XS_DEFAULT[0]
        for gi, (c0, n) in enumerate(groups):
            pss, pst_ = PS.next()
            xts = []
            for c in range(16):
                xt, xtok = xs_ring.next() if False else (None, None)
            for c in range(16):
                xt, xtok = xs_ring.next()
                dma("sp", xt[:, :n], src[c * 128:(c + 1) * 128, src_col0 + c0: src_col0 + c0 + n],
                    writes=[xtok])
                sq, sqtok = sq_ring.next()
                kb.op("act", lambda e, sq=sq, xt=xt, n=n: e.activation(out=sq[:, :n], in_=xt[:, :n], func=AF.Square),
                      reads=[xtok], writes=[sqtok])
                kb.op("pe", lambda e, pss=pss, sq=sq, n=n, c=c: e.matmul(pss[:, :n], lhsT=ones_bf[:], rhs=sq[:, :n],
                                                                      start=(c == 0), stop=(c == 15)),
                      reads=[sqtok, CONST], writes=[pst_])
            rs, rstok = rstd_ring.next()
            rsqrt_ms(rs[:, :n], pss[:, :n], ncheck, pst_, rstok)
            for c in range(16):
                xt, xtok = xs_ring.next()
                dma("sp", xt[:, :n], src[c * 128:(c + 1) * 128, src_col0 + c0: src_col0 + c0 + n],
                    writes=[xtok])
                eng = "dve"
                kb.op(eng, lambda e, xt=xt, rs=rs, c=c, c0=c0, n=n: e.scalar_tensor_tensor(
                    out=out_t[:, c, c0:c0 + n], in0=xt[:, :n], scalar=gvec_t[:, c:c + 1], in1=rs[:, :n],
                    op0=ALU.mult, op1=ALU.mult),
                    reads=[xtok, rstok, CONST], writes=[out_toks[gi]])


    def load_w(w_dram, col0, ncols, kch=16):
        wt, wtok = w_ring.next()
        dma("pool", wt[:, :kch, :ncols],
            w_dram[:, col0:col0 + ncols].rearrange("(k p) n -> p k n", p=128),
            writes=[wtok])
        return wt, wtok


    def own_groups():
        return [(0, 128), (128, 512), (640, 512), (1152, 512), (1664, 512)]

    def pre_groups():
        return [(0, 512), (512, 512), (1024, 512), (1536, 384)]

    try:
        def mm(ps_ap, lhsT, rhs, start, stop, reads, writes):
            return kb.op("pe", lambda e: e.matmul(ps_ap, lhsT=lhsT, rhs=rhs, start=start, stop=stop), reads, writes)

        def tr(ps_ap, in_ap, reads, writes):
            k = in_ap.shape[0]
            return kb.op("pe", lambda e: e.transpose(out=ps_ap, in_=in_ap, identity=ident[:k, :k]), list(reads) + [CONST], writes)

        def act(out, in_, func, reads, writes, **kw):
            return kb.op("act", lambda e: e.activation(out=out, in_=in_, func=func, **kw), reads, writes)

        def tt(eng, out, in0, in1, op, reads, writes):
            return kb.op(eng, lambda e: e.tensor_tensor(out=out, in0=in0, in1=in1, op=op), reads, writes)

        def ts(eng, out, in0, s1, s2, op0, op1, reads, writes):
            if s2 is None:
                return kb.op(eng, lambda e: e.tensor_scalar(out=out, in0=in0, scalar1=s1, scalar2=None, op0=op0), reads, writes)
            return kb.op(eng, lambda e: e.tensor_scalar(out=out, in0=in0, scalar1=s1, scalar2=s2, op0=op0, op1=op1), reads, writes)

        def stt(out, in0, scalar, in1, op0, op1, reads, writes):
            return kb.op("dve", lambda e: e.scalar_tensor_tensor(out=out, in0=in0, scalar=scalar, in1=in1, op0=op0, op1=op1),
                         reads, writes)

        def cp(eng, out, in_, reads, writes):
            if eng == "act":
                return act(out, in_, AF.Copy, reads, writes)
            return kb.op(eng, lambda e: e.tensor_copy(out=out, in_=in_), reads, writes)

        def rsqrt_ms_dve(out_ap, in_ap, n, in_tok, out_tok):
            ts("dve", out_ap, in_ap, 1.0 / n, EPS, ALU.mult, ALU.add, [in_tok], [out_tok])
            act(out_ap, out_ap, AF.Ln, [out_tok], [out_tok])
            act(out_ap, out_ap, AF.Exp, [out_tok], [out_tok], scale=-0.5)

        PS.items = PS.items[:4]
        pT4 = psum[4][:].bitcast(BF16)[:, 0:128]
        pT6 = psum[6][:].bitcast(BF16)[:, 0:128]
        PT = Ring([(pT4, ptok[4]), (pT6, ptok[6])])
        PSS = Ring([(psum[5][:, 0:128], ptok[5]), (psum[7][:, 0:128], ptok[7])])
        ACC = [(psum[6], ptok[6]), (psum[7], ptok[7])]

        krope_s = dscr("krope_s", [64, T], BF16)
        lat_s = dscr("lat_s", [512, T], BF16)
        cqn_s = dscr("cqn_s", [512, TO], BF16)

        def bc_row(name, src_ap, n):
            t = sb(name, [128, n], F32)
            dma("sp", t[:], src_ap.partition_broadcast(128), writes=[CONST])
            return t

        gk_rope_bc = bc_row("gk_rope_bc", a_k_norm_g[128:192], 64)
        gq_bc = bc_row("gq_bc", a_q_norm_g, 192)
        gk_nope = sb("gk_nope", [128, 1], F32)
        dma("sp", gk_nope[:], a_k_norm_g[0:128].rearrange("(p o) -> p o", o=1), writes=[CONST])
        kb.op("dve", lambda e: e.memset(ssrope[:], 0.0), [], ssrope_tok)
        decay = [float(x) for x in _consts()["_decay"]]

        st_ring = bufring("stg", 3, [128, 512], BF16)
        rp_ring = bufring("rp", 4, [128, 256], F32)
        craw_ring = bufring("craw", 4, [128, 512], F32)


        def rope_tm(out_bf, x_sb, nh, half, cos_ap, sin_ap, reads, wtok):
            x4 = x_sb.rearrange("p h (t d) -> p h t d", t=2)
            xsw = x4[:, :, ::-1, :] if False else None
            cb = cos_ap.unsqueeze(1).unsqueeze(1).to_broadcast([128, nh, 2, half])
            sbb = sin_ap.unsqueeze(1).to_broadcast([128, nh, half])
            (ta, tatok), (tb, tbtok) = rp_ring.next(), rp_ring.next()
            n2 = nh * 2 * half
            ta4 = ta[:, :n2].rearrange("p (h t d) -> p h t d", h=nh, t=2)
            tb4 = tb[:, :n2].rearrange("p (h t d) -> p h t d", h=nh, t=2)
            rd = list(reads) + [ROPE]
            tt("dve", ta4, x4, cb, ALU.mult, rd, [tatok])
            tt("dve", tb4[:, :, 0, :], x4[:, :, 1, :], sbb, ALU.mult, rd, [tbtok])
            tt("dve", tb4[:, :, 1, :], x4[:, :, 0, :], sbb, ALU.mult, rd + [tbtok], [tbtok])
            o4 = out_bf.rearrange("p h (t d) -> p h t d", t=2)
            tt("dve", o4[:, :, 0, :], ta4[:, :, 0, :], tb4[:, :, 0, :], ALU.subtract, [tatok, tbtok], [wtok])
            tt("dve", o4[:, :, 1, :], ta4[:, :, 1, :], tb4[:, :, 1, :], ALU.add, [tatok, tbtok, wtok], [wtok])

        def fm_sweep(w_dram, col0, ncols, groups, evac, act_t, act_toks, act_col0=0, kch=16):
            for t0 in range(0, ncols, 256):
                wt, wtok = load_w(w_dram, col0 + t0, 256, kch)
                for gi, (c0, n) in enumerate(groups):
                    for j in range(2):
                        pss, pst_ = PS.next()
                        for k in range(kch):
                            mm(pss[:, :n], wt[:, k, j * 128:(j + 1) * 128], act_t[:, k, act_col0 + c0:act_col0 + c0 + n],
                               k == 0, k == kch - 1, [wtok, act_toks[gi]], [pst_])
                        evac((t0 // 128) + j, gi, c0, n, pss, pst_)

        def lat_sweep(col0, gv, groups, toks, out_dram, out_col0):
            w0, w0tok = load_w(a_w_in, col0, 256)
            w1, w1tok = load_w(a_w_in, col0 + 256, 256)
            lm = debug.get("lat_mode", 0)
            for gi, (c0, n) in enumerate(groups):
                ssp, ssptok = ACC[gi % 2]
                raws = []
                pend_ss = None
                for cc in range(4):
                    wt, wtok = (w0, w0tok) if cc < 2 else (w1, w1tok)
                    pss, pst_ = PS.next()
                    for k in range(16):
                        mm(pss[:, :n], wt[:, k, (cc % 2) * 128:(cc % 2 + 1) * 128], hT[:, k, c0:c0 + n], k == 0, k == 15,
                           [wtok, toks[gi]], [pst_])
                    raw, rawtok = craw_ring.next()
                    cp(debug.get("cpeng", "dve"), raw[:, :n], pss[:, :n], [pst_], [rawtok])
                    sq, sqtok = sq_ring.next()
                    if debug.get("sqsrc", "psum") == "psum":
                        act(sq[:, :n], pss[:, :n], AF.Square, [pst_], [sqtok])
                    else:
                        act(sq[:, :n], raw[:, :n], AF.Square, [rawtok], [sqtok])
                    if pend_ss is not None:
                        pend_ss()
                    pend_ss = (lambda sq=sq, sqtok=sqtok, cc=cc: mm(ssp[:, :n], ones_bf[:], sq[:, :n], cc == 0, cc == 3,
                                                                   [sqtok, CONST], [ssptok]))
                    raws.append((raw, rawtok))
                pend_ss()
                rs, rstok = rstd_ring.next()
                rsqrt_ms(rs[:, :n], ssp[:, :n], 512, ssptok, rstok)
                for cc in range(4):
                    raw, rawtok = raws[cc]
                    st, sttok = st_ring.next()
                    stt(st[:, :n], raw[:, :n], gv[:, cc:cc + 1], rs[:, :n], ALU.mult, ALU.mult, [rawtok, rstok, CONST], [sttok])
                    dma("sp", out_dram[cc * 128:(cc + 1) * 128, out_col0 + c0:out_col0 + c0 + n], st[:, :n], reads=[sttok])
                yield

        junk = sb("junk", [128, 192], F32)
        JUNK = Tok("junk")

        def krope_sweep(blocks, tok_of_block, list_b0):
            wt, wtok = krw, krwtok
            dma("pool", wt[:, :, :64], a_w_in[:, 4096:4160].rearrange("(k p) n -> p k n", p=128), writes=[wtok])
            for lb in blocks:
                gb = list_b0 + lb
                pss, pst_ = PSS.next()
                for k in range(16):
                    mm(pss[:, :64], hT[:, k, lb * 128:(lb + 1) * 128], wt[:, k, :64], k == 0, k == 15,
                       [wtok, tok_of_block(lb)], [pst_])
                kraw, krawtok = kr_ring.next()
                cp("dve", kraw[:], pss[:, :64], [pst_], [krawtok])
                yield
                act(junk[:, :64], kraw[:], AF.Square, [krawtok, JUNK, ssrope_tok[gb]], [JUNK, ssrope_tok[gb]], accum_out=ssrope[:, gb:gb + 1])
                kr, krtok = kr_ring.next()
                tt("pool", kr[:], kraw[:], gk_rope_bc[:], ALU.mult, [krawtok, CONST], [krtok])
                krb, krbtok = krb_ring.next()
                rope_tm(krb[:].rearrange("p (h d) -> p h d", h=1), kr[:].rearrange("p (h d) -> p h d", h=1), 1, 32,
                        cosm[:, gb, :], sinm[:, gb, :], [krtok], krbtok)
                pt, pttok = PT.next()
                tr(pt[:64, :], krb[:], [krbtok], [pttok])
                krt, krttok = krt_ring.next()
                cp("act", krt[:], pt[:64, :], [pttok], [krttok])
                dma("act", krope_s[:, gb * 128:(gb + 1) * 128], krt[:], reads=[krttok])
                yield

        state = sb("state", [128, 8, 128], F32)
        state_bf = sb("state_bf", [128, 8, 128], BF16)
        ST_tok = [Tok(f"state{h}") for h in range(8)]
        STB_tok = [Tok(f"stateb{h}") for h in range(8)]
        kb.op("dve", lambda e: e.memset(state[:], 0.0), writes=ST_tok)
        kb.op("pool", lambda e: e.memset(state_bf[:], 0.0), writes=STB_tok)
        PTW = Ring([(psum[4][:].bitcast(BF16), ptok[4]), (psum[6][:].bitcast(BF16), ptok[6])])
        PSW = Ring([(psum[5], ptok[5]), (psum[7], ptok[7])])

        def v3(ap):
            return ap.rearrange("p (h d) -> p h d", h=2)

        def run_threads(gens):
            gens = list(gens)
            while gens:
                for g_ in list(gens):
                    try:
                        next(g_)
                    except StopIteration:
                        gens.remove(g_)

        def ret_thread(hp, RS, blocks, tok_of_block, list_b0, own):
            h0 = hp * 2
            wk, wktok = load_w(a_w_in, 1024 + hp * 256, 256)
            wv, wvtok = load_w(a_w_in, 2048 + hp * 256, 256)
            if own:
                wq, wqtok = load_w(a_w_in, hp * 256, 256)

            def stage_a(lb):
                gb = list_b0 + lb
                ht = tok_of_block(lb)
                cols = slice(lb * 128, (lb + 1) * 128)
                A = {"cols": cols}

                def proj(w, wtok_):
                    pss, pst_ = PS.next()
                    for k in range(16):
                        mm(pss[:, :256], hT[:, k, cols], w[:, k, :], k == 0, k == 15, [wtok_, ht], [pst_])
                    return pss, pst_
                if own:
                    pss, pst_ = proj(wq, wqtok)
                    A["qtm"], A["qtmtok"] = RS["qtm"].next()
                    rope_tm(v3(A["qtm"][:]), v3(pss[:, :256]), 2, 64, cosr[:, gb, :], sinr[:, gb, :], [pst_], A["qtmtok"])
                    A["gt"], A["gttok"] = RS["gt"].next()
                    dma("sp", A["gt"][:], gate_s[h0 * 128:(h0 + 2) * 128, cols].rearrange("(j p) c -> p j c", p=128),
                        writes=[A["gttok"]])
                    yield
                pss, pst_ = proj(wk, wktok)
                A["ktm"], A["ktmtok"] = RS["ktm"].next()
                rope_tm(v3(A["ktm"][:]), v3(pss[:, :256]), 2, 64, cosr[:, gb, :], sinr[:, gb, :], [pst_], A["ktmtok"])
                yield
                pss, pst_ = proj(wv, wvtok)
                A["vtm"], A["vtmtok"] = RS["vtm"].next()
                cp("act", A["vtm"][:], pss[:, :256], [pst_], [A["vtmtok"]])
                yield
                return A

            def stage_b(A):
                cols = A["cols"]
                ktm, ktmtok, vtm, vtmtok = A["ktm"], A["ktmtok"], A["vtm"], A["vtmtok"]
                sts = [ST_tok[h0], ST_tok[h0 + 1]]
                stbs = [STB_tok[h0], STB_tok[h0 + 1]]
                ks, kstok = RS["b256"].next()
                tt("dve", v3(ks[:]), v3(ktm[:]), toend[:, h0:h0 + 2].unsqueeze(2).to_broadcast([128, 2, 128]), ALU.mult,
                   [ktmtok, CONST], [kstok])
                if own:
                    qtm, qtmtok = A["qtm"], A["qtmtok"]
                    qs, qstok = RS["b256"].next()
                    tt("dve", v3(qs[:]), v3(qtm[:]), fscol[:, h0:h0 + 2].unsqueeze(2).to_broadcast([128, 2, 128]), ALU.mult,
                       [qtmtok, CONST], [qstok])
                    pT, pTtok = PTW.next()
                    for j in range(2):
                        tr(pT[:, j * 128:(j + 1) * 128], qtm[:, j * 128:(j + 1) * 128], [qtmtok], [pTtok])
                        tr(pT[:, 256 + j * 128:256 + (j + 1) * 128], qs[:, j * 128:(j + 1) * 128], [qstok], [pTtok])
                        tr(pT[:, 512 + j * 128:512 + (j + 1) * 128], ktm[:, j * 128:(j + 1) * 128], [ktmtok], [pTtok])
                    tT, tTtok = RS["tT"].next()
                    cp("act", tT[:], pT[:, :768], [pTtok], [tTtok])
                    yield
                    sc, sctok = PSW.next()
                    for j in range(2):
                        mm(sc[:, j * 128:(j + 1) * 128], tT[:, 512 + j * 128:512 + (j + 1) * 128], tT[:, j * 128:(j + 1) * 128],
                           True, True, [tTtok], [sctok])
                    AT, ATtok = RS["b256"].next()
                    tt("dve", v3(AT[:]), v3(sc[:, :256]), dmaskT[:, h0:h0 + 2, :], ALU.mult, [sctok, CONST], [ATtok])
                    yield
                    rp, rptok = PSW.next()
                    for j in range(2):
                        js = slice(j * 128, (j + 1) * 128)
                        mm(rp[:, js], vtm[:, js], AT[:, js], True, False, [vtmtok, ATtok], [rptok])
                        mm(rp[:, js], state_bf[:, h0 + j, :], tT[:, 256 + j * 128:256 + (j + 1) * 128], False, True,
                           [stbs[j], tTtok], [rptok])
                    oraw, orawtok = RS["f256"].next()
                    cp("dve", oraw[:], rp[:, :256], [rptok], [orawtok])
                    sq, sqtok = RS["b256"].next()
                    act(sq[:], oraw[:], AF.Square, [orawtok], [sqtok])
                    yield
                kv, kvtok = PSW.next()
                for j in range(2):
                    js = slice(j * 128, (j + 1) * 128)
                    mm(kv[:, js], ks[:, js], vtm[:, js], True, True, [kstok, vtmtok], [kvtok])
                for j in range(2):
                    js = slice(j * 128, (j + 1) * 128)
                    stt(state[:, h0 + j, :], state[:, h0 + j, :], decay[h0 + j], kv[:, js], ALU.mult, ALU.add,
                        [sts[j], kvtok], [sts[j]])
                cp("pool", state_bf[:, h0:h0 + 2, :], state[:, h0:h0 + 2, :], sts, stbs)
                yield
                if own:
                    ssp, ssptok = PSW.next()
                    for j in range(2):
                        js = slice(j * 128, (j + 1) * 128)
                        mm(ssp[:, js], ones_bf[:], sq[:, js], True, True, [sqtok, CONST], [ssptok])
                    rs, rstok = RS["f256"].next()
                    rsqrt_ms_dve(rs[:], ssp[:, :256], 128, ssptok, rstok)
                    tt("dve", rs[:], oraw[:], rs[:], ALU.mult, [orawtok, rstok], [rstok])
                    tt("pool", v3(rs[:]), v3(rs[:]), g_ret[:, h0:h0 + 2].unsqueeze(2).to_broadcast([128, 2, 128]), ALU.mult,
                       [rstok, CONST], [rstok])
                    st, sttok = st_ring.next()
                    tt("pool", st[:, :256], rs[:], A["gt"][:].rearrange("p j c -> p (j c)"), ALU.mult, [rstok, A["gttok"]], [sttok])
                    dma("pool", mix_s[h0 * 128:(h0 + 2) * 128, cols].rearrange("(j p) c -> p j c", p=128), v3(st[:, :256]),
                        reads=[sttok])
                    yield

            pendA = None
            for lb in blocks:
                ga = stage_a(lb)
                gb_ = stage_b(pendA) if pendA is not None else iter(())
                A = None
                a_done = b_done = False
                while not (a_done and b_done):
                    if not a_done:
                        try:
                            next(ga)
                        except StopIteration as e_:
                            A = e_.value
                            a_done = True
                    if not b_done:
                        try:
                            next(gb_)
                        except StopIteration:
                            b_done = True
                    yield
                pendA = A
            for _ in stage_b(pendA):
                yield

        def ret_sweep(blocks, tok_of_block, list_b0, own, extra=()):
            for hp0 in (0, 2):
                run_threads([ret_thread(hp0 + i, RSETS[i], blocks, tok_of_block, list_b0, own) for i in range(2)]
                            + (list(extra) if hp0 == 0 else []))

        def gate_evac(dst, dst_col0):
            def f(cc, gi, c0, n, pss, pst_):
                st, sttok = st_ring.next()
                act(st[:, :n], pss[:, :n], AF.Silu, [pst_], [sttok])
                dma("act", dst[cc * 128:(cc + 1) * 128, dst_col0 + c0:dst_col0 + c0 + n], st[:, :n], reads=[sttok])
            return f

        es_rope = ExitStack()
        es_rope_r = ExitStack()
        open_stacks.extend([es_rope, es_rope_r])

        def sbx(stack, name, shape, dt):
            return stack.enter_context(nc.sbuf_tensor(name, list(shape), dt))

        cosm = sbx(es_rope, "cosm", [128, NB, 32], F32)
        sinm = sbx(es_rope, "sinm", [128, NB, 32], F32)
        w_ring = Ring([(sbx(es_rope_r, f"wt{i}", [128, 16, 256], BF16), Tok(f"wt{i}")) for i in range(6)])
        hT = sbx(es_rope_r, "hT", [128, 16, TO], BF16)
        cosr = sbx(es_rope_r, "cosr", [128, NB, 64], F32)
        sinr = sbx(es_rope_r, "sinr", [128, NB, 64], F32)

        def bufring_s(stack, name, n, shape, dt):
            return Ring([(sbx(stack, f"{name}{i}", shape, dt), Tok(f"{name}{i}")) for i in range(n)])
        RSETS = []
        for ti in range(2):
            RSETS.append(dict(
                ktm=bufring_s(es_rope_r, f"ktm{ti}_", 2, [128, 256], BF16),
                vtm=bufring_s(es_rope_r, f"vtm{ti}_", 2, [128, 256], BF16),
                qtm=bufring_s(es_rope_r, f"qtm{ti}_", 2, [128, 256], BF16),
                b256=bufring_s(es_rope_r, f"b256{ti}_", 4, [128, 256], BF16),
                f256=bufring_s(es_rope_r, f"f256{ti}_", 2, [128, 256], F32),
                tT=bufring_s(es_rope_r, f"tT{ti}_", 1, [128, 768], BF16),
                gt=bufring_s(es_rope_r, f"gt{ti}_", 2, [128, 2, 128], BF16)))
        kr_ring = bufring_s(es_rope_r, "kr", 4, [128, 64], F32)
        krb_ring = bufring_s(es_rope_r, "krb", 2, [128, 64], BF16)
        krt_ring = bufring_s(es_rope_r, "krt", 2, [64, 128], BF16)
        krw = sbx(es_rope_r, "krw", [128, 16, 64], BF16)
        krwtok = Tok("krw")
        ROPE = Tok("rope")
        with ExitStack() as tmp:
            r2 = xs_ring.items[0][0][:, 0:512].rearrange("p (b f) -> p b f", b=8)
            rf = xs_ring.items[1][0][:, 0:512].rearrange("p (b f) -> p b f", b=8)
            ri = xs_ring.items[2][0][:, 0:512].bitcast(I32).rearrange("p (b f) -> p b f", b=8)
            RTL = [it[1] for it in xs_ring.items]
        if True:
            def rope_tables():
              for (cs, sn, inv_t, half) in ((cosm, sinm, invm, 32), (cosr, sinr, invr, 64)):
                for q8 in range(4):
                    bs = slice(q8 * 8, q8 * 8 + 8)
                    for (dst, shift) in ((sn, 0.0), (cs, 0.25)):
                        tt("dve", r2[:, :, :half], posf[:, bs].unsqueeze(2).to_broadcast([128, 8, half]),
                           inv_t[:, :half].unsqueeze(1).to_broadcast([128, 8, half]), ALU.mult, [CONST] + RTL, RTL)
                        ts("dve", r2[:, :, :half], r2[:, :, :half], 1.0 / (2 * math.pi), shift, ALU.mult, ALU.add, RTL, RTL)
                        cp("dve", ri[:, :, :half], r2[:, :, :half], RTL, RTL)
                        cp("dve", rf[:, :, :half], ri[:, :, :half], RTL, RTL)
                        tt("dve", r2[:, :, :half], r2[:, :, :half], rf[:, :, :half], ALU.subtract, RTL, RTL)
                        ts("dve", rf[:, :, :half], r2[:, :, :half], 0.5, None, ALU.is_gt, None, RTL, RTL)
                        tt("dve", r2[:, :, :half], r2[:, :, :half], rf[:, :, :half], ALU.subtract, RTL, RTL)
                        ts("dve", rf[:, :, :half], r2[:, :, :half], -0.5, None, ALU.is_lt, None, RTL, RTL)
                        tt("dve", r2[:, :, :half], r2[:, :, :half], rf[:, :, :half], ALU.add, RTL, RTL)
                        act(dst[:, bs, :], r2[:, :, :half], AF.Sin, RTL, [ROPE] + RTL, scale=2 * math.pi)
                    yield

        ckpt(1)
        PG = pre_groups()
        OG = own_groups()

        def tokP(lb):
            return hT_tok[lb // 4]

        def tokO(lb):
            return hT_tok[0] if lb == 0 else hT_tok[1 + (lb - 1) // 4]

        xs_big = Ring(xs_ring.items + craw_ring.items)
        norm_pass(xT, 0, PG, g_a, hT, hT_tok[:4], xs_ring=xs_big)
        ckpt(2)
        run_threads([lat_sweep(3584, g_kva, PG, hT_tok[:4], lat_s, 0), rope_tables()])
        ckpt(3)
        ckpt(4)
        ret_sweep(range(15), tokP, 0, False, extra=[krope_sweep(range(15), tokP, 0)])
        ckpt(5)
        norm_pass(xT, TP, OG, g_a, hT, hT_tok[:5], xs_ring=xs_big)
        ckpt(6)
        fm_sweep(a_w_in, 4160, 2048, OG, gate_evac(gate_s, 0), hT, hT_tok[:5])
        kb.barrier()
        for _ in lat_sweep(3584, g_kva, OG, hT_tok[:5], lat_s, TP):
            pass
        ckpt(8)
        for _ in lat_sweep(3072, g_qa, OG, hT_tok[:5], cqn_s, 0):
            pass
        ckpt(9)
        ckpt(10)
        ret_sweep(range(17), tokO, 15, True, extra=[krope_sweep(range(17), tokO, 15)])
        ckpt(11)
        kb.barrier()
        es_rope_r.close()

        with ExitStack() as ph3:
            latT = sbx(ph3, "latT", [128, 4, T], BF16)
            kropeT = sbx(ph3, "kropeT", [128, T], BF16)
            wkvb_ring = Ring([(sbx(ph3, f"wkvb{i}", [128, 4, 256], BF16), Tok(f"wkvb{i}")) for i in range(2)])
            wqb_ring = Ring([(sbx(ph3, f"wqb{i}", [128, 4, 192], BF16), Tok(f"wqb{i}")) for i in range(2)])
            cqn_ring = Ring([(sbx(ph3, f"cqnb{i}", [128, 4, 128], BF16), Tok(f"cqnb{i}")) for i in range(3)])
            BS = []
            for i in range(2):
                BS.append(dict(KT=sbx(ph3, f"KT{i}", [128, T], BF16), Vh=sbx(ph3, f"Vh{i}", [128, NB, 128], BF16),
                               qT=sbx(ph3, f"qT{i}", [128, TO], BF16), qrT=sbx(ph3, f"qrT{i}", [128, TO], BF16),
                               kscale=sbx(ph3, f"kscale{i}", [128, NB], F32),
                               KTt=Tok(f"KT{i}"), VHt=Tok(f"Vh{i}"), QTt=Tok(f"qT{i}"), KSC=Tok(f"ksc{i}")))
            qn = sbx(ph3, "qn", [128, 192], F32)
            qbf_ring = Ring([(sbx(ph3, f"qbf{i}", [128, 192], BF16), Tok(f"qbf{i}")) for i in range(2)])
            qss = sbx(ph3, "qss", [128, 1], F32)
            rden_ring = Ring([(sbx(ph3, f"rden{i}", [128, 512], F32), Tok(f"rden{i}")) for i in range(2)])
            attf_ring = Ring([(sbx(ph3, f"attf{i}", [128, 512], F32), Tok(f"attf{i}")) for i in range(2)])
            gta_ring = Ring([(sbx(ph3, f"gta{i}", [128, 512], BF16), Tok(f"gta{i}")) for i in range(2)])
            pt_ring = Ring([(sbx(ph3, f"ptile{i}", [128, 512], BF16), Tok(f"ptile{i}")) for i in range(5)])
            psum_acc = [(sbx(ph3, f"psm{i}", [128, 512], F32), Tok(f"psm{i}")) for i in range(2)]
            ones_f32 = sbx(ph3, "ones_f32", [128, 128], F32)
            kb.op("pool", lambda e: e.memset(ones_f32[:], 1.0), writes=[CONST])
            LAT, KRT, CQN, WKV, WQ = Tok("lat"), Tok("krt"), Tok("cqn"), Tok("wkv"), Tok("wq")
            QN, QSS = Tok("qn"), Tok("qss")
            for c in range(4):
                dma("sp", latT[:, c, :], lat_s[c * 128:(c + 1) * 128, :], writes=[LAT])
            kb.op("pool", lambda e: e.memset(kropeT[64:128, :], 0.0), writes=[KRT])
            dma("sp", kropeT[0:64, :], krope_s, writes=[KRT])
            for i_ in range(2):
                kb.op("pool", lambda e, i_=i_: e.memset(BS[i_]["qrT"][64:128, :], 0.0), writes=[BS[i_]["QTt"]])
            PT = Ring([(pT4, ptok[4])])
            PS_save = PS.items
            PS.items = PS_save[:3]
            PSS = PS
            ACCP = [((psum[3], ptok[3]), (psum[6], ptok[6])), ((psum[5], ptok[5]), (psum[7], ptok[7]))]
            def prep(h, B):
                KT, Vh, qT, qrT, kscale = B["KT"], B["Vh"], B["qT"], B["qrT"], B["kscale"]
                KTt, VHt, QTt, KSC = B["KTt"], B["VHt"], B["QTt"], B["KSC"]
                wkvb, WKV = wkvb_ring.next()
                dma("pool", wkvb[:], a_w_kv_b[:, h * 256:(h + 1) * 256].rearrange("(k p) n -> p k n", p=128), writes=[WKV])
                wqb, WQ = wqb_ring.next()
                dma("pool", wqb[:], a_w_q_b[:, h * 192:(h + 1) * 192].rearrange("(k p) n -> p k n", p=128), writes=[WQ])
                kss, ksstok = psum[4][:, 0:128], ptok[4]
                def kss_mm(g, sq, sqtok):
                    for b4 in range(4):
                        b = g * 4 + b4
                        mm(kss[:, b:b + 1], sq[:, b4 * 128:(b4 + 1) * 128], ones_bf[:, 0:1], True, True, [sqtok, CONST], [ksstok])

                prevk = None
                for g in range(8):
                    gs = slice(g * 512, (g + 1) * 512)
                    pss, pst_ = PS.next()
                    for k in range(4):
                        mm(pss[:], wkvb[:, k, 0:128], latT[:, k, gs], k == 0, k == 3, [WKV, LAT], [pst_])
                    sq, sqtok = sq_ring.next()
                    act(sq[:], pss[:], AF.Square, [pst_], [sqtok])
                    ts("dve", KT[:, gs], pss[:], gk_nope[:, 0:1], None, ALU.mult, None, [pst_, CONST], [KTt])
                    if prevk is not None:
                        kss_mm(*prevk)
                    prevk = (g, sq, sqtok)
                    yield
                kss_mm(*prevk)
                tt("dve", kscale[:], kss[:, :NB], ssrope[:], ALU.add, [ksstok] + ssrope_tok, [KSC])
                rsqrt_ms(kscale[:], kscale[:], 192, KSC, KSC)
                ts("dve", kscale[:], kscale[:], 192.0 ** -0.5, None, ALU.mult, None, [KSC], [KSC])
                for g in range(8):
                    pss, pst_ = PS.next()
                    for b4 in range(4):
                        b = g * 4 + b4
                        for k in range(4):
                            mm(pss[:, b4 * 128:(b4 + 1) * 128], latT[:, k, b * 128:(b + 1) * 128],
                               wkvb[:, k, 128:256], k == 0, k == 3, [WKV, LAT], [pst_])
                    cp("act", Vh[:, g * 4:(g + 1) * 4, :].rearrange("p b d -> p (b d)"), pss[:], [pst_], [VHt])
                    yield
                def q_tr(qbf, QBF, cols):
                    p1, p1tok = PT.next()
                    tr(p1, qbf[:, 0:128], [QBF], [p1tok])
                    cp("act", qT[:, cols], p1, [p1tok], [QTt])
                    p2, p2tok = PT.next()
                    tr(p2[:64, :], qbf[:, 128:192], [QBF], [p2tok])
                    cp("act", qrT[0:64, cols], p2[:64, :], [p2tok], [QTt])

                prevq = None
                for lb in range(17):
                    gb = 15 + lb
                    cols = slice(lb * 128, (lb + 1) * 128)
                    pss, pst_ = PS.next()
                    cqb, CQN = cqn_ring.next()
                    dma("sp", cqb[:], cqn_s[:, cols].rearrange("(k p) n -> p k n", p=128), writes=[CQN])
                    for k in range(4):
                        mm(pss[:, :192], cqb[:, k, :], wqb[:, k, :], k == 0, k == 3, [CQN, WQ], [pst_])
                    kb.op("dve", lambda e: e.memset(qss[:], 0.0), [QSS], [QSS])
                    act(junk[:, :192], pss[:, :192], AF.Square, [pst_, JUNK, QSS], [JUNK, QSS], accum_out=qss[:, 0:1])
                    rsqrt_ms(qss[:], qss[:], 192, QSS, QSS)
                    stt(qn[:], pss[:, :192], qss[:, 0:1], gq_bc[:], ALU.mult, ALU.mult, [pst_, QSS, CONST], [QN])
                    qbf, QBF = qbf_ring.next()
                    cp("pool", qbf[:, 0:128], qn[:, 0:128], [QN], [QBF])
                    rope_tm(qbf[:, 128:192].rearrange("p (h d) -> p h d", h=1), qn[:, 128:192].rearrange("p (h d) -> p h d", h=1),
                            1, 32, cosm[:, gb, :], sinm[:, gb, :], [QN], QBF)
                    yield
                    if prevq is not None:
                        q_tr(*prevq)
                        yield
                    prevq = (qbf, QBF, cols)
                q_tr(*prevq)
                yield

            def attn(h, B):
                KT, Vh, qT, qrT, kscale = B["KT"], B["Vh"], B["qT"], B["qrT"], B["kscale"]
                KTt, VHt, QTt, KSC = B["KTt"], B["VHt"], B["QTt"], B["KSC"]
                work = []
                for gi, (c0, n) in enumerate(OG):
                    qb0 = (TP + c0) // 128
                    nk = qb0 + n // 128
                    for j in range(nk):
                        work.append((gi, c0, n, qb0, nk, j))

                def s_stage(w):
                    gi, c0, n, qb0, nk, j = w
                    lo = max(0, j - qb0) * 128
                    pss, pst_ = PS.next()
                    mm(pss[:, lo:n], KT[:, j * 128:(j + 1) * 128], qT[:, c0 + lo:c0 + n], True, False, [KTt, QTt], [pst_])
                    mm(pss[:, lo:n], kropeT[:, j * 128:(j + 1) * 128], qrT[:, c0 + lo:c0 + n], False, True, [KRT, QTt], [pst_])
                    pt, pttok = pt_ring.next()
                    act(pt[:, lo:n], pss[:, lo:n], AF.Exp, [pst_, KSC, CONST], [pttok], scale=kscale[:, j:j + 1],
                        bias=kbias[:, j:j + 1])
                    if j >= qb0:
                        tt("pool", pt[:, lo:lo + 128], pt[:, lo:lo + 128], tri[:], ALU.mult, [pttok, CONST], [pttok])
                    return pt, pttok, lo

                def pv_stage(w, sres):
                    gi, c0, n, qb0, nk, j = w
                    pt, pttok, lo = sres
                    (ao, aotok), (ad, adtok) = ACCP[(h * 5 + gi) % 2]
                    mm(ao[:, lo:n], Vh[:, j, :], pt[:, lo:n], j == 0, j == nk - 1, [VHt, pttok], [aotok])
                    psm, psmtok = psum_acc[(h * 5 + gi) % 2]
                    if j == 0:
                        cp("dve", psm[:, :n], pt[:, :n], [pttok, psmtok], [psmtok])
                    else:
                        tt("dve", psm[:, lo:n], psm[:, lo:n], pt[:, lo:n], ALU.add, [pttok, psmtok], [psmtok])
                    if j == nk - 1:
                        deferred.append([2, lambda: epilogue(gi, c0, n, ao, aotok, ad, adtok, psm, psmtok)])

                def epilogue(gi, c0, n, ao, aotok, ad, adtok, psm, psmtok):
                    if True:
                        mm(ad[:, :n], ones_f32[:], psm[:, :n], True, True, [CONST, psmtok], [adtok])
                        rden, RDEN = rden_ring.next()
                        attf, ATTF = attf_ring.next()
                        gta, GTA = gta_ring.next()
                        ts("dve", rden[:, :n], ad[:, :n], 1e-30, None, ALU.add, None, [adtok], [RDEN])
                        act(rden[:, :n], rden[:, :n], AF.Ln, [RDEN], [RDEN])
                        act(rden[:, :n], rden[:, :n], AF.Exp, [RDEN], [RDEN], scale=-1.0)
                        tt("dve", attf[:, :n], ao[:, :n], rden[:, :n], ALU.mult, [aotok, RDEN], [ATTF])
                        dma("sp", gta[:, :n], gate_s[(8 + h) * 128:(9 + h) * 128, c0:c0 + n], writes=[GTA])
                        st, sttok = st_ring.next()
                        tt("pool", st[:, :n], attf[:, :n], gta[:, :n], ALU.mult, [ATTF, GTA], [sttok])
                        dma("pool", mix_s[(8 + h) * 128:(9 + h) * 128, c0:c0 + n], st[:, :n], reads=[sttok])

                deferred = []

                def tick():
                    for d_ in list(deferred):
                        d_[0] -= 1
                        if d_[0] <= 0:
                            deferred.remove(d_)
                            d_[1]()

                pend = []
                for w in work:
                    pend.append((w, s_stage(w)))
                    if len(pend) > 2:
                        pv_stage(*pend.pop(0))
                    tick()
                    yield
                while pend:
                    pv_stage(*pend.pop(0))
                    tick()
                    yield
                while deferred:
                    tick()
                    yield

            for _ in prep(0, BS[0]):
                pass
            for h in range(8):
                ga = attn(h, BS[h % 2])
                gp = prep(h + 1, BS[(h + 1) % 2]) if h < 7 else iter(())
                a_alive = p_alive = True
                while a_alive:
                    for _ in range(3):
                        try:
                            next(ga)
                        except StopIteration:
                            a_alive = False
                            break
                    if p_alive:
                        try:
                            next(gp)
                        except StopIteration:
                            p_alive = False
                for _ in gp:
                    pass
            kb.barrier()
            PS.items = PS_save
        es_rope.close()

        ckpt(12)
        def load_act(src, ncols_total, groups, src_col0=0):
            lo = min(c0 for c0, n in groups)
            hi = max(c0 + n for c0, n in groups)
            for c in range(16):
                dma("sp" if c % 2 == 0 else "act", hT[:, c, lo:hi], src[c * 128:(c + 1) * 128, src_col0 + lo:src_col0 + hi],
                    writes=[hT_tok[gi] for gi in range(len(groups))])

        def resid_evac(res_src, res_col0, dst, dst_col0):
            pend = []

            def flush():
                while pend:
                    pend.pop(0)()

            def f(cc, gi, c0, n, pss, pst_):
                xt, xtok = xs_ring.next()
                dma("sp", xt[:, :n], res_src[cc * 128:(cc + 1) * 128, res_col0 + c0:res_col0 + c0 + n], writes=[xtok])
                ev, evtok = ev_ring.next()
                tt("dve", ev[:, :n], pss[:, :n], xt[:, :n], ALU.add, [pst_, xtok], [evtok])
                flush()
                pend.append(lambda: dma("sp", dst[cc * 128:(cc + 1) * 128, dst_col0 + c0:dst_col0 + c0 + n], ev[:, :n],
                                        reads=[evtok]))
            f.flush = flush
            return f

        es_C = ExitStack()
        open_stacks.append(es_C)
        w_ring = Ring([(sbx(es_C, f"wtc{i}", [128, 16, 256], BF16), Tok(f"wtc{i}")) for i in range(4)])
        hT = sbx(es_C, "hTc", [128, 16, TO], BF16)
        xs_ring = Ring([(sbx(es_C, f"xsc{i}", [128, 512], F32), Tok(f"xsc{i}")) for i in range(8)])
        ev_ring = Ring([(sbx(es_C, f"evc{i}", [128, 512], F32), Tok(f"evc{i}")) for i in range(3)])
        load_act(mix_s, TO, OG)
        _rev = resid_evac(xT, TP, x1_s, 0)
        fm_sweep(a_w_out, 0, 2048, OG, _rev, hT, hT_tok[:5])
        _rev.flush()
        kb.barrier()

        ckpt(13)
        norm_pass(x1_s, 0, OG, g_c, hT, hT_tok[:5], xs_ring=xs_ring)
        OG4 = [(128 + i * 512, 512) for i in range(4)]
        with ExitStack() as ph5:
            cw = sbx(ph5, "cw", [128, 16, 31], F32)
            dma("sp", cw[:], c_conv_wT.rearrange("(c p) k -> p c k", p=128), writes=[CONST])
            u_ring = Ring([(sbx(ph5, f"u{i}", [128, TO], BF16), Tok(f"u{i}")) for i in range(3)])
            dg_ring = Ring([(sbx(ph5, f"dg{i}", [128, 31, 128], BF16), Tok(f"dg{i}")) for i in range(2)])
            sg_ring = Ring([(sbx(ph5, f"sg{i}", [128, 512], F32), Tok(f"sg{i}")) for i in range(2)])

            def conv_pe(c, u, utok):
                dg, dgtok = dg_ring.next()
                tt("dve", dg[:], ident[:].unsqueeze(1).to_broadcast([128, 31, 128]),
                   cw[:, c, :].unsqueeze(2).to_broadcast([128, 31, 128]), ALU.mult, [CONST], [dgtok])
                for g in range(4):
                    pc, pctok = PS.next()
                    for kk in range(31):
                        o = 98 + kk + g * 512
                        mm(pc[:], dg[:, kk, :], u[:, o:o + 512], kk == 0, kk == 30, [dgtok, utok], [pctok])
                    ev, evtok = ev_ring.next()
                    act(ev[:], pc[:], AF.Identity, [pctok, CONST], [evtok], bias=cb[:, c:c + 1])
                    dma("act", v_s[c * 128:(c + 1) * 128, g * 512:(g + 1) * 512], ev[:], reads=[evtok])

            prev = None
            for t0 in range(0, 2048, 256):
                wa, watok = load_w(c_w_in, t0, 256)
                wb, wbtok = load_w(c_w_in, 2048 + t0, 256)
                for j in range(2):
                    c = t0 // 128 + j
                    u, utok = u_ring.next()
                    for gi, (c0, n) in enumerate(OG):
                        pa, patok = PS.next()
                        for k in range(16):
                            mm(pa[:, :n], wa[:, k, j * 128:(j + 1) * 128], hT[:, k, c0:c0 + n], k == 0, k == 15, [watok, hT_tok[gi]], [patok])
                        pb, pbtok = PS.next()
                        for k in range(16):
                            mm(pb[:, :n], wb[:, k, j * 128:(j + 1) * 128], hT[:, k, c0:c0 + n], k == 0, k == 15, [wbtok, hT_tok[gi]], [pbtok])
                        sg, sgtok = sg_ring.next()
                        act(sg[:, :n], pb[:, :n], AF.Sigmoid, [pbtok], [sgtok])
                        tt("dve", u[:, c0:c0 + n], pa[:, :n], sg[:, :n], ALU.mult, [patok, sgtok, utok], [utok])
                    if prev is not None:
                        conv_pe(*prev)
                    prev = (c, u, utok)
            conv_pe(*prev)
            fm_sweep(c_w_in, 4096, 2048, OG4, gate_evac(gate1_s, -128), hT, hT_tok[1:5])
            kb.barrier()

        es_C.close()
        with ExitStack() as ph6:
            wout = sbx(ph6, "wout", [128, 16, 2048], BF16)
            ev_ring = Ring([(sbx(ph6, f"evd{i}", [128, 512], F32), Tok(f"evd{i}")) for i in range(4)])
            xs_ring = Ring([(sbx(ph6, f"xsd{i}", [128, 512], F32), Tok(f"xsd{i}")) for i in range(4)])
            WOUT = [Tok(f"wout{i}") for i in range(8)]
            for i in range(8):
                dma("pool", wout[:, :, i * 256:(i + 1) * 256], c_w_out[:, i * 256:(i + 1) * 256].rearrange("(k p) n -> p k n", p=128),
                    writes=[WOUT[i]])
            vg = sbx(ph6, "vg", [128, 16, 512], F32)
            mg_ring = Ring([(sbx(ph6, f"mg{i}", [128, 16, 512], BF16), [Tok(f"mg{i}_{c}") for c in range(16)]) for i in range(2)])
            mean = sbx(ph6, "mean", [128, 512], F32)
            m2 = sbx(ph6, "m2", [128, 512], F32)
            lrs = sbx(ph6, "lrs", [128, 512], F32)
            sl_ring = Ring([(sbx(ph6, f"sl{i}", [128, 512], F32), Tok(f"sl{i}")) for i in range(3)])
            g1_ring = Ring([(sbx(ph6, f"g1{i}", [128, 512], BF16), Tok(f"g1{i}")) for i in range(3)])
            VG = [Tok(f"vg{c}") for c in range(16)]
            MEAN, M2, LRS = Tok("mean"), Tok("m2"), Tok("lrs")

            def ln_group(g):
                gs = slice(g * 512, (g + 1) * 512)
                mg, mgtoks = mg_ring.next()
                pm, pmtok = psum[6], ptok[6]
                pq, pqtok = psum[7], ptok[7]
                pend_st = None
                for c in range(16):
                    dma("sp", vg[:, c, :], v_s[c * 128:(c + 1) * 128, gs], writes=[VG[c]])
                    vb, vbtok = st_ring.next()
                    cp("dve", vb[:], vg[:, c, :], [VG[c]], [vbtok])
                    sq, sqtok = sq_ring.next()
                    act(sq[:], vg[:, c, :], AF.Square, [VG[c]], [sqtok])
                    if pend_st is not None:
                        pend_st()

                    def pend_st(vb=vb, vbtok=vbtok, sq=sq, sqtok=sqtok, c=c):
                        mm(pm[:], ones_bf[:], vb[:], c == 0, c == 15, [vbtok, CONST], [pmtok])
                        mm(pq[:], ones_bf[:], sq[:], c == 0, c == 15, [sqtok, CONST], [pqtok])
                    yield
                pend_st()
                act(mean[:], pm[:], AF.Copy, [pmtok], [MEAN], scale=1.0 / 2048)
                tt("dve", m2[:], mean[:], mean[:], ALU.mult, [MEAN], [M2])
                stt(lrs[:], pq[:], 1.0 / 2048, m2[:], ALU.mult, ALU.subtract, [pqtok, M2], [LRS])
                rsqrt_ms(lrs[:], lrs[:], 1, LRS, LRS)
                def st1(c):
                    tt("dve", vg[:, c, :], vg[:, c, :], mean[:], ALU.subtract, [VG[c], MEAN], [VG[c]])
                    tt("pool", vg[:, c, :], vg[:, c, :], lrs[:], ALU.mult, [VG[c], LRS], [VG[c]])

                def st2(c):
                    sl, SL = sl_ring.next()
                    act(sl[:], vg[:, c, :], AF.Silu, [VG[c], CONST], [SL], scale=lng[:, c:c + 1], bias=lnb[:, c:c + 1])
                    g1, G1 = g1_ring.next()
                    dma("sp", g1[:], gate1_s[c * 128:(c + 1) * 128, gs], writes=[G1])
                    return sl, SL, g1, G1

                def st3(c, sl, SL, g1, G1):
                    tt("dve", mg[:, c, :], sl[:], g1[:], ALU.mult, [SL, G1], [mgtoks[c]])

                r2s = {}
                for c in range(16 + 2):
                    if c < 16:
                        st1(c)
                    if 1 <= c <= 16:
                        r2s[c - 1] = st2(c - 1)
                    if c >= 2:
                        st3(c - 2, *r2s.pop(c - 2))
                    yield
                LNRES[g] = (g, mg, mgtoks)

            def out_group(g, mg, mgtoks):
                ev_f = resid_evac(x1_s, 128, outT, 0)
                for m in range(16):
                    pss, pst_ = PS.next()
                    for k in range(16):
                        mm(pss[:], wout[:, k, m * 128:(m + 1) * 128], mg[:, k, :], k == 0, k == 15, [WOUT[m // 2], mgtoks[k]], [pst_])
                    ev_f(m, g, g * 512, 512, pss, pst_)
                    yield
                ev_f.flush()

            LNRES = {}
            for _ in ln_group(0):
                pass
            for g in range(4):
                go = out_group(*LNRES[g])
                gl = ln_group(g + 1) if g < 3 else iter(())
                o_alive = l_alive = True
                while o_alive or l_alive:
                    if o_alive:
                        try:
                            next(go)
                        except StopIteration:
                            o_alive = False
                    for _ in range(2):
                        if l_alive:
                            try:
                                next(gl)
                            except StopIteration:
                                l_alive = False
    except _Stop:
        for st_ in reversed(open_stacks):
            st_.close()
    with ExitStack() as ses:
        sems = {e: ses.enter_context(nc.semaphore(f"s_{e}")) for e in KB.ENGS}
        dsems = {e: [ses.enter_context(nc.semaphore(f"d_{e}{i}")) for i in range(NDMASEM)] for e in KB.ENGS}
        with nc.allow_low_precision("bf16 matmul operands, fp32 accumulation"):
            kb.emit(sems, dsems)
    es.close()
    return nc


def _consts():
    bf = ml_dtypes.bfloat16
    c = {}
    c["c_ident"] = np.eye(128, dtype=np.float32).astype(bf)
    k = np.arange(128)
    c["c_tri"] = (k[:, None] <= k[None, :]).astype(np.float32).astype(bf)
    log_g = np.log1p(-(2.0 ** (-5.0 - np.arange(8, dtype=np.float64))))
    rel = (k[None, :] - k[:, None]).astype(np.float64)
    dm = np.where(rel[:, None, :] >= 0, np.exp(log_g[None, :, None] * np.maximum(rel, 0)[:, None, :]), 0.0)
    c["c_dmaskT"] = (dm * (128 ** -0.5)).astype(np.float32)
    c["c_toend"] = (np.exp(log_g[None, :] * (127.0 - k)[:, None]) * (128 ** -0.5)).astype(np.float32)
    fs = np.exp(log_g[:, None] * (k + 1.0)[None, :])
    c["c_fscol"] = np.ascontiguousarray(fs.T).astype(np.float32)
    invr = (10000.0 ** (-np.arange(0, 128, 2, dtype=np.float32) / 128)).astype(np.float32)
    invm = (10000.0 ** (-np.arange(0, 64, 2, dtype=np.float32) / 64)).astype(np.float32)
    c["c_invr"] = np.broadcast_to(invr[None], (128, 64)).copy()
    c["c_invm"] = np.broadcast_to(invm[None], (128, 32)).copy()
    c["_decay"] = np.exp(log_g * 128.0)
    return c


def make_in_maps(inputs):
    x = np.asarray(inputs["x"], dtype=np.float32)
    positions = np.asarray(inputs["positions"], dtype=np.int32)
    consts = _consts()
    shared = {}
    for k in ("a_norm_g", "a_w_in", "a_q_a_norm_g", "a_w_q_b", "a_kv_a_norm_g", "a_w_kv_b", "a_q_norm_g",
              "a_k_norm_g", "a_ret_norm_g", "a_w_out", "c_norm_g", "c_w_in", "c_conv_b", "c_ln_g", "c_ln_b",
              "c_w_out"):
        shared[k] = np.ascontiguousarray(np.asarray(inputs[k], dtype=np.float32)[0])
    shared["c_conv_wT"] = np.ascontiguousarray(np.asarray(inputs["c_conv_w"], dtype=np.float32)[0].T)
    for k, v in consts.items():
        if not k.startswith("_"):
            shared[k] = v
    in_maps = []
    for core in range(8):
        b, h = core // 2, core % 2
        xl = np.zeros((T, D), np.float32)
        pl = np.zeros((T,), np.int32)
        kbv = np.zeros((T,), np.float32)
        if h == 0:
            xl[2048:] = x[b, :2048]
            pl[2048:] = positions[b, :2048]
            kbv[:2048] = -30000.0
        else:
            xl[:] = x[b]
            pl[:] = positions[b]
        m = dict(shared)
        m["xT"] = np.ascontiguousarray(xl.T)
        m["pos_tm"] = np.ascontiguousarray(pl.reshape(NB, 128).T)
        m["kbias"] = np.ascontiguousarray(kbv.reshape(NB, 128).T)
        in_maps.append(m)
    return in_maps


def kernel(**inputs):
    nc = build()
    in_maps = make_in_maps(inputs)
    res = run_bass_kernel_spmd(nc, in_maps, core_ids=list(range(8)))
    out = np.zeros((4, 4096, D), np.float32)
    for core in range(8):
        b, h = core // 2, core % 2
        out[b, h * 2048:(h + 1) * 2048] = res.results[core]["outT"].T
    return out
```
